# Optimizing a Trainium2 kernel written in Bass

```python
import math
import jax, jax.numpy as jnp
from jax import lax
import numpy as np

D_MODEL = 1024
BATCH = 8
SEQ = 4096
DEPTH = 1
DEC_BATCH = 128
DEC_SEQ = 8
PAST_LEN = 8192
PAGE_SIZE = 128

HEAD_DIM = 64
N_HEADS_A = 12
DILATED_BRANCHES = ((128, 1), (512, 4), (2048, 16))
WIN_MAX = 2048
Q_BLOCK = 128
N_BUCKETS = 32
BUCKET_MAX_DIST = 2048
N_HEADS_B = 4
DK_B = 32
DV_B = 64
GATE_RANK = 16
GATE_NORM = 16.0
GLA_CHUNK = 16
WIDTH_A = N_HEADS_A * HEAD_DIM
WIDTH_BK = N_HEADS_B * DK_B
WIDTH_BV = N_HEADS_B * DV_B
MIX_WIDTH = WIDTH_A + WIDTH_BV
PROJ_SPLITS = (WIDTH_A, WIDTH_A, WIDTH_A, WIDTH_BK, WIDTH_BK, WIDTH_BV, WIDTH_BV, GATE_RANK)
PROJ_WIDTH = sum(PROJ_SPLITS)
D_FF = 2816
EPS = 1e-6

kernel_name = 'hybrid_dilated_gla_macaron_step'


def _rmsnorm(x, gain):
    xf = x.astype(jnp.float32)
    y = xf * lax.rsqrt(jnp.mean(xf * xf, axis=-1, keepdims=True) + EPS)
    return (y * gain.astype(jnp.float32)).astype(x.dtype)


def _half_ffn(x, gain, w1, w3, w2):
    h = _rmsnorm(x, gain)
    return x + 0.5 * ((jax.nn.silu(h @ w1) * (h @ w3)) @ w2)


def _t5_bucket(dist):
    max_exact = N_BUCKETS // 2
    d = jnp.maximum(dist, 1).astype(jnp.float32)
    large = max_exact + (jnp.log(d / max_exact) / math.log(BUCKET_MAX_DIST / max_exact)
                         * (N_BUCKETS - max_exact)).astype(jnp.int32)
    large = jnp.minimum(large, N_BUCKETS - 1)
    return jnp.where(dist < max_exact, dist, large)


def _project(h, w_in, q_gain, k_gain, w_gk2, b_gk):
    B, S, _ = h.shape
    z = h @ w_in
    qa, ka, va, qb, kb, vb, rb, glr = jnp.split(z, np.cumsum(PROJ_SPLITS)[:-1].tolist(), axis=-1)
    qa = _rmsnorm(qa.reshape(B, S, N_HEADS_A, HEAD_DIM), q_gain)
    ka = _rmsnorm(ka.reshape(B, S, N_HEADS_A, HEAD_DIM), k_gain)
    va = va.reshape(B, S, N_HEADS_A, HEAD_DIM)
    qb = qb.reshape(B, S, N_HEADS_B, DK_B) * (DK_B ** -0.5)
    kb = kb.reshape(B, S, N_HEADS_B, DK_B)
    vb = vb.reshape(B, S, N_HEADS_B, DV_B)
    gk = jax.nn.log_sigmoid((glr @ w_gk2 + b_gk).astype(jnp.float32)) / GATE_NORM
    gk = gk.reshape(B, S, N_HEADS_B, DK_B)
    return qa, ka, va, qb, kb, vb, rb, gk


def _dilated_prompt(q, k, v, rel_bias, window, dilation):
    B, S, H, Dh = q.shape
    L = S // dilation
    nk = window // dilation
    bq = math.gcd(L, Q_BLOCK)
    nb = L // bq

    def split(t):
        return t.reshape(B, L, dilation, H, Dh).transpose(0, 2, 1, 3, 4)

    qs, ks, vs = split(q), split(k), split(v)
    pad = ((0, 0), (0, 0), (nk, 0), (0, 0), (0, 0))
    kp, vp = jnp.pad(ks, pad), jnp.pad(vs, pad)
    kidx = jnp.arange(nb)[:, None] * bq + jnp.arange(bq + nk)[None, :]
    kb, vb = kp[:, :, kidx], vp[:, :, kidx]
    qb = qs.reshape(B, dilation, nb, bq, H, Dh)
    logits = jnp.einsum('brnqhd,brnkhd->brnhqk', qb, kb,
                        preferred_element_type=jnp.float32) * (Dh ** -0.5)
    i = jnp.arange(bq)[:, None]
    j = jnp.arange(bq + nk)[None, :]
    step = i - j + nk
    kpos = jnp.arange(nb)[:, None, None] * bq + j[None] - nk
    valid = (step >= 0) & (step <= nk) & (kpos >= 0)
    bias = rel_bias[_t5_bucket(jnp.clip(step, 0, nk) * dilation)].astype(jnp.float32)
    logits = logits + bias.transpose(2, 0, 1)
    logits = jnp.where(valid[None, None, :, None], logits, -jnp.inf)
    lse = jax.nn.logsumexp(logits, axis=-1)
    p = jnp.exp(logits - lse[..., None])
    o = jnp.einsum('brnhqk,brnkhd->brnqhd', p.astype(vb.dtype), vb,
                   preferred_element_type=jnp.float32)
    o = o.reshape(B, dilation, L, H, Dh).transpose(0, 2, 1, 3, 4).reshape(B, S, H, Dh)
    lse = lse.transpose(0, 1, 2, 4, 3).reshape(B, dilation, L, H).transpose(0, 2, 1, 3).reshape(B, S, H)
    return o, lse


def _dilated_sample(q, k_all, v_all, rel_bias, window, dilation, w_buf):
    T, Dh = q.shape[1], q.shape[-1]
    nk = window // dilation
    steps = jnp.arange(nk + 1)
    idx = (w_buf + jnp.arange(T))[:, None] - steps[None, :] * dilation
    valid = idx >= 0
    idx_c = jnp.maximum(idx, 0)
    kg, vg = k_all[:, idx_c], v_all[:, idx_c]
    logits = jnp.einsum('bthd,btkhd->bhtk', q, kg,
                        preferred_element_type=jnp.float32) * (Dh ** -0.5)
    bias = rel_bias[_t5_bucket(steps * dilation)].astype(jnp.float32)
    logits = logits + bias.T[None, :, None, :]
    logits = jnp.where(valid[None, None], logits, -jnp.inf)
    lse = jax.nn.logsumexp(logits, axis=-1)
    p = jnp.exp(logits - lse[..., None])
    o = jnp.einsum('bhtk,btkhd->bthd', p.astype(vg.dtype), vg,
                   preferred_element_type=jnp.float32)
    return o, lse.transpose(0, 2, 1)


def _merge_branches(outs, lses):
    w = jax.nn.softmax(jnp.stack(lses, 0), axis=0)
    return jnp.einsum('nbsh,nbshd->bshd', w, jnp.stack(outs, 0))


def _gla(q, k, v, g, s0, chunk):
    B, S, H, Dk = q.shape
    Dv = v.shape[-1]
    n = S // chunk
    f32 = jnp.float32
    qc = q.astype(f32).reshape(B, n, chunk, H, Dk)
    kc = k.astype(f32).reshape(B, n, chunk, H, Dk)
    vc = v.astype(f32).reshape(B, n, chunk, H, Dv)
    b = jnp.cumsum(g.astype(f32).reshape(B, n, chunk, H, Dk), axis=2)
    t_idx = jnp.arange(chunk)
    causal = (t_idx[:, None] >= t_idx[None, :])[None, None, :, :, None, None]
    diff = b[:, :, :, None] - b[:, :, None, :]
    decay = jnp.exp(jnp.where(causal, diff, -jnp.inf))
    scores = jnp.einsum('bnthk,bnshk,bntshk->bnhts', qc, kc, decay)
    o_intra = jnp.einsum('bnhts,bnshv->bnthv', scores, vc)
    b_last = b[:, :, -1]
    k_dec = kc * jnp.exp(b_last[:, :, None] - b)
    ds = jnp.einsum('bnshk,bnshv->bnhkv', k_dec, vc)

    def step(state, inp):
        dec, d_state = inp
        return jnp.exp(dec)[..., None] * state + d_state, state

    s_fin, s_start = lax.scan(step, s0.astype(f32),
                              (jnp.moveaxis(b_last, 1, 0), jnp.moveaxis(ds, 1, 0)))
    s_start = jnp.moveaxis(s_start, 0, 1)
    o_inter = jnp.einsum('bnthk,bnhkv->bnthv', qc * jnp.exp(b), s_start)
    return (o_intra + o_inter).reshape(B, S, H, Dv), s_fin


def _merge_out(oa, ob, rb, gla_gain, w_out):
    B, S = oa.shape[:2]
    ob = _rmsnorm(ob, gla_gain).reshape(B, S, WIDTH_BV) * jax.nn.silu(rb.astype(jnp.float32))
    cat = jnp.concatenate([oa.reshape(B, S, WIDTH_A), ob], axis=-1).astype(w_out.dtype)
    return cat @ w_out


def _mix_prompt(h, w_in, q_gain, k_gain, rel_bias, w_gk2, b_gk, gla_gain, w_out):
    B, S, _ = h.shape
    qa, ka, va, qb, kb, vb, rb, gk = _project(h, w_in, q_gain, k_gain, w_gk2, b_gk)
    outs, lses = [], []
    for window, dil in DILATED_BRANCHES:
        o, l = _dilated_prompt(qa, ka, va, rel_bias, window, dil)
        outs.append(o)
        lses.append(l)
    oa = _merge_branches(outs, lses)
    s0 = jnp.zeros((B, N_HEADS_B, DK_B, DV_B), jnp.float32)
    ob, s_fin = _gla(qb, kb, vb, gk, s0, math.gcd(S, GLA_CHUNK))
    y = _merge_out(oa, ob, rb, gla_gain, w_out)
    n_keep = min(WIN_MAX, S)
    return y, ka[:, S - n_keep:], va[:, S - n_keep:], s_fin


def _mix_sample(h, win_k, win_v, gla_state, w_in, q_gain, k_gain, rel_bias, w_gk2, b_gk, gla_gain, w_out):
    T = h.shape[1]
    w_buf = win_k.shape[1]
    qa, ka, va, qb, kb, vb, rb, gk = _project(h, w_in, q_gain, k_gain, w_gk2, b_gk)
    k_all = jnp.concatenate([win_k, ka.astype(win_k.dtype)], axis=1)
    v_all = jnp.concatenate([win_v, va.astype(win_v.dtype)], axis=1)
    outs, lses = [], []
    for window, dil in DILATED_BRANCHES:
        o, l = _dilated_sample(qa, k_all, v_all, rel_bias, window, dil, w_buf)
        outs.append(o)
        lses.append(l)
    oa = _merge_branches(outs, lses)
    ob, s_new = _gla(qb, kb, vb, gk, gla_state, math.gcd(T, GLA_CHUNK))
    y = _merge_out(oa, ob, rb, gla_gain, w_out)
    return y, k_all[:, T:], v_all[:, T:], s_new


def setup_inputs(seed: int = 0) -> dict:
    key = jax.random.key(seed)
    ks = jax.random.split(key, 24)
    f32 = jnp.float32

    def nrm(k, shape, scale):
        return jax.random.normal(k, shape, f32) * scale

    w_buf = min(WIN_MAX, PAST_LEN)
    return {
        'x_prompt': nrm(ks[0], (BATCH, SEQ, D_MODEL), 1.0),
        'x_sample': nrm(ks[1], (DEC_BATCH, DEC_SEQ, D_MODEL), 1.0),
        'cache_win_k': nrm(ks[2], (DEPTH, DEC_BATCH, w_buf, N_HEADS_A, HEAD_DIM), 1.0),
        'cache_win_v': nrm(ks[3], (DEPTH, DEC_BATCH, w_buf, N_HEADS_A, HEAD_DIM), 1.0),
        'state_gla': nrm(ks[4], (DEPTH, DEC_BATCH, N_HEADS_B, DK_B, DV_B), 0.5),
        'ffn1_norm': 1.0 + nrm(ks[5], (DEPTH, D_MODEL), 0.02),
        'ffn1_w1': nrm(ks[6], (DEPTH, D_MODEL, D_FF), D_MODEL ** -0.5),
        'ffn1_w3': nrm(ks[7], (DEPTH, D_MODEL, D_FF), D_MODEL ** -0.5),
        'ffn1_w2': nrm(ks[8], (DEPTH, D_FF, D_MODEL), D_FF ** -0.5),
        'mix_norm': 1.0 + nrm(ks[9], (DEPTH, D_MODEL), 0.02),
        'w_in': nrm(ks[10], (DEPTH, D_MODEL, PROJ_WIDTH), D_MODEL ** -0.5),
        'q_norm': 1.0 + nrm(ks[11], (DEPTH, HEAD_DIM), 0.02),
        'k_norm': 1.0 + nrm(ks[12], (DEPTH, HEAD_DIM), 0.02),
        'rel_bias': nrm(ks[13], (N_BUCKETS, N_HEADS_A), 0.2),
        'w_gk2': nrm(ks[14], (DEPTH, GATE_RANK, WIDTH_BK), GATE_RANK ** -0.5),
        'b_gk': nrm(ks[15], (DEPTH, WIDTH_BK), 0.1),
        'gla_norm': 1.0 + nrm(ks[16], (DEPTH, DV_B), 0.02),
        'w_out': nrm(ks[17], (DEPTH, MIX_WIDTH, D_MODEL), MIX_WIDTH ** -0.5),
        'ffn2_norm': 1.0 + nrm(ks[18], (DEPTH, D_MODEL), 0.02),
        'ffn2_w1': nrm(ks[19], (DEPTH, D_MODEL, D_FF), D_MODEL ** -0.5),
        'ffn2_w3': nrm(ks[20], (DEPTH, D_MODEL, D_FF), D_MODEL ** -0.5),
        'ffn2_w2': nrm(ks[21], (DEPTH, D_FF, D_MODEL), D_FF ** -0.5),
    }


def reference(x_prompt, x_sample, cache_win_k, cache_win_v, state_gla,
              ffn1_norm, ffn1_w1, ffn1_w3, ffn1_w2, mix_norm, w_in, q_norm, k_norm,
              rel_bias, w_gk2, b_gk, gla_norm, w_out, ffn2_norm, ffn2_w1, ffn2_w3, ffn2_w2):
    yp, ys = x_prompt, x_sample
    kp_l, vp_l, sp_l, ks_l, vs_l, ss_l = [], [], [], [], [], []
    for l in range(DEPTH):
        yp = _half_ffn(yp, ffn1_norm[l], ffn1_w1[l], ffn1_w3[l], ffn1_w2[l])
        m, kp, vp, sp = _mix_prompt(_rmsnorm(yp, mix_norm[l]), w_in[l], q_norm[l], k_norm[l],
                                    rel_bias, w_gk2[l], b_gk[l], gla_norm[l], w_out[l])
        yp = yp + m.astype(yp.dtype)
        yp = _half_ffn(yp, ffn2_norm[l], ffn2_w1[l], ffn2_w3[l], ffn2_w2[l])
        ys = _half_ffn(ys, ffn1_norm[l], ffn1_w1[l], ffn1_w3[l], ffn1_w2[l])
        m, kn, vn, sn = _mix_sample(_rmsnorm(ys, mix_norm[l]), cache_win_k[l], cache_win_v[l],
                                    state_gla[l], w_in[l], q_norm[l], k_norm[l], rel_bias,
                                    w_gk2[l], b_gk[l], gla_norm[l], w_out[l])
        ys = ys + m.astype(ys.dtype)
        ys = _half_ffn(ys, ffn2_norm[l], ffn2_w1[l], ffn2_w3[l], ffn2_w2[l])
        kp_l.append(kp); vp_l.append(vp); sp_l.append(sp)
        ks_l.append(kn); vs_l.append(vn); ss_l.append(sn)
    return (yp, ys, jnp.stack(kp_l), jnp.stack(vp_l), jnp.stack(sp_l),
            jnp.stack(ks_l), jnp.stack(vs_l), jnp.stack(ss_l))
```

```python
import contextlib
import math
import numpy as np
import concourse.bass as bass
import concourse.mybir as mybir
from concourse.bass_utils import run_bass_kernel_spmd

F32 = mybir.dt.float32
BF16 = mybir.dt.bfloat16
U8 = mybir.dt.uint8
AF = mybir.ActivationFunctionType
ALU = mybir.AluOpType
AX = mybir.AxisListType

ENGS = ("tensor", "scalar", "vector", "gpsimd", "sync")
NCORES = 8
D = 1024
DFF = 2816
NFC = DFF // 128
SEQ = 4096
NTP = SEQ // 128
NSEQ_S = 16
TS = 8
WBUF = 2048
NH = 12
HD = 64
WA = NH * HD
PROJ = 3088
EPS = 1e-6


class Buf:
    __slots__ = ("name", "writer", "readers", "excl")

    def __init__(self, name, excl=False):
        self.name = name
        self.writer = None
        self.readers = []
        self.excl = excl


class Op:
    __slots__ = ("eng", "fn", "deps", "is_dma", "key", "sig", "marked", "pos")

    def __init__(self, eng, fn, is_dma, key):
        self.eng = eng
        self.fn = fn
        self.deps = []
        self.is_dma = is_dma
        self.key = key
        self.sig = None
        self.marked = False
        self.pos = 0


def _reduce_ops(ops):
    best = {}
    for d in ops:
        k = ("d", d.key) if d.is_dma else ("e", d.eng)
        if k not in best or best[k].pos < d.pos:
            best[k] = d
    return list(best.values())


class Prog:
    def __init__(self, nc):
        self.nc = nc
        self.q = {e: [] for e in ENGS}
        self.final_waits = []
        self.pending = {e: [] for e in ENGS}
        self.live_dma = []
        self.npos = 0

    def op(self, eng, fn, reads=(), writes=(), dma=False, key=None, extra=()):
        o = Op(eng, fn, dma, key)
        self.npos += 1
        o.pos = self.npos
        deps = []
        for b in reads:
            if b.writer is not None:
                deps.append(b.writer)
            if b.excl:
                deps.extend(r for r in b.readers if r.eng != eng)
        for b in writes:
            if b.writer is not None:
                deps.append(b.writer)
            deps.extend(b.readers)
        deps.extend(extra)
        if self.pending[eng]:
            deps.extend(self.pending[eng])
            self.pending[eng] = []
        for d in _reduce_ops(deps):
            if d is o:
                continue
            if (not dma) and eng == "tensor" and d.eng == "tensor" and not d.is_dma:
                continue
            if dma and d.is_dma and d.key == key:
                continue
            o.deps.append(d)
            d.marked = True
        for b in reads:
            b.readers = _reduce_ops(b.readers + [o])
        for b in writes:
            b.writer = o
            b.readers = []
        if dma:
            assert key is not None
            self.live_dma.append(o)
        self.q[eng].append(o)
        return o

    def I(self, eng, name, reads=(), writes=(), dma=False, key=None, extra=(), **kw):
        return self.op(eng, lambda e: getattr(e, name)(**kw), reads, writes, dma, key, extra)

    def barrier(self):
        lasts = []
        for e in ENGS:
            for o in reversed(self.q[e]):
                if not o.is_dma:
                    lasts.append(o)
                    break
        lasts.extend(self.live_dma)
        self.live_dma = []
        for e in ENGS:
            self.pending[e] = list(lasts)

    def must_finish(self, o):
        o.marked = True
        self.final_waits.append(o)

    def emit(self):
        nc = self.nc
        with contextlib.ExitStack() as st:
            esem = {e: st.enter_context(nc.semaphore("s_" + e)) for e in ENGS if e != "sync"}
            ecnt = {e: 0 for e in esem}
            dsem, dcnt = {}, {}
            for e in ENGS:
                for o in self.q[e]:
                    if o.is_dma:
                        if o.key not in dsem:
                            dsem[o.key] = st.enter_context(nc.semaphore("d_%d" % len(dsem)))
                            dcnt[o.key] = 0
                        dcnt[o.key] += 16
                        o.sig = (dsem[o.key], dcnt[o.key])
                    elif o.marked:
                        ecnt[e] += 1
                        o.sig = (esem[e], ecnt[e])
            self.n_sems = len(dsem) + len(esem)
            self.counts = dict(ecnt)
            block = st.enter_context(nc.Block())

            def run(e, eng):
                waited = {}
                for o in self.q[e]:
                    need = {}
                    for d in o.deps:
                        sem, val = d.sig
                        k = id(sem)
                        if need.get(k, (None, 0))[1] < val:
                            need[k] = (sem, val)
                    for k, (sem, val) in need.items():
                        if waited.get(k, 0) < val:
                            eng.wait_ge(sem, val)
                            waited[k] = val
                    ins = o.fn(eng)
                    if o.sig is not None:
                        ins.then_inc(o.sig[0], 16 if o.is_dma else 1)
                if e == "sync":
                    need = {}
                    for o in self.final_waits:
                        sem, val = o.sig
                        k = id(sem)
                        if need.get(k, (None, 0))[1] < val:
                            need[k] = (sem, val)
                    for k, (sem, val) in need.items():
                        if waited.get(k, 0) < val:
                            eng.wait_ge(sem, val)
                            waited[k] = val

            @block.tensor
            def _(eng):
                run("tensor", eng)

            @block.scalar
            def _(eng):
                run("scalar", eng)

            @block.vector
            def _(eng):
                run("vector", eng)

            @block.gpsimd
            def _(eng):
                run("gpsimd", eng)

            @block.sync
            def _(eng):
                run("sync", eng)


class Arena:
    def __init__(self, nc):
        self.nc = nc
        self.base = (nc.sbuf_base + 63) // 64 * 64
        self.limit = nc.sbuf_top
        self.top = self.base
        self.n = 0
        self.peak = self.base

    def alloc(self, name, shape, dtype):
        isz = {F32: 4, BF16: 2, U8: 1}[dtype]
        nbytes = int(np.prod(shape[1:])) * isz
        nbytes = (nbytes + 63) // 64 * 64
        off = self.top
        assert off + nbytes <= self.limit, ("SBUF overflow", name, off, nbytes, self.limit)
        self.top += nbytes
        self.peak = max(self.peak, self.top)
        self.n += 1
        return self.nc.alloc_sbuf_tensor_at("%s_%d" % (name, self.n), list(shape), dtype, offset=off)

    def mark(self):
        return self.top

    def release(self, m):
        self.top = m


def load_ffn_weights(P, A, w1, w3, w2, gain, tag):
    W = {}
    W["w1"] = A.alloc(tag + "w1", [128, 8, DFF], BF16)
    W["w3"] = A.alloc(tag + "w3", [128, 8, DFF], BF16)
    W["w2"] = A.alloc(tag + "w2", [128, NFC, D], BF16)
    W["g"] = A.alloc(tag + "g", [128, 8], F32)
    W["b_w1"] = [Buf(tag + "w1_%d" % i) for i in range(2)]
    W["b_w3"] = [Buf(tag + "w3_%d" % i) for i in range(2)]
    W["b_w2"] = [Buf(tag + "w2_%d" % i) for i in range(2)]
    W["b_g"] = Buf(tag + "g")
    P.I("sync", "dma_start", writes=[W["b_g"]], dma=True, key=tag + "g", out=W["g"][:], in_=gain)
    half = DFF // 2
    for hf in range(2):
        for nm, src in (("w1", w1), ("w3", w3)):
            for kc in range(8):
                P.I("gpsimd", "dma_start", writes=[W["b_" + nm][hf]], dma=True, key=tag + nm + str(hf), out=W[nm][:, kc, hf * half:(hf + 1) * half],
                         in_=src[kc * 128:(kc + 1) * 128, hf * half:(hf + 1) * half])
        for c in range(hf * 11, hf * 11 + 11):
            P.I("gpsimd", "dma_start", writes=[W["b_w2"][hf]], dma=True, key=tag + "w2" + str(hf), out=W["w2"][:, c, :], in_=w2[c * 128:(c + 1) * 128, :])
    return W


def ffn_phase(P, A, PS, C, W, tiles, tag):
    m0 = A.mark()
    NB = 3
    xt = [A.alloc(tag + "xt", [128, 2, D], F32) for _ in range(NB)]
    b_xt = [[Buf(tag + "xt%d_%d" % (i, t)) for t in range(2)] for i in range(NB)]
    hb = A.alloc(tag + "hb", [128, 2, D], BF16)
    b_hb = [Buf(tag + "hb%d" % t) for t in range(2)]
    hT = [A.alloc(tag + "hT", [128, 8, 256], BF16) for _ in range(2)]
    b_hT = [Buf(tag + "hT%d" % i) for i in range(2)]
    sq = A.alloc(tag + "sq", [128, D], BF16)
    b_sq = Buf(tag + "sq")
    ss = A.alloc(tag + "ss", [128, 2], F32)
    rs = A.alloc(tag + "rs", [128, 2], F32)
    b_ss = [Buf(tag + "ss%d" % t) for t in range(2)]
    b_rs = [Buf(tag + "rs%d" % t) for t in range(2)]
    s1 = [A.alloc(tag + "s1", [128, 256], F32) for _ in range(2)]
    b_s1 = [Buf(tag + "s1_%d" % i) for i in range(2)]
    ub = [A.alloc(tag + "ub", [128, 256], BF16) for _ in range(3)]
    b_ub = [Buf(tag + "ub%d" % i) for i in range(3)]
    b_ps13 = [Buf(tag + "ps13_%d" % i, excl=True) for i in range(2)]
    b_py = [[Buf(tag + "py%d_%d" % (t, h), excl=True) for h in range(2)] for t in range(2)]
    b_pt = [Buf(tag + "pt%d" % t, excl=True) for t in range(2)]
    psT = [PS[:, 6 + t, :].bitcast(BF16) for t in range(2)]

    def load(i):
        st_ = tiles[i]
        for t, (src, dst, sb, db) in enumerate(st_):
            P.I("sync", "dma_start", reads=sb, writes=[b_xt[i % NB][t]], dma=True, key="xt%d_%d" % (i % NB, t), out=xt[i % NB][:, t, :], in_=src)

    def norm(i):
        st_ = tiles[i]
        x = xt[i % NB]
        for t in range(len(st_)):
            P.I("scalar", "activation", reads=[b_xt[i % NB][t]], writes=[b_sq, b_ss[t]], out=sq[:], in_=x[:, t, :], func=AF.Square,
                                                             accum_out=ss[:, t:t + 1])
            P.I("scalar", "activation", reads=[b_ss[t], C["b_const"]], writes=[b_rs[t]], out=rs[:, t:t + 1], in_=ss[:, t:t + 1], func=AF.Ln,
                                                        scale=1.0 / D, bias=C["eps"][:])
            P.I("scalar", "activation", reads=[b_rs[t]], writes=[b_rs[t]], out=rs[:, t:t + 1], in_=rs[:, t:t + 1], func=AF.Exp,
                                                        scale=-0.5)
            P.I("scalar", "activation", reads=[b_xt[i % NB][t], b_rs[t]], writes=[b_hb[t]], out=hb[:, t, :], in_=x[:, t, :], func=AF.Copy,
                                                             scale=rs[:, t:t + 1])
        for t in range(len(st_)):
            for kc in range(8):
                P.I("tensor", "transpose", reads=[b_hb[t], C["b_const"]], writes=[b_pt[t]], out=psT[t][:, kc * 128:(kc + 1) * 128],
                                                                   in_=hb[:, t, kc * 128:(kc + 1) * 128],
                                                                   identity=C["identb"][:])
            P.I("vector", "tensor_tensor", reads=[b_pt[t], W["b_g"]], writes=[b_hT[i % 2]], out=hT[i % 2][:, :, t * 128:(t + 1) * 128],
                in0=psT[t].rearrange("p (c n) -> p c n", c=8),
                in1=W["g"][:].unsqueeze(2).to_broadcast([128, 8, 128]), op=ALU.mult)

    def ffn(i):
        st_ = tiles[i]
        nt = len(st_)
        N = nt * 128
        x = xt[i % NB]
        h = hT[i % 2]
        pend = None
        for c in range(NFC + 1):
            if c < NFC:
                pb = c % 2
                hf = c // 11
                for j, nm in enumerate(("w1", "w3")):
                    for kc in range(8):
                        P.I("tensor", "matmul", reads=[W["b_" + nm][hf], b_hT[i % 2]], writes=[b_ps13[pb]], out=PS[:, pb, j * 256:j * 256 + N], lhsT=W[nm][:, kc, c * 128:(c + 1) * 128],
                            rhs=h[:, kc, 0:N], start=(kc == 0), stop=(kc == 7))
            if pend is not None:
                cc, u, bu = pend
                hf2 = cc // 11
                for t in range(nt):
                    for dh in range(2):
                        P.I("tensor", "matmul", reads=[W["b_w2"][hf2], bu], writes=[b_py[t][dh]], out=PS[:, 2 + t * 2 + dh, :], lhsT=u[:, t * 128:(t + 1) * 128],
                            rhs=W["w2"][:, cc, dh * 512:(dh + 1) * 512], start=(cc == 0), stop=(cc == NFC - 1))
                pend = None
            if c < NFC:
                sb_ = c % 2
                u = ub[c % 3]
                P.I("scalar", "activation", reads=[b_ps13[pb]], writes=[b_s1[sb_]], out=s1[sb_][:, 0:N], in_=PS[:, pb, 0:N],
                                                                         func=AF.Silu)
                P.I("vector", "tensor_tensor", reads=[b_s1[sb_], b_ps13[pb]], writes=[b_ub[c % 3]], out=u[:, 0:N], in0=s1[sb_][:, 0:N], in1=PS[:, pb, 256:256 + N], op=ALU.mult)
                pend = (c, u, b_ub[c % 3])
        for t, (src, dst, sb, db) in enumerate(st_):
            for dh in range(2):
                P.I("vector", "scalar_tensor_tensor", reads=[b_py[t][dh], b_xt[i % NB][t]], writes=[b_xt[i % NB][t]], out=x[:, t, dh * 512:(dh + 1) * 512], in0=PS[:, 2 + t * 2 + dh, :], scalar=0.5,
                    in1=x[:, t, dh * 512:(dh + 1) * 512], op0=ALU.mult, op1=ALU.add)
            o = P.I("sync", "dma_start", reads=[b_xt[i % NB][t]], writes=db, dma=True, key="xo%d_%d" % (i % NB, t), out=dst, in_=x[:, t, :])
            if tag == "f2":
                P.must_finish(o)

    n = len(tiles)
    load(0)
    if n > 1:
        load(1)
    norm(0)
    for i in range(n):
        if i + 2 < n:
            load(i + 2)
        if i + 1 < n:
            norm(i + 1)
        ffn(i)
    A.release(m0)


C128 = [("ident", 128), ("blockones", 128), ("caus", 128), ("caus_s", 128), ("bdmask", 256), ("headmask", 4),
        ("seqmask", 16), ("notstart", 128), ("ones128", 128)]
P128 = [("qg", 1), ("kg", 1), ("bgk", 1), ("gg", 256), ("f1g", 8), ("mixg", 8), ("f2g", 8)]


def _offsets(spec):
    off, o = {}, 0
    for n, w in spec:
        off[n] = (o, w)
        o += w
    return off, o


def make_consts():
    p = np.arange(128)
    c = {}
    c["ident"] = np.eye(128)
    c["blockones"] = (p[:, None] // 64 == p[None, :] // 64)
    c["caus"] = (p[None, :] >= p[:, None])
    c["caus_s"] = (p[None, :] >= p[:, None]) & (p[None, :] // TS == p[:, None] // TS)
    c["bdmask"] = (p[:, None] // 32 == np.arange(256)[None, :] // 64)
    c["headmask"] = (p[:, None] // 32 == np.arange(4)[None, :])
    c["seqmask"] = (p[:, None] // TS == np.arange(NSEQ_S)[None, :])
    c["notstart"] = np.tile((p % TS != 0)[None, :], (128, 1))
    c["ones128"] = np.ones((128, 128))
    off, tot = _offsets(C128)
    out = np.zeros((128, tot), np.float32)
    for n, (o, w) in off.items():
        out[:, o:o + w] = c[n].astype(np.float32)
    return out


def make_params(inp):
    off, tot = _offsets(P128)
    out = np.zeros((128, tot), np.float32)

    def put(n, a):
        o, w = off[n]
        out[:, o:o + w] = a
    put("qg", np.tile(inp["q_norm"][0], 2)[:, None])
    put("kg", np.tile(inp["k_norm"][0], 2)[:, None])
    put("bgk", inp["b_gk"][0][:, None])
    put("gg", np.tile(np.tile(inp["gla_norm"][0], 4)[None, :], (128, 1)))
    put("f1g", inp["ffn1_norm"][0].reshape(8, 128).T)
    put("mixg", inp["mix_norm"][0].reshape(8, 128).T)
    put("f2g", inp["ffn2_norm"][0].reshape(8, 128).T)
    return out


def setup_consts(P, A, c128, p128, wgk2_d):
    C = {}
    bC = C["b_const"] = Buf("const")
    off, tot = _offsets(C128)
    poff, ptot = _offsets(P128)
    cs = A.alloc("c128", [128, tot], F32)
    ps = A.alloc("p128", [128, ptot], F32)
    C["c128"], C["p128"] = cs, ps
    C["p128_d"], C["poff"] = p128, poff
    P.I("sync", "dma_start", writes=[bC], dma=True, key="c_c128", out=cs[:], in_=c128)
    P.I("sync", "dma_start", writes=[bC], dma=True, key="c_p128", out=ps[:], in_=p128)
    for n, (o, w) in off.items():
        C[n] = cs[:, o:o + w]
    for n, (o, w) in poff.items():
        C[n] = ps[:, o:o + w]
    C["identf"] = C["ident"]
    C["eps"] = A.alloc("eps", [128, 1], F32)
    C["one"] = A.alloc("one", [128, 1], F32)
    C["negb"] = A.alloc("negb", [128, 1], F32)
    identb = A.alloc("identb", [128, 128], BF16)
    blockb = A.alloc("blockb", [128, 128], BF16)
    wg_f = A.alloc("wgk2f", [16, 128], F32)
    wg_b = A.alloc("wgk2b", [16, 128], BF16)
    P.I("sync", "dma_start", writes=[bC], dma=True, key="c_wgk2", out=wg_f[:], in_=wgk2_d)
    P.I("vector", "memset", writes=[bC], ap=C["eps"][:], constant=EPS)
    P.I("vector", "memset", writes=[bC], ap=C["one"][:], constant=1.0)
    P.I("vector", "tensor_copy", reads=[bC], writes=[bC], out=identb[:], in_=C["ident"])
    P.I("vector", "tensor_copy", reads=[bC], writes=[bC], out=blockb[:], in_=C["blockones"])
    P.I("vector", "tensor_copy", reads=[bC], writes=[bC], out=wg_b[:], in_=wg_f[:])
    P.I("vector", "tensor_scalar", reads=[bC], writes=[bC], out=C["negb"][:], in0=C["bgk"], scalar1=-1.0,
        scalar2=None, op0=ALU.mult)
    C["identb"] = identb
    C["blockones"] = blockb
    C["wgk2"] = wg_b
    return C


class NormStage:
    def __init__(self, P, A, PS, C, tag, nxt=2, bank_t=(7, 6)):
        self.P, self.C, self.PS, self.tag = P, C, PS, tag
        self.nxt = nxt
        self.xt = [A.alloc(tag + "xt", [128, 2, D], F32) for _ in range(nxt)]
        self.b_xt = [[Buf(tag + "xt%d_%d" % (i, t)) for t in range(2)] for i in range(nxt)]
        self.hb = A.alloc(tag + "hb", [128, 2, D], BF16)
        self.b_hb = [Buf(tag + "hb%d" % t) for t in range(2)]
        self.hT = A.alloc(tag + "hT", [128, 8, 256], BF16)
        self.b_hT = Buf(tag + "hT")
        self.sq = A.alloc(tag + "sq", [128, D], BF16)
        self.b_sq = Buf(tag + "sq")
        self.ss = A.alloc(tag + "ss", [128, 2], F32)
        self.rs = A.alloc(tag + "rs", [128, 2], F32)
        self.b_ss = [Buf(tag + "ss%d" % t) for t in range(2)]
        self.b_rs = [Buf(tag + "rs%d" % t) for t in range(2)]
        self.bank_t = bank_t
        self.psT = [PS[:, bank_t[t], :].bitcast(BF16) for t in range(2)]

    def load(self, i, srcs):
        P = self.P
        for t, (src, sb) in enumerate(srcs):
            P.I("sync", "dma_start", reads=sb, writes=[self.b_xt[i % self.nxt][t]], dma=True, key="xt%d_%d" % (i % self.nxt, t), out=self.xt[i % self.nxt][:, t, :], in_=src)

    def norm(self, i, nt, gain, b_gain, b_banks):
        P, C = self.P, self.C
        x = self.xt[i % self.nxt]
        bx = self.b_xt[i % self.nxt]
        for t in range(nt):
            P.I("scalar", "activation", reads=[bx[t]], writes=[self.b_sq, self.b_ss[t]], out=self.sq[:], in_=x[:, t, :], func=AF.Square,
                                                        accum_out=self.ss[:, t:t + 1])
            P.I("scalar", "activation", reads=[self.b_ss[t], C["b_const"]], writes=[self.b_rs[t]], out=self.rs[:, t:t + 1], in_=self.ss[:, t:t + 1], func=AF.Ln,
                                                        scale=1.0 / D, bias=C["eps"][:])
            P.I("scalar", "activation", reads=[self.b_rs[t]], writes=[self.b_rs[t]], out=self.rs[:, t:t + 1], in_=self.rs[:, t:t + 1],
                                                        func=AF.Exp, scale=-0.5)
            P.I("scalar", "activation", reads=[bx[t], self.b_rs[t]], writes=[self.b_hb[t]], out=self.hb[:, t, :], in_=x[:, t, :], func=AF.Copy,
                                                        scale=self.rs[:, t:t + 1])
        for t in range(nt):
            bb = b_banks[self.bank_t[t]]
            for kc in range(8):
                P.I("tensor", "transpose", reads=[self.b_hb[t], C["b_const"]], writes=bb, out=self.psT[t][:, kc * 128:(kc + 1) * 128],
                                                                   in_=self.hb[:, t, kc * 128:(kc + 1) * 128],
                                                                   identity=C["identb"][:])
            P.I("vector", "tensor_tensor", reads=bb + [b_gain], writes=[self.b_hT], out=self.hT[:, :, t * 128:(t + 1) * 128],
                in0=self.psT[t].rearrange("p (c n) -> p c n", c=8),
                in1=gain[:].unsqueeze(2).to_broadcast([128, 8, 128]), op=ALU.mult)


def load_win(P, A, w_in, mixg, part):
    c0, c1 = (2304, PROJ) if part == "gla" else (0, 2304)
    W = {"c0": c0}
    W["w"] = A.alloc("win" + part, [128, 8, c1 - c0], BF16)
    W["g"] = A.alloc("mixg" + part, [128, 8], F32)
    W["b_w"] = Buf("win" + part)
    W["b_g"] = Buf("mixg" + part)
    P.I("sync", "dma_start", writes=[W["b_g"]], dma=True, key="mixg" + part, out=W["g"][:], in_=mixg)
    hw = (c1 - c0) // 2
    for kc in range(8):
        for hf in range(2):
            P.I("gpsimd", "dma_start", writes=[W["b_w"]], dma=True, key="win" + part,
                out=W["w"][:, kc, hf * hw:(hf + 1) * hw],
                in_=w_in[kc * 128:(kc + 1) * 128, c0 + hf * hw:c0 + (hf + 1) * hw])
    return W


def proj_phase(P, A, PS, C, W, QT, KT, b_QT, b_KT, io, part):
    m0 = A.mark()
    NS = NormStage(P, A, PS, C, "pj" + part, nxt=2, bank_t=(7, 6))
    gla = (part == "gla")
    c0 = W["c0"]
    bk = [[Buf("pjps%d_%d" % (b, h), excl=True) for h in range(2)] for b in range(8)]
    full = lambda b: [bk[b][0], bk[b][1]]
    b_banks = {b: full(b) for b in range(8)}
    Win = W["w"]
    bW = W["b_w"]
    bC = C["b_const"]

    def sb(name, shape, dt, n=1):
        ts = [A.alloc("pj" + name, shape, dt) for _ in range(n)]
        bs = [Buf("pj%s%d" % (name, i)) for i in range(n)]
        return (ts, bs) if n > 1 else (ts[0], bs[0])

    if not gla:
        sqk, b_sqk = sb("sqk", [128, 256], BF16, 2)
        rq, b_rq = sb("rq", [128, 256], F32, 2)
        kf, b_kf = sb("kf", [128, 256], F32, 2)
        klo, b_klo = sb("klo", [128, 256], BF16, 2)
        kout, b_kout = sb("kout", [128, WA], F32, 2)
        vout, b_vout = sb("vout", [128, WA], F32, 2)
        vaug, b_vaug = sb("vaug", [128, NH, 65], BF16, 2)
    else:
        glrT, b_glrT = sb("glrT", [16, 256], BF16)
        e1, b_e1 = sb("e1", [128, 256], F32)
        csm, b_csm = sb("csm", [128, 256], F32)
        eq, b_eq = sb("eq", [128, 256], F32)
        ek, b_ek = sb("ek", [128, 256], F32)
        qtil, b_qtil = sb("qtil", [128, 256], BF16)
        ktil, b_ktil = sb("ktil", [128, 256], BF16)
        ktm, b_ktm = sb("ktm", [128, 4, 256], BF16)
        ktok, b_ktok = sb("ktok", [128, 128], BF16, 2)
        vb, b_vb = sb("vb", [128, 256], BF16, 2)
        er, b_er = sb("er", [128, 256], F32, 2)
        srb, b_srb = sb("srb", [128, 256], F32, 2)
        amt, b_amt = sb("amt", [128, 4, 128], BF16, 2)
        t1, b_t1 = sb("t1", [128, 256], F32)
        sbd, b_sbd = sb("sbd", [128, 256], F32)
        sbdb, b_sbdb = sb("sbdb", [128, 256], BF16)
        osq, b_osq = sb("osq", [128, 256], F32)
        ssum, b_ssum = sb("ssum", [128, 4], F32)
        r4, b_r4 = sb("r4", [128, 4], F32)
        obf, b_obf = sb("obf", [128, 256], F32)
        obb, b_obb = sb("obb", [128, 256], BF16, 2)
        qm, b_qm = sb("qm", [128, NSEQ_S, 128], BF16)
        ktokm, b_ktokm = sb("ktokm", [128, NSEQ_S, 128], BF16)
        s0f, b_s0f = sb("s0f", [128, NSEQ_S, 64], F32)
        s0bd, b_s0bd = sb("s0bd", [128, NSEQ_S, 256], BF16)
        tS, b_tS = sb("tS", [128, 4, 256], F32)
        snew, b_snew = sb("snew", [128, NSEQ_S, 64], F32)
        gfin, b_gfin = sb("gfin", [128, 64], F32)

    if not gla:
        for i in range(2):
            P.I("vector", "memset", writes=[b_vaug[i]], ap=vaug[i][:], constant=1.0)
    else:
        P.I("vector", "memset", writes=[b_sbd], ap=sbd[:], constant=0.0)
        P.I("vector", "memset", writes=[b_sbdb], ap=sbdb[:], constant=0.0)
        P.I("vector", "memset", writes=[b_qm], ap=qm[:], constant=0.0)
        wqk_f, b_wqkf = sb("wqkf", [128, 8, 256], F32)
        wqk_hi, b_wqkh = sb("wqkh", [128, 8, 256], BF16)
        wqk_lo, b_wqkl = sb("wqkl", [128, 8, 256], BF16)
        hf32, b_hf32 = sb("hf32", [128, D], F32)
        hlo, b_hlo = sb("hlo", [128, D], BF16)
        hT0h, b_hT0h = sb("hT0h", [128, 8, 128], BF16)
        hT0l, b_hT0l = sb("hT0l", [128, 8, 128], BF16)
        qv, b_qv = sb("qv", [128, 128], F32)
        kv, b_kv = sb("kv", [128, 128], F32)
        qlo, b_qlo = sb("qlo", [128, 128], BF16)
        klo_, b_klo_ = sb("klo_", [128, 128], BF16)
        ktml, b_ktml = sb("ktml", [128, 4, 128], BF16)
        P.I("sync", "dma_start", writes=[b_wqkf], dma=True, key="wqkf", out=wqk_f[:],
            in_=io["w_in_d"][:, 2304:2560].rearrange("(c p) n -> p c n", p=128))
        P.I("vector", "tensor_tensor", reads=[b_wqkf, W["b_g"]], writes=[b_wqkf], out=wqk_f[:], in0=wqk_f[:],
            in1=W["g"][:].unsqueeze(2).to_broadcast([128, 8, 256]), op=ALU.mult)
        P.I("vector", "tensor_copy", reads=[b_wqkf], writes=[b_wqkh], out=wqk_hi[:], in_=wqk_f[:])
        P.I("vector", "tensor_tensor", reads=[b_wqkf, b_wqkh], writes=[b_wqkl], out=wqk_lo[:], in0=wqk_f[:], in1=wqk_hi[:],
            op=ALU.subtract)
        P.I("sync", "dma_start", writes=[b_s0f], dma=True, key="s0f", out=s0f[:], in_=io["sg"].rearrange("s h k v -> (h k) s v"))
        P.I("vector", "tensor_tensor", reads=[b_s0f, bC], writes=[b_s0bd], out=s0bd[:].rearrange("p s (h v) -> p s h v", h=4),
            in0=s0f[:].unsqueeze(2).to_broadcast([128, NSEQ_S, 4, 64]),
            in1=C["bdmask"].rearrange("p (h v) -> p h v", h=4).unsqueeze(1).to_broadcast([128, NSEQ_S, 4, 64]),
            op=ALU.mult)

    nsup = 33 if gla else 17
    sup_list = io.get("sup_list") or list(range(nsup))
    tiles_of = (lambda i: [i]) if gla else (lambda i: [2 * i, 2 * i + 1] if i < 16 else [32])

    def do_load(ii):
        NS.load(ii, [(io["X1"][g * 128:(g + 1) * 128, :], [io["b_X1"][g]]) for g in tiles_of(sup_list[ii])])

    do_load(0)
    for ii, i in enumerate(sup_list):
        gts = tiles_of(i)
        nt = len(gts)
        N = nt * 128
        sample = (gts[0] == 32)
        col0 = gts[0] * 128
        if ii + 1 < len(sup_list):
            do_load(ii + 1)
        NS.norm(ii, nt, W["g"], W["b_g"], b_banks)
        hT, b_hT = NS.hT, NS.b_hT
        need_out = [(sample or g >= 16) and not io.get("no_out") for g in gts]
        need_vout = [(sample or g >= 16) and not io.get("no_vout") for g in gts]
        wide = [3, 5]

        if not gla:
            def proj_chunk(j):
                fb_ = j % 2
                for kc in range(8):
                    P.I("tensor", "matmul", reads=[bW, b_hT], writes=[bk[fb_][0]], out=PS[:, fb_, 0:N], lhsT=Win[:, kc, j * 128:(j + 1) * 128], rhs=hT[:, kc, 0:N],
                        start=(kc == 0), stop=(kc == 7))
            proj_chunk(0)
            for j in range(12):
                fb, fo = j % 2, 0
                b_f = bk[fb][0]
                if j + 1 < 12:
                    proj_chunk(j + 1)
                s2 = j % 2
                P.I("scalar", "activation", reads=[b_f], writes=[b_sqk[s2]], out=sqk[s2][:, 0:N], in_=PS[:, fb, fo:fo + N],
                                                                            func=AF.Square)
                P.I("tensor", "matmul", reads=[b_sqk[s2], bC], writes=[bk[2][0]], out=PS[:, 2, 0:N], lhsT=C["blockones"][:],
                                                          rhs=sqk[s2][:, 0:N], start=True, stop=True)
                P.I("scalar", "activation", reads=[bk[2][0], bC], writes=[b_rq[s2]], out=rq[s2][:, 0:N], in_=PS[:, 2, 0:N],
                                                              func=AF.Ln, scale=1.0 / HD, bias=C["eps"][:])
                P.I("scalar", "activation", reads=[b_rq[s2]], writes=[b_rq[s2]], out=rq[s2][:, 0:N], in_=rq[s2][:, 0:N], func=AF.Exp,
                                                              scale=-0.5)
                P.I("vector", "tensor_tensor", reads=[b_f, b_rq[s2]], writes=[b_kf[s2]], out=kf[s2][:, 0:N], in0=PS[:, fb, fo:fo + N], in1=rq[s2][:, 0:N], op=ALU.mult)
                if j < 6:
                    P.I("vector", "tensor_scalar", reads=[b_kf[s2], bC], writes=[b_QT], out=QT[:, j, col0:col0 + N], in0=kf[s2][:, 0:N], scalar1=C["qg"][:, 0:1], scalar2=0.125,
                        op0=ALU.mult, op1=ALU.mult)
                else:
                    jj = j - 6
                    P.I("vector", "tensor_scalar", reads=[b_kf[s2], bC], writes=[b_kf[s2]], out=kf[s2][:, 0:N], in0=kf[s2][:, 0:N], scalar1=C["kg"][:, 0:1], scalar2=None, op0=ALU.mult)
                    P.I("scalar", "activation", reads=[b_kf[s2]], writes=[b_KT], out=KT[:, jj, col0:col0 + N], in_=kf[s2][:, 0:N],
                                                                         func=AF.Copy)
                    if any(need_out):
                        P.I("vector", "tensor_tensor", reads=[b_kf[s2], b_KT], writes=[b_klo[s2]], out=klo[s2][:, 0:N],
                            in0=kf[s2][:, 0:N], in1=KT[:, jj, col0:col0 + N], op=ALU.subtract)
                    for t in range(nt):
                        if need_out[t]:
                            wb = wide[t] + (jj * 128) // 512
                            wo = (jj * 128) % 512
                            P.I("tensor", "matmul", reads=[b_KT, bC], writes=full(wb), out=PS[:, wb, wo:wo + 128],
                                lhsT=KT[:, jj, col0 + t * 128:col0 + (t + 1) * 128], rhs=C["identb"][:], start=True, stop=False)
                            P.I("tensor", "matmul", reads=[b_klo[s2], bC], writes=full(wb), out=PS[:, wb, wo:wo + 128],
                                lhsT=klo[s2][:, t * 128:(t + 1) * 128], rhs=C["identb"][:], start=False, stop=True)
            for t in range(nt):
                if need_out[t]:
                    g = gts[t]
                    P.I("scalar", "activation", reads=full(wide[t]), writes=[b_kout[t]], out=kout[t][:, 0:512], in_=PS[:, wide[t], :], func=AF.Copy)
                    P.I("vector", "tensor_copy", reads=full(wide[t] + 1), writes=[b_kout[t]], out=kout[t][:, 512:768], in_=PS[:, wide[t] + 1, 0:256])
                    if sample:
                        for s in range(NSEQ_S):
                            o = P.I("sync", "dma_start", reads=[b_kout[t]], writes=[io["b_wks_new"]], dma=True, key="kout%d" % t, out=io["wks"][s, WBUF - TS:WBUF, :],
                                                                             in_=kout[t][s * TS:(s + 1) * TS, :])
                        P.must_finish(o)
                    else:
                        r0 = (g - 16) * 128
                        o = P.I("sync", "dma_start", reads=[b_kout[t]], writes=[Buf("wkp")], dma=True, key="kout%d" % t, out=io["wkp"][r0:r0 + 128, :], in_=kout[t][:])
                        P.must_finish(o)

            for t in range(nt):
                g = gts[t]
                for hf in range(2):
                    for kc in range(8):
                        P.I("tensor", "matmul", reads=[bW, b_hT], writes=full(wide[t] + hf), out=PS[:, wide[t] + hf, 0:384], lhsT=hT[:, kc, t * 128:(t + 1) * 128],
                            rhs=Win[:, kc, 1536 + hf * 384:1536 + (hf + 1) * 384], start=(kc == 0), stop=(kc == 7))
                for hf in range(2):
                    P.I("vector", "tensor_copy", reads=full(wide[t] + hf), writes=[b_vaug[t]], out=vaug[t][:, hf * 6:(hf + 1) * 6, 0:64],
                        in_=PS[:, wide[t] + hf, 0:384].rearrange("p (h v) -> p h v", h=6))
                    if need_vout[t]:
                        P.I("vector", "tensor_copy", reads=full(wide[t] + hf), writes=[b_vout[t]],
                            out=vout[t][:, hf * 384:(hf + 1) * 384], in_=PS[:, wide[t] + hf, 0:384])
                P.I("sync", "dma_start", reads=[b_vaug[t]], writes=[io["b_Vs"][g]], dma=True, key="vaug%d" % t, out=io["Vs"][g * 128:(g + 1) * 128, :],
                                                             in_=vaug[t][:].rearrange("p h v -> p (h v)"))
                if need_vout[t]:
                    if sample:
                        for s in range(NSEQ_S):
                            o = P.I("sync", "dma_start", reads=[b_vout[t]], writes=[io["b_wvs_new"]], dma=True, key="vout%d" % t, out=io["wvs"][s, WBUF - TS:WBUF, :],
                                                                             in_=vout[t][s * TS:(s + 1) * TS, :])
                        P.must_finish(o)
                    elif not io.get("no_vdma"):
                        r0 = (g - 16) * 128
                        o = P.I("sync", "dma_start", reads=[b_vout[t]], writes=[Buf("wvp")], dma=True, key="vout%d" % t, out=io["wvp"][r0:r0 + 128, :], in_=vout[t][:])
                        P.must_finish(o)

        if gla:
            def fm(colbase, M, slot):
                fb, fo = slot, 0
                for kc in range(8):
                    P.I("tensor", "matmul", reads=[bW, b_hT], writes=[bk[fb][0]], out=PS[0:M, fb, fo:fo + N], lhsT=Win[:, kc, colbase:colbase + M],
                                                             rhs=hT[:, kc, 0:N], start=(kc == 0), stop=(kc == 7))
                return PS[0:M, fb, fo:fo + N], bk[fb][0]
            precise = (gts[0] == 0)
            if precise:
                x0 = NS.xt[ii % NS.nxt]
                bx0 = NS.b_xt[ii % NS.nxt][0]
                P.I("scalar", "activation", reads=[bx0, NS.b_rs[0]], writes=[b_hf32], out=hf32[:], in_=x0[:, 0, :], func=AF.Copy,
                    scale=NS.rs[:, 0:1])
                P.I("vector", "tensor_tensor", reads=[b_hf32, NS.b_hb[0]], writes=[b_hlo], out=hlo[:], in0=hf32[:], in1=NS.hb[:, 0, :],
                    op=ALU.subtract)
                P.I("vector", "tensor_copy", reads=full(7), writes=[b_hT0h], out=hT0h[:].rearrange("p c n -> p (c n)"), in_=NS.psT[0])
                pl = PS[:, 5, :].bitcast(BF16)
                for kc in range(8):
                    P.I("tensor", "transpose", reads=[b_hlo, bC], writes=full(5), out=pl[:, kc * 128:(kc + 1) * 128],
                        in_=hlo[:, kc * 128:(kc + 1) * 128], identity=C["identb"][:])
                P.I("vector", "tensor_copy", reads=full(5), writes=[b_hT0l], out=hT0l[:].rearrange("p c n -> p (c n)"), in_=pl)

                def fm3(col, slot):
                    combos = ((wqk_hi, b_wqkh, hT0h, b_hT0h), (wqk_lo, b_wqkl, hT0h, b_hT0h), (wqk_hi, b_wqkh, hT0l, b_hT0l))
                    n_mm = 0
                    for wt, bw, ht, bh in combos:
                        for kc in range(8):
                            P.I("tensor", "matmul", reads=[bw, bh], writes=[bk[slot][0]], out=PS[:, slot, 0:128],
                                lhsT=wt[:, kc, col:col + 128], rhs=ht[:, kc, :], start=(n_mm == 0), stop=(n_mm == 23))
                            n_mm += 1
                    return PS[:, slot, 0:128], bk[slot][0]
                ps_qb, b_pqb = fm3(0, 0)
                ps_kb, b_pkb = fm3(128, 1)
            else:
                ps_qb, b_pqb = fm(2304 - c0, 128, 0)
                ps_kb, b_pkb = fm(2432 - c0, 128, 1)
            ps_gl, b_pgl = fm(3072 - c0, 16, 2)
            P.I("scalar", "activation", reads=[b_pgl], writes=[b_glrT], out=glrT[:, 0:N], in_=ps_gl, func=AF.Copy)
            ps_xg = PS[:, 6, 0:N]
            P.I("tensor", "matmul", reads=[b_glrT, bC], writes=[bk[6][0]], out=ps_xg, lhsT=C["wgk2"][:], rhs=glrT[:, 0:N], start=True, stop=True)
            P.I("scalar", "activation", reads=[bk[6][0], bC], writes=[b_e1], out=e1[:, 0:N], in_=ps_xg, func=AF.Exp, scale=-1.0, bias=C["negb"][:])
            P.I("scalar", "activation", reads=[b_e1, bC], writes=[b_e1], out=e1[:, 0:N], in_=e1[:, 0:N], func=AF.Ln, scale=1.0, bias=C["one"][:])
            for t in range(nt):
                d0 = C["notstart"] if sample else C["ones128"]
                P.I("vector", "tensor_tensor_scan", reads=[b_e1, bC], writes=[b_csm], out=csm[:, t * 128:(t + 1) * 128], data0=d0[:], data1=e1[:, t * 128:(t + 1) * 128], initial=0.0,
                    op0=ALU.mult, op1=ALU.add)
            P.I("scalar", "activation", reads=[b_csm], writes=[b_eq], out=eq[:, 0:N], in_=csm[:, 0:N], func=AF.Exp, scale=-1.0 / 16)
            P.I("scalar", "activation", reads=[b_csm], writes=[b_ek], out=ek[:, 0:N], in_=csm[:, 0:N], func=AF.Exp, scale=1.0 / 16)
            P.I("vector", "scalar_tensor_tensor", reads=[b_pqb, b_eq], writes=[b_qtil], out=qtil[:, 0:N], in0=ps_qb, scalar=32.0 ** -0.5,
                                                            in1=eq[:, 0:N], op0=ALU.mult, op1=ALU.mult)
            P.I("vector", "tensor_tensor", reads=[b_pkb, b_ek], writes=[b_ktil], out=ktil[:, 0:N], in0=ps_kb, in1=ek[:, 0:N], op=ALU.mult)
            for h in range(4):
                P.I("vector", "scalar_tensor_tensor", reads=[b_pkb, b_ek, bC], writes=[b_ktm], out=ktm[:, h, 0:N], in0=ps_kb, scalar=C["headmask"][:, h:h + 1], in1=ek[:, 0:N],
                    op0=ALU.mult, op1=ALU.mult)

            if precise:
                P.I("vector", "scalar_tensor_tensor", reads=[b_pqb, b_eq], writes=[b_qv], out=qv[:], in0=ps_qb, scalar=32.0 ** -0.5,
                    in1=eq[:, 0:128], op0=ALU.mult, op1=ALU.mult)
                P.I("vector", "tensor_tensor", reads=[b_qv, b_qtil], writes=[b_qlo], out=qlo[:], in0=qv[:], in1=qtil[:, 0:128],
                    op=ALU.subtract)
                P.I("vector", "tensor_tensor", reads=[b_pkb, b_ek], writes=[b_kv], out=kv[:], in0=ps_kb, in1=ek[:, 0:128], op=ALU.mult)
                P.I("vector", "tensor_tensor", reads=[b_kv, b_ktil], writes=[b_klo_], out=klo_[:], in0=kv[:], in1=ktil[:, 0:128],
                    op=ALU.subtract)
                for h in range(4):
                    P.I("vector", "tensor_scalar", reads=[b_klo_, bC], writes=[b_ktml], out=ktml[:, h, :], in0=klo_[:],
                        scalar1=C["headmask"][:, h:h + 1], scalar2=None, op0=ALU.mult)
            for t in range(nt):
                g = gts[t]
                cs = slice(t * 128, (t + 1) * 128)
                pkt = PS[:, 2, 0:64].bitcast(BF16)
                P.I("tensor", "transpose", reads=[b_ktil, bC], writes=[bk[2][0]], out=pkt, in_=ktil[:, cs], identity=C["identb"][:])
                P.I("vector", "tensor_copy", reads=[bk[2][0]],
                     writes=[b_ktok[t]], out=ktok[t][:], in_=pkt)
                for kc in range(8):
                    P.I("tensor", "matmul", reads=[bW, b_hT], writes=full(7), out=PS[:, 7, :], lhsT=hT[:, kc, cs], rhs=Win[:, kc, 2560 - c0:3072 - c0],
                                                                    start=(kc == 0), stop=(kc == 7))
                P.I("scalar", "activation", reads=full(7), writes=[b_vb[t]], out=vb[t][:], in_=PS[:, 7, 0:256], func=AF.Copy)
                P.I("scalar", "activation", reads=full(7), writes=[b_er[t]], out=er[t][:], in_=PS[:, 7, 256:512], func=AF.Exp, scale=-1.0)
                P.I("vector", "tensor_scalar_add", reads=[b_er[t]], writes=[b_er[t]], out=er[t][:], in0=er[t][:], scalar1=1.0)
                P.I("vector", "reciprocal", reads=[b_er[t]], writes=[b_er[t]], out=er[t][:], in_=er[t][:])
                P.I("vector", "tensor_tensor", reads=[b_er[t]] + full(7), writes=[b_srb[t]], out=srb[t][:], in0=er[t][:], in1=PS[:, 7, 256:512], op=ALU.mult)
                P.I("vector", "tensor_tensor", reads=[b_srb[t], bC], writes=[b_srb[t]], out=srb[t][:], in0=srb[t][:], in1=C["gg"], op=ALU.mult)
                wa = wide[t]
                for h in range(4):
                    P.I("tensor", "matmul", reads=[b_ktm, b_qtil], writes=full(wa), out=PS[:, wa, h * 128:(h + 1) * 128], lhsT=ktm[:, h, cs],
                                                                          rhs=qtil[:, cs], start=True, stop=not precise)
                    if precise:
                        P.I("tensor", "matmul", reads=[b_ktm, b_qlo], writes=full(wa), out=PS[:, wa, h * 128:(h + 1) * 128],
                            lhsT=ktm[:, h, cs], rhs=qlo[:], start=False, stop=False)
                        P.I("tensor", "matmul", reads=[b_ktml, b_qtil], writes=full(wa), out=PS[:, wa, h * 128:(h + 1) * 128],
                            lhsT=ktml[:, h, :], rhs=qtil[:, cs], start=False, stop=True)
                cm = C["caus_s"] if sample else C["caus"]
                P.I("vector", "tensor_tensor", reads=full(wa) + [bC], writes=[b_amt[t]], out=amt[t][:], in0=PS[:, wa, :].rearrange("p (h n) -> p h n", h=4),
                    in1=cm[:].unsqueeze(1).to_broadcast([128, 4, 128]), op=ALU.mult)
                po = PS[:, wa + 1, 0:256]
                b_po = bk[wa + 1][0]
                pS = PS[:, wa + 2, 0:256]
                b_pS = bk[wa + 2][0]
                if not sample:
                    P.I("tensor", "matmul", reads=[b_qtil, b_sbdb], writes=[b_po], out=po, lhsT=qtil[:, cs], rhs=sbdb[:], start=True, stop=False)
                else:
                    P.I("vector", "tensor_copy", reads=[b_qtil], writes=[b_qm], out=bass.AP(qm[:].tensor, qm[:].offset, [list(qm[:].ap[0]), [128 + TS, NSEQ_S], [1, TS]]),
                        in_=qtil[:, 0:128].rearrange("p (s t) -> p s t", s=NSEQ_S))
                    for s in range(NSEQ_S):
                        P.I("tensor", "matmul", reads=[b_qm, b_s0bd], writes=[b_po], out=po, lhsT=qm[:, s, :], rhs=s0bd[:, s, :],
                                                                     start=(s == 0), stop=False)
                for h in range(4):
                    P.I("tensor", "matmul", reads=[b_amt[t], b_vb[t]], writes=[b_po], out=PS[:, wa + 1, h * 64:(h + 1) * 64], lhsT=amt[t][:, h, :], rhs=vb[t][:, h * 64:(h + 1) * 64],
                        start=False, stop=(h == 3))
                if not sample:
                    P.I("tensor", "matmul", reads=[b_ktok[t], b_vb[t]], writes=[b_pS], out=pS, lhsT=ktok[t][:], rhs=vb[t][:], start=True, stop=True)
                    ebl = eq[:, t * 128 + 127:t * 128 + 128]
                    P.I("vector", "scalar_tensor_tensor", reads=[b_pS, b_eq, bC], writes=[b_t1], out=t1[:], in0=pS, scalar=ebl, in1=C["bdmask"], op0=ALU.mult, op1=ALU.mult)
                    P.I("vector", "scalar_tensor_tensor", reads=[b_sbd, b_t1, b_eq], writes=[b_sbd], out=sbd[:], in0=sbd[:], scalar=ebl, in1=t1[:], op0=ALU.mult, op1=ALU.add)
                    P.I("scalar", "activation", reads=[b_sbd],
                         writes=[b_sbdb], out=sbdb[:], in_=sbd[:], func=AF.Copy)
                P.I("scalar", "activation", reads=[b_po], writes=[b_osq], out=osq[:], in_=po, func=AF.Square)
                P.I("vector", "tensor_reduce", reads=[b_osq], writes=[b_ssum], out=ssum[:], in_=osq[:].rearrange("p (h v) -> p h v", h=4),
                                                         axis=AX.X, op=ALU.add)
                P.I("scalar", "activation", reads=[b_ssum, bC], writes=[b_r4], out=r4[:], in_=ssum[:], func=AF.Ln, scale=1.0 / 64, bias=C["eps"][:])
                P.I("scalar", "activation", reads=[b_r4],
                     writes=[b_r4], out=r4[:], in_=r4[:], func=AF.Exp, scale=-0.5)
                P.I("vector", "tensor_tensor", reads=[b_po, b_r4], writes=[b_obf], out=obf[:].rearrange("p (h v) -> p h v", h=4), in0=po.rearrange("p (h v) -> p h v", h=4),
                    in1=r4[:].unsqueeze(2).to_broadcast([128, 4, 64]), op=ALU.mult)
                P.I("vector", "tensor_tensor", reads=[b_obf, b_srb[t]], writes=[b_obb[t]], out=obb[t][:], in0=obf[:], in1=srb[t][:], op=ALU.mult)
                P.I("sync", "dma_start", reads=[b_obb[t]], writes=[io["b_OB"][g]], dma=True, key="obb%d" % t, out=io["OB"][g * 128:(g + 1) * 128, :], in_=obb[t][:])

            if sample:
                P.I("vector", "tensor_tensor", reads=[b_ktok[0], bC], writes=[b_ktokm], out=ktokm[:], in0=ktok[0][:].unsqueeze(1).to_broadcast([128, NSEQ_S, 128]),
                    in1=C["seqmask"].unsqueeze(2).to_broadcast([128, NSEQ_S, 128]), op=ALU.mult)
                for gq in range(4):
                    for s4 in range(4):
                        s = gq * 4 + s4
                        bnk = 3 + s4 // 2
                        off = (s4 % 2) * 256
                        P.I("tensor", "matmul", reads=[b_ktokm, b_vb[0]], writes=[bk[bnk][s4 % 2]], out=PS[:, bnk, off:off + 256], lhsT=ktokm[:, s, :], rhs=vb[0][:], start=True, stop=True)
                    P.I("vector", "tensor_tensor", reads=full(3) + full(4) + [bC], writes=[b_tS], out=tS[:].rearrange("p s (h v) -> p s h v", h=4),
                        in0=PS[:, 3:5, :].rearrange("p b (s h v) -> p (b s) h v", s=2, h=4),
                        in1=C["bdmask"].rearrange("p (h v) -> p h v", h=4).unsqueeze(1).to_broadcast([128, 4, 4, 64]),
                        op=ALU.mult)
                    P.I("vector", "tensor_reduce", reads=[b_tS], writes=[b_snew], out=snew[:, gq * 4:(gq + 1) * 4, :], in_=tS[:].rearrange("p s (h v) -> p s v h", h=4),
                        axis=AX.X, op=ALU.add)
                P.I("vector", "tensor_tensor", reads=[b_snew, b_s0f], writes=[b_snew], out=snew[:], in0=snew[:], in1=s0f[:], op=ALU.add)
                P.I("vector", "tensor_tensor", reads=[b_snew, b_eq], writes=[b_snew], out=snew[:], in0=snew[:],
                    in1=eq[:, 0:128].rearrange("p (s t) -> p s t", t=TS)[:, :, TS - 1:TS].to_broadcast([128, NSEQ_S, 64]),
                    op=ALU.mult)
                o = P.I("sync", "dma_start", reads=[b_snew], writes=[Buf("gls")], dma=True, key="gls", out=io["gls"].rearrange("s h k v -> (h k) s v"), in_=snew[:])
                P.must_finish(o)
            if gts[-1] == 31:
                P.I("vector", "tensor_tensor", reads=[b_sbd, bC], writes=[b_t1], out=t1[:], in0=sbd[:], in1=C["bdmask"], op=ALU.mult)
                P.I("vector", "tensor_reduce", reads=[b_t1], writes=[b_gfin], out=gfin[:], in_=t1[:].rearrange("p (h v) -> p v h", h=4),
                                                         axis=AX.X, op=ALU.add)
                o = P.I("sync", "dma_start", reads=[b_gfin], writes=[Buf("glp")],
                         dma=True, key="glp", out=io["glp"], in_=gfin[:])
                P.must_finish(o)
    A.release(m0)


BRANCHES = ((128, 1), (512, 4), (2048, 16))


def _bucket(dist):
    dist = np.asarray(dist, np.int64)
    d = np.maximum(dist, 1).astype(np.float32)
    large = 16 + (np.log(d / np.float32(16)) / np.float32(math.log(2048 / 16)) * np.float32(16)).astype(np.int32)
    large = np.minimum(large, 31)
    return np.where(dist < 16, dist, large)


def make_onehots():
    ohp = np.zeros((3, 32, 384), np.float32)
    for bi, (w, d) in enumerate(BRANCHES):
        for i in range(383):
            st = i - 127
            if 0 <= st <= 128:
                ohp[bi, _bucket(st * d), i] = 1.0

    def mult(dist):
        return (dist <= 128) * 1 + ((dist % 4 == 0) and dist <= 512) * 1 + ((dist % 16 == 0) and dist <= 2048) * 1
    ohs = np.zeros((5, 32, 136), np.float32)
    for ty in range(5):
        dmin = (512 - 128 * ty - 127) if ty < 4 else -127
        for j in range(135):
            dist = dmin + j
            if dist >= 0:
                ohs[ty, _bucket(dist), j] = mult(dist)
    ohg = np.zeros((32, 128), np.float32)
    for p in range(96):
        ohg[_bucket(2048 - 16 * p), p] = 1.0
    return ohp, ohs, ohg


def attn_phase(P, A, PS, C, QT, KT, b_QT, b_KT, io):
    m0 = A.mark()
    bC = C["b_const"]
    bk = [Buf("atps%d" % b, excl=True) for b in range(8)]

    def sb(name, shape, dt, n=1):
        ts = [A.alloc("at" + name, shape, dt) for _ in range(n)]
        bs = [Buf("at%s%d" % (name, i)) for i in range(n)]
        return (ts, bs) if n > 1 else (ts[0], bs[0])

    EBp, b_EBp = sb("EBp", [128, NH, 256], F32, 3)
    EBs, b_EBs = sb("EBs", [128, NH, TS], F32, 13)
    m_setup = A.mark()
    rb_f, b_rbf = sb("rbf", [32, NH], F32)
    e_f, b_ef = sb("ef", [32, NH], F32)
    e_hi, b_ehi = sb("ehi", [32, NH], BF16)
    e_lo, b_elo = sb("elo", [32, NH], BF16)
    ebc_hi, b_ebh = sb("ebchi", [32, NH, 128], BF16)
    ebc_lo, b_ebl = sb("ebclo", [32, NH, 128], BF16)
    ohp_f, b_ohpf = sb("ohpf", [32, 3, 384], F32)
    ohp_b, b_ohpb = sb("ohpb", [32, 3, 384], BF16)
    ohs_f, b_ohsf = sb("ohsf", [32, 5, 136], F32)
    ohs_b, b_ohsb = sb("ohsb", [32, 5, 136], BF16)
    ohg_f, b_ohgf = sb("ohgf", [32, 128], F32)
    ohg_b, b_ohgb = sb("ohgb", [32, 128], BF16)
    fsb, b_fsb = sb("fsb", [128, NH, 384], F32)
    gsb, b_gsb = sb("gsb", [128, NH], F32)

    P.I("sync", "dma_start", writes=[b_rbf], dma=True, key="at_rb", out=rb_f[:], in_=io["rel_bias"])
    P.I("sync", "dma_start", writes=[b_ohpf], dma=True, key="at_ohp", out=ohp_f[:], in_=io["ohp"].rearrange("t b i -> b t i"))
    P.I("sync", "dma_start", writes=[b_ohsf], dma=True, key="at_ohs", out=ohs_f[:], in_=io["ohs"].rearrange("t b i -> b t i"))
    P.I("sync", "dma_start", writes=[b_ohgf], dma=True, key="at_ohg", out=ohg_f[:], in_=io["ohg"])
    P.I("scalar", "activation", reads=[b_rbf], writes=[b_ef], out=e_f[:], in_=rb_f[:], func=AF.Exp)
    P.I("vector", "tensor_copy", reads=[b_ef], writes=[b_ehi], out=e_hi[:], in_=e_f[:])
    P.I("vector", "tensor_tensor", reads=[b_ef, b_ehi], writes=[b_elo], out=e_lo[:], in0=e_f[:], in1=e_hi[:], op=ALU.subtract)
    P.I("vector", "tensor_copy", reads=[b_ehi], writes=[b_ebh], out=ebc_hi[:], in_=e_hi[:].unsqueeze(2).to_broadcast([32, NH, 128]))
    P.I("vector", "tensor_copy", reads=[b_elo], writes=[b_ebl], out=ebc_lo[:], in_=e_lo[:].unsqueeze(2).to_broadcast([32, NH, 128]))
    P.I("vector", "tensor_copy", reads=[b_ohpf], writes=[b_ohpb], out=ohp_b[:], in_=ohp_f[:])
    P.I("vector", "tensor_copy", reads=[b_ohsf], writes=[b_ohsb], out=ohs_b[:], in_=ohs_f[:])
    P.I("vector", "tensor_copy", reads=[b_ohgf], writes=[b_ohgb], out=ohg_b[:], in_=ohg_f[:])

    FD = io["FD"]
    b_FD = [Buf("FD%d" % i) for i in range(8)]

    def toeplitz(ty, oh, Wd, ncols, dest, b_dest):
        for h in range(NH):
            bnk = h % 4
            for eb, b_e, first in ((ebc_hi, b_ebh, True), (ebc_lo, b_ebl, False)):
                P.I("tensor", "matmul", reads=[b_e, b_ohpb, b_ohsb], writes=[bk[bnk]], out=PS[:, bnk, 0:Wd],
                    lhsT=eb[:, h, :], rhs=oh, start=first, stop=not first)
            P.I("vector", "tensor_copy", reads=[bk[bnk]], writes=[b_fsb], out=fsb[:, h, 0:Wd], in_=PS[:, bnk, 0:Wd])
        fd = FD[ty]
        P.I("sync", "dma_start", reads=[b_fsb], writes=[b_FD[ty]], dma=True, key="at_fdw", out=fd.rearrange("p (h w) -> p h w", h=NH)[:, :, 0:Wd], in_=fsb[:, :, 0:Wd])
        src = bass.AP(fd.tensor, fd.offset + 127, [[NH * 384 - 1, 128], [384, NH], [1, ncols]])
        P.I("sync", "dma_start", reads=[b_FD[ty]], writes=[b_dest], dma=True, key="at_fdr%d" % ty, out=dest, in_=src)

    for bi in range(3):
        toeplitz(bi, ohp_b[:, bi, 0:383], 383, 256, EBp[bi][:], b_EBp[bi])
    for ty in range(5):
        toeplitz(3 + ty, ohs_b[:, ty, 0:135], 135, TS, EBs[ty][:], b_EBs[ty])
    for eb_, b_e, first in ((e_hi, b_ehi, True), (e_lo, b_elo, False)):
        P.I("tensor", "matmul", reads=[b_e, b_ohgb], writes=[bk[4]], out=PS[:, 4, 0:NH], lhsT=ohg_b[:], rhs=eb_[:],
            start=first, stop=not first)
    P.I("vector", "tensor_copy", reads=[bk[4]], writes=[b_gsb], out=gsb[:], in_=PS[:, 4, 0:NH])
    for r in range(8):
        P.I("vector", "memset", writes=[b_EBs[5 + r]], ap=EBs[5 + r][:], constant=0.0)
        P.I("vector", "tensor_copy", reads=[b_gsb], writes=[b_EBs[5 + r]], out=EBs[5 + r][:, :, r:r + 1], in_=gsb[:].unsqueeze(2))

    P.barrier()
    A.release(m_setup)
    vt, b_vt = sb("vt", [128, NH, 65], BF16, 4)
    pexp, b_pexp = sb("pexp", [128, 2, 256], F32, 2)
    pT, b_pT = sb("pT", [128, 2, 256], BF16, 2)
    osb, b_osb = sb("osb", [128, 780], F32, 2)
    Vs = io["Vs"]
    ocol = lambda h: (0, h * 65) if h < 7 else (1, (h - 7) * 65)
    if not io.get("skip_prompt_attn"):
        qi = 0
        vi = 0
        pending = []

        def flush():
            while pending:
                pending.pop(0)()
        for bi, (w, d) in enumerate(BRANCHES):
            nb = SEQ // d // 128
            for r in range(d):
                def load_v(n, slot, r=r, d=d):
                    src = bass.AP(Vs.tensor, Vs.offset + (r + d * 128 * n) * 780, [[d * 780, 128], [1, 780]])
                    P.I("sync", "dma_start", reads=io["b_Vs"][:32], writes=[b_vt[slot]], dma=True, key="at_vt%d" % slot,
                        out=vt[slot][:].rearrange("p h v -> p (h v)"), in_=src)
                slots = {}
                slots[0] = vi % 4
                load_v(0, vi % 4)
                vi += 1
                for n in range(nb):
                    if n + 1 < nb:
                        slots[n + 1] = vi % 4
                        load_v(n + 1, vi % 4)
                        vi += 1
                    cq = slice(r + d * 128 * n, r + d * 128 * (n + 1), d)
                    cp = slice(r + d * 128 * (n - 1), r + d * 128 * n, d) if n > 0 else None
                    ob = 4 + 2 * (qi % 2)
                    first_in_bank = [True, True]
                    for hp in range(6):
                        sbk = 2 * (hp % 2)
                        for hh in range(2):
                            pb = hh * 64
                            P.I("tensor", "matmul", reads=[b_QT, b_KT], writes=[bk[sbk + hh]], out=PS[:, sbk + hh, 0:128],
                                lhsT=KT[pb:pb + 64, hp, cq], rhs=QT[pb:pb + 64, hp, cq], start=True, stop=True)
                            if n > 0:
                                P.I("tensor", "matmul", reads=[b_QT, b_KT], writes=[bk[sbk + hh]], out=PS[:, sbk + hh, 128:256],
                                    lhsT=KT[pb:pb + 64, hp, cp], rhs=QT[pb:pb + 64, hp, cq], start=True, stop=True)
                        wc = 256 if n > 0 else 128
                        s2 = hp % 2
                        P.I("scalar", "activation", reads=[bk[sbk], bk[sbk + 1]], writes=[b_pexp[s2]], out=pexp[s2][:, :, 0:wc],
                            in_=PS[:, sbk:sbk + 2, 0:wc], func=AF.Exp)
                        P.I("vector", "tensor_tensor", reads=[b_pexp[s2], b_EBp[bi]], writes=[b_pT[s2]], out=pT[s2][:, :, 0:wc],
                            in0=pexp[s2][:, :, 0:wc], in1=EBp[bi][:, 2 * hp:2 * hp + 2, 0:wc], op=ALU.mult)

                        def pv(hp=hp, s2=s2, n=n, ob=ob, fib=first_in_bank, sl_n=slots[n], sl_p=slots.get(n - 1),
                               qi=qi, bi=bi, r=r, d=d):
                            for hh in range(2):
                                h = 2 * hp + hh
                                bo, co = ocol(h)
                                P.I("tensor", "matmul", reads=[b_pT[s2], b_vt[sl_n]], writes=[bk[ob + bo]],
                                    out=PS[:, ob + bo, co:co + 65], lhsT=pT[s2][:, hh, 0:128], rhs=vt[sl_n][:, h, :],
                                    start=fib[bo], stop=(n == 0), skip_group_check=True)
                                fib[bo] = False
                                if n > 0:
                                    P.I("tensor", "matmul", reads=[b_pT[s2], b_vt[sl_p]], writes=[bk[ob + bo]],
                                        out=PS[:, ob + bo, co:co + 65], lhsT=pT[s2][:, hh, 128:256], rhs=vt[sl_p][:, h, :],
                                        start=False, stop=True, skip_group_check=True)
                            if hp == 5:
                                o2 = qi % 2
                                P.I("vector", "tensor_copy", reads=[bk[ob]], writes=[b_osb[o2]], out=osb[o2][:, 0:455],
                                    in_=PS[:, ob, 0:455])
                                P.I("scalar", "activation", reads=[bk[ob + 1]], writes=[b_osb[o2]], out=osb[o2][:, 455:780],
                                    in_=PS[:, ob + 1, 0:325], func=AF.Copy)
                                Ob = io["Obr"][bi]
                                dst = bass.AP(Ob.tensor, Ob.offset + (r + d * 128 * n) * 780, [[d * 780, 128], [1, 780]])
                                P.I("sync", "dma_start", reads=[b_osb[o2]], writes=[io["b_Obr"][bi]], dma=True,
                                    key="at_osb%d" % o2, out=dst, in_=osb[o2][:])
                        flush()
                        pending.append(pv)
                    qi += 1
        flush()

    P.barrier()
    A.release(m_setup)
    if not io.get("skip_sample_attn"):
        ND = 4
        ktf, b_ktf = sb("ktf", [128, WA], F32, ND)
        vtf, b_vtf = sb("vtf", [128, WA], F32, ND)
        ktb, b_ktb = sb("ktb", [128, WA], BF16, ND)
        vau, b_vau = sb("vau", [128, NH, 65], BF16, ND)
        ktT, b_ktT = sb("ktT", [128, 6, 128], BF16, ND)
        pes, b_pes = sb("pes", [128, 2, 48], F32, 2)
        pTs, b_pTs = sb("pTs", [128, 2, 48], BF16, 2)
        vnew, b_vnew = sb("vnew", [TS, NH * 65], BF16, 2)
        osm, b_osm = sb("osm", [TS, 780], F32, 2)
        for i in range(ND):
            P.I("gpsimd", "memset", writes=[b_vau[i]], ap=vau[i][:], constant=1.0)
        ck, cv = io["ck"], io["cv"]
        recs = []
        for s in range(NSEQ_S):
            tl = [("A", i) for i in range(4)] + [("B", r) for r in range(8)] + [("N", 0)]
            for j, (kind, idx) in enumerate(tl):
                recs.append(dict(s=s, kind=kind, idx=idx, first=(j == 0), last=(kind == "N"), fib=None))
        fibs = {s: [True, True] for s in range(NSEQ_S)}

        def stA(t):
            rc = recs[t]
            s, kind, idx = rc["s"], rc["kind"], rc["idx"]
            sl = t % ND
            if rc["first"]:
                P.I("sync", "dma_start", reads=[io["b_Vs"][32]], writes=[b_vnew[s % 2]], dma=True, key="at_vnew%d" % (s % 2),
                    out=vnew[s % 2][:], in_=Vs[SEQ + s * TS:SEQ + (s + 1) * TS, :])
            if kind == "N":
                return
            if kind == "A":
                nk = 128
                ksrc = ck[s, 1536 + 128 * idx:1536 + 128 * (idx + 1), :]
                vsrc = cv[s, 1536 + 128 * idx:1536 + 128 * (idx + 1), :]
            else:
                nk = 96
                ksrc = bass.AP(ck.tensor, ck[s, idx, :].offset, [[16 * WA, 96], [1, WA]])
                vsrc = bass.AP(cv.tensor, cv[s, idx, :].offset, [[16 * WA, 96], [1, WA]])
            P.I("sync", "dma_start", writes=[b_ktf[sl]], dma=True, key="at_ktf%d" % sl, out=ktf[sl][0:nk, :], in_=ksrc)
            P.I("sync", "dma_start", writes=[b_vtf[sl]], dma=True, key="at_vtf%d" % sl, out=vtf[sl][0:nk, :], in_=vsrc)
            P.I("gpsimd", "tensor_copy", reads=[b_ktf[sl]], writes=[b_ktb[sl]], out=ktb[sl][0:nk, :], in_=ktf[sl][0:nk, :])
            P.I("gpsimd", "tensor_copy", reads=[b_vtf[sl]], writes=[b_vau[sl]], out=vau[sl][0:nk, :, 0:64],
                in_=vtf[sl][0:nk, :].rearrange("p (h v) -> p h v", h=NH))
            pb_ = 2 + (t % 2)
            pst = PS[:, pb_, :].bitcast(BF16)
            for c in range(6):
                P.I("tensor", "transpose", reads=[b_ktb[sl], bC], writes=[bk[pb_]], out=pst[:, c * 128:c * 128 + nk],
                    in_=ktb[sl][0:nk, c * 128:(c + 1) * 128], identity=C["identb"][0:nk, 0:nk])
            P.I("vector", "tensor_copy", reads=[bk[pb_]], writes=[b_ktT[sl]], out=ktT[sl][:, :, 0:nk],
                in_=pst[:, 0:768].rearrange("p (c n) -> p c n", c=6)[:, :, 0:nk])

        def tile_params(t):
            rc = recs[t]
            kind, idx = rc["kind"], rc["idx"]
            if kind == "A":
                return 128, EBs[idx], b_EBs[idx]
            if kind == "B":
                return 96, EBs[5 + idx], b_EBs[5 + idx]
            return TS, EBs[4], b_EBs[4]

        def stB(t):
            rc = recs[t]
            s, last = rc["s"], rc["last"]
            sl = t % ND
            s2 = t % 2
            nk, eb, b_eb = tile_params(t)
            qcols = slice(SEQ + s * TS, SEQ + (s + 1) * TS)
            for hp in range(6):
                for hh in range(2):
                    pb = hh * 64
                    lhs = ktT[sl][pb:pb + 64, hp, 0:nk] if not last else KT[pb:pb + 64, hp, qcols]
                    P.I("tensor", "matmul", reads=[b_ktT[sl], b_QT, b_KT], writes=[bk[hh]], out=PS[0:nk, hh, hp * TS:(hp + 1) * TS],
                        lhsT=lhs, rhs=QT[pb:pb + 64, hp, qcols], start=True, stop=True)
            P.I("scalar", "activation", reads=[bk[0], bk[1]], writes=[b_pes[s2]], out=pes[s2][0:nk], in_=PS[0:nk, 0:2, 0:48],
                func=AF.Exp)
            P.I("vector", "tensor_tensor", reads=[b_pes[s2], b_eb], writes=[b_pTs[s2]],
                out=pTs[s2][0:nk].rearrange("p a (b t) -> p a b t", t=TS),
                in0=pes[s2][0:nk].rearrange("p a (b t) -> p a b t", t=TS),
                in1=eb[0:nk].rearrange("p (b a) t -> p a b t", a=2), op=ALU.mult)

        def stC(t):
            rc = recs[t]
            s, last = rc["s"], rc["last"]
            sl = t % ND
            s2 = t % 2
            nk, eb, b_eb = tile_params(t)
            ob = 4 + 2 * (s % 2)
            fib = fibs[s]
            for h in range(NH):
                hp, hh = h // 2, h % 2
                bo, co = ocol(h)
                rhs = vau[sl][0:nk, h, :] if not last else vnew[s % 2][:, h * 65:(h + 1) * 65]
                P.I("tensor", "matmul", reads=[b_pTs[s2], b_vau[sl], b_vnew[s % 2]], writes=[bk[ob + bo]],
                    out=PS[0:TS, ob + bo, co:co + 65], lhsT=pTs[s2][0:nk, hh, hp * TS:(hp + 1) * TS], rhs=rhs,
                    start=fib[bo], stop=last, skip_group_check=True)
                fib[bo] = False
            if last:
                o2 = s % 2
                P.I("vector", "tensor_copy", reads=[bk[ob]], writes=[b_osm[o2]], out=osm[o2][:, 0:455], in_=PS[0:TS, ob, 0:455])
                P.I("scalar", "activation", reads=[bk[ob + 1]], writes=[b_osm[o2]], out=osm[o2][:, 455:780],
                    in_=PS[0:TS, ob + 1, 0:325], func=AF.Copy)
                P.I("sync", "dma_start", reads=[b_osm[o2]], writes=[io["b_Obr"][3]], dma=True, key="at_osm%d" % o2,
                    out=io["Obr"][0][SEQ + s * TS:SEQ + (s + 1) * TS, :], in_=osm[o2][:])

        NT = len(recs)
        stA(0)
        stA(1)
        stB(0)
        for t in range(NT):
            if t + 2 < NT:
                stA(t + 2)
            if t + 1 < NT:
                stB(t + 1)
            stC(t)
    A.release(m0)


def merge_phase(P, A, PS, C, io, w_out):
    m0 = A.mark()
    bC = C["b_const"]
    bk = [Buf("mgps%d" % b, excl=True) for b in range(8)]

    def sb(name, shape, dt, n=1):
        ts = [A.alloc("mg" + name, shape, dt) for _ in range(n)]
        bs = [Buf("mg%s%d" % (name, i)) for i in range(n)]
        return (ts, bs) if n > 1 else (ts[0], bs[0])

    Wo, b_Wo = sb("wo", [128, 8, D], BF16)
    for c in range(8):
        P.I("gpsimd", "dma_start", writes=[b_Wo], dma=True, key="mg_wo", out=Wo[:, c, :], in_=w_out[c * 128:(c + 1) * 128, :])
    o3, b_o3 = sb("o3", [128, 3, 780], F32, 2)
    rden, b_rden = sb("rden", [128, NH], F32)
    cat, b_cat = sb("cat", [128, D], BF16, 2)
    catT, b_catT = sb("catT", [128, 8, 128], BF16, 2)
    x1, b_x1 = sb("x1", [128, D], F32, 2)
    for g in range(33):
        s2 = g % 2
        nb = 3 if g < 32 else 1
        for bi in range(nb):
            P.I("sync", "dma_start", reads=[io["b_Obr"][bi if g < 32 else 3]], writes=[b_o3[s2]], dma=True, key="mg_o3_%d" % s2,
                out=o3[s2][:, bi, :], in_=io["Obr"][bi][g * 128:(g + 1) * 128, :])
        P.I("sync", "dma_start", reads=[io["b_OB"][g]], writes=[b_cat[s2]], dma=True, key="mg_cat%d" % s2,
            out=cat[s2][:, WA:D], in_=io["OB"][g * 128:(g + 1) * 128, :])
        P.I("sync", "dma_start", reads=[io["b_X1"][g]], writes=[b_x1[s2]], dma=True, key="mg_x1_%d" % s2,
            out=x1[s2][:], in_=io["X1"][g * 128:(g + 1) * 128, :])
        for bi in range(1, nb):
            P.I("vector", "tensor_tensor", reads=[b_o3[s2]], writes=[b_o3[s2]], out=o3[s2][:, 0, :], in0=o3[s2][:, 0, :],
                in1=o3[s2][:, bi, :], op=ALU.add)
        ov = o3[s2][:, 0, :].rearrange("p (h v) -> p h v", v=65)
        P.I("vector", "reciprocal", reads=[b_o3[s2]], writes=[b_rden], out=rden[:].unsqueeze(2), in_=ov[:, :, 64:65])
        P.I("vector", "tensor_tensor", reads=[b_o3[s2], b_rden], writes=[b_cat[s2]],
            out=cat[s2][:, 0:WA].rearrange("p (h v) -> p h v", v=64), in0=ov[:, :, 0:64],
            in1=rden[:].unsqueeze(2).to_broadcast([128, NH, 64]), op=ALU.mult)
        pst = PS[:, s2, :].bitcast(BF16)
        for c in range(8):
            P.I("tensor", "transpose", reads=[b_cat[s2], bC], writes=[bk[s2]], out=pst[:, c * 128:(c + 1) * 128],
                in_=cat[s2][:, c * 128:(c + 1) * 128], identity=C["identb"][:])
        P.I("scalar", "activation", reads=[bk[s2]], writes=[b_catT[s2]], out=catT[s2][:].rearrange("p c n -> p (c n)"), in_=pst,
            func=AF.Copy)
        for dh in range(2):
            bnk = 2 + 2 * s2 + dh
            for c in range(8):
                P.I("tensor", "matmul", reads=[b_catT[s2], b_Wo], writes=[bk[bnk]], out=PS[:, bnk, :], lhsT=catT[s2][:, c, :],
                    rhs=Wo[:, c, dh * 512:(dh + 1) * 512], start=(c == 0), stop=(c == 7))
            P.I("vector", "tensor_tensor", reads=[bk[bnk], b_x1[s2]], writes=[b_x1[s2]], out=x1[s2][:, dh * 512:(dh + 1) * 512],
                in0=PS[:, bnk, :], in1=x1[s2][:, dh * 512:(dh + 1) * 512], op=ALU.add)
        P.I("sync", "dma_start", reads=[b_x1[s2]], writes=[io["b_X2"][g]], dma=True, key="mg_x2_%d" % s2,
            out=io["X2"][g * 128:(g + 1) * 128, :], in_=x1[s2][:])
    A.release(m0)


def ffn_precise_tile(P, A, PS, C, w1, w3, w2, g_sb, src, dst, b_src, b_dst):
    m0 = A.mark()
    bC = C["b_const"]
    bk = [Buf("fpps%d" % b, excl=True) for b in range(8)]

    def sb(name, shape, dt, n=1):
        ts = [A.alloc("fp" + name, shape, dt) for _ in range(n)]
        bs = [Buf("fp%s%d" % (name, i)) for i in range(n)]
        return (ts, bs) if n > 1 else (ts[0], bs[0])

    x, b_x = sb("x", [128, D], F32)
    sq, b_sq = sb("sq", [128, D], BF16)
    ss, b_ss = sb("ss", [128, 1], F32)
    rs, b_rs = sb("rs", [128, 1], F32)
    hf, b_hf = sb("hf", [128, D], F32)
    hh, b_hh = sb("hh", [128, D], BF16)
    hl, b_hl = sb("hl", [128, D], BF16)
    hTh, b_hTh = sb("hTh", [128, 8, 128], BF16)
    hTl, b_hTl = sb("hTl", [128, 8, 128], BF16)
    wf = {n: sb("wf" + n, [128, 8, 128], F32, 2) for n in ("w1", "w3")}
    wh = {n: sb("wh" + n, [128, 8, 128], BF16, 2) for n in ("w1", "w3")}
    wl = {n: sb("wl" + n, [128, 8, 128], BF16, 2) for n in ("w1", "w3")}
    w2f, b_w2f = sb("w2f", [128, D], F32, 2)
    w2h, b_w2h = sb("w2h", [128, D], BF16, 2)
    w2l, b_w2l = sb("w2l", [128, D], BF16, 2)
    sa, b_sa = sb("sa", [128, 128], F32)
    uf, b_uf = sb("uf", [128, 128], F32)
    uh, b_uh = sb("uh", [128, 128], BF16, 2)
    ul, b_ul = sb("ul", [128, 128], BF16, 2)

    P.I("sync", "dma_start", reads=b_src, writes=[b_x], dma=True, key="fp_x", out=x[:], in_=src)
    P.I("scalar", "activation", reads=[b_x], writes=[b_sq, b_ss], out=sq[:], in_=x[:], func=AF.Square, accum_out=ss[:])
    P.I("scalar", "activation", reads=[b_ss, bC], writes=[b_rs], out=rs[:], in_=ss[:], func=AF.Ln, scale=1.0 / D, bias=C["eps"][:])
    P.I("scalar", "activation", reads=[b_rs], writes=[b_rs], out=rs[:], in_=rs[:], func=AF.Exp, scale=-0.5)
    P.I("scalar", "activation", reads=[b_x, b_rs], writes=[b_hf], out=hf[:], in_=x[:], func=AF.Copy, scale=rs[:, 0:1])
    P.I("vector", "tensor_copy", reads=[b_hf], writes=[b_hh], out=hh[:], in_=hf[:])
    P.I("vector", "tensor_tensor", reads=[b_hf, b_hh], writes=[b_hl], out=hl[:], in0=hf[:], in1=hh[:], op=ALU.subtract)
    for srcT, b_s, dstT, b_d, bank in ((hh, b_hh, hTh, b_hTh, 6), (hl, b_hl, hTl, b_hTl, 7)):
        pst = PS[:, bank, :].bitcast(BF16)
        for kc in range(8):
            P.I("tensor", "transpose", reads=[b_s, bC], writes=[bk[bank]], out=pst[:, kc * 128:(kc + 1) * 128],
                in_=srcT[:, kc * 128:(kc + 1) * 128], identity=C["identb"][:])
        P.I("vector", "tensor_copy", reads=[bk[bank]], writes=[b_d], out=dstT[:].rearrange("p c n -> p (c n)"), in_=pst)

    ny = 0
    for c in range(NFC):
        s = c % 2
        for n, wsrc in (("w1", w1), ("w3", w3)):
            wft, b_wf = wf[n][0][s], wf[n][1][s]
            wht, b_wh = wh[n][0][s], wh[n][1][s]
            wlt, b_wl = wl[n][0][s], wl[n][1][s]
            P.I("sync", "dma_start", writes=[b_wf], dma=True, key="fp_%s_%d" % (n, s), out=wft[:],
                in_=wsrc[:, c * 128:(c + 1) * 128].rearrange("(kc p) n -> p kc n", p=128))
            P.I("vector", "tensor_tensor", reads=[b_wf, bC], writes=[b_wf], out=wft[:], in0=wft[:],
                in1=g_sb.unsqueeze(2).to_broadcast([128, 8, 128]), op=ALU.mult)
            P.I("scalar", "activation", reads=[b_wf], writes=[b_wh], out=wht[:], in_=wft[:], func=AF.Copy)
            P.I("vector", "tensor_tensor", reads=[b_wf, b_wh], writes=[b_wl], out=wlt[:], in0=wft[:], in1=wht[:], op=ALU.subtract)
        P.I("sync", "dma_start", writes=[b_w2f[s]], dma=True, key="fp_w2_%d" % s, out=w2f[s][:], in_=w2[c * 128:(c + 1) * 128, :])
        P.I("scalar", "activation", reads=[b_w2f[s]], writes=[b_w2h[s]], out=w2h[s][:], in_=w2f[s][:], func=AF.Copy)
        P.I("vector", "tensor_tensor", reads=[b_w2f[s], b_w2h[s]], writes=[b_w2l[s]], out=w2l[s][:], in0=w2f[s][:], in1=w2h[s][:],
            op=ALU.subtract)
        for j, n in enumerate(("w1", "w3")):
            combos = ((wh[n][0][s], wh[n][1][s], hTh, b_hTh), (wl[n][0][s], wl[n][1][s], hTh, b_hTh),
                      (wh[n][0][s], wh[n][1][s], hTl, b_hTl))
            k = 0
            for wt, bw, ht, bh in combos:
                for kc in range(8):
                    P.I("tensor", "matmul", reads=[bw, bh], writes=[bk[j]], out=PS[:, j, 0:128], lhsT=wt[:, kc, :], rhs=ht[:, kc, :],
                        start=(k == 0), stop=(k == 23))
                    k += 1
        P.I("scalar", "activation", reads=[bk[0]], writes=[b_sa], out=sa[:], in_=PS[:, 0, 0:128], func=AF.Silu)
        P.I("vector", "tensor_tensor", reads=[b_sa, bk[1]], writes=[b_uf], out=uf[:], in0=sa[:], in1=PS[:, 1, 0:128], op=ALU.mult)
        P.I("vector", "tensor_copy", reads=[b_uf], writes=[b_uh[s]], out=uh[s][:], in_=uf[:])
        P.I("vector", "tensor_tensor", reads=[b_uf, b_uh[s]], writes=[b_ul[s]], out=ul[s][:], in0=uf[:], in1=uh[s][:], op=ALU.subtract)
        for u_, b_u, w_, b_w in ((uh[s], b_uh[s], w2h[s], b_w2h[s]), (ul[s], b_ul[s], w2h[s], b_w2h[s]),
                                 (uh[s], b_uh[s], w2l[s], b_w2l[s])):
            for dh in range(2):
                P.I("tensor", "matmul", reads=[b_u, b_w], writes=[bk[2 + dh]], out=PS[:, 2 + dh, :], lhsT=u_[:],
                    rhs=w_[:, dh * 512:(dh + 1) * 512], start=(ny < 2), stop=(ny >= 6 * NFC - 2))
                ny += 1
    for dh in range(2):
        P.I("vector", "scalar_tensor_tensor", reads=[bk[2 + dh], b_x], writes=[b_x], out=x[:, dh * 512:(dh + 1) * 512],
            in0=PS[:, 2 + dh, :], scalar=0.5, in1=x[:, dh * 512:(dh + 1) * 512], op0=ALU.mult, op1=ALU.add)
    P.I("sync", "dma_start", reads=[b_x], writes=b_dst, dma=True, key="fp_out", out=dst, in_=x[:])
    A.release(m0)


T_ALL = SEQ + NSEQ_S * TS


def build_program(opts=None):
    opts = opts or {}
    nc = bass.Bass("TRN2", target_bir_lowering=False)

    def dt(n, s, d, k):
        return nc.dram_tensor(n, list(s), d, kind=k).ap()
    I, O, S = "ExternalInput", "ExternalOutput", "Internal"
    xp = dt("xp", [SEQ, D], F32, I)
    xs = dt("xs", [NSEQ_S * TS, D], F32, I)
    ck = dt("ck", [NSEQ_S, WBUF, WA], F32, I)
    cv = dt("cv", [NSEQ_S, WBUF, WA], F32, I)
    sg = dt("sg", [NSEQ_S, 4, 32, 64], F32, I)
    wts = {n: dt(n, s, F32, I) for n, s in (("f1w1", [D, DFF]), ("f1w3", [D, DFF]), ("f1w2", [DFF, D]),
                                              ("f2w1", [D, DFF]), ("f2w3", [D, DFF]), ("f2w2", [DFF, D]),
                                              ("w_in", [D, PROJ]), ("w_out", [D, D]))}
    c128 = dt("c128", [128, _offsets(C128)[1]], F32, I)
    p128 = dt("p128", [128, _offsets(P128)[1]], F32, I)
    wgk2 = dt("wgk2", [16, 128], F32, I)
    rel_bias = dt("rel_bias", [32, NH], F32, I)
    ohp = dt("ohp", [3, 32, 384], F32, I)
    ohs = dt("ohs", [5, 32, 136], F32, I)
    ohg = dt("ohg", [32, 128], F32, I)
    yp = dt("yp", [SEQ, D], F32, O)
    ys = dt("ys", [NSEQ_S * TS, D], F32, O)
    io = dict(sg=sg, ck=ck, cv=cv, rel_bias=rel_bias, ohp=ohp, ohs=ohs, ohg=ohg, w_in_d=wts["w_in"])
    io["wkp"] = dt("wkp", [WBUF, WA], F32, O)
    io["wvp"] = dt("wvp", [WBUF, WA], F32, O)
    io["glp"] = dt("glp", [128, 64], F32, O)
    io["wks"] = dt("wks", [NSEQ_S, WBUF, WA], F32, O)
    io["wvs"] = dt("wvs", [NSEQ_S, WBUF, WA], F32, O)
    io["gls"] = dt("gls", [NSEQ_S, 4, 32, 64], F32, O)
    probe = bool(opts.get("probe"))
    dump = bool(opts.get("dump"))
    io["X1"] = dt("X1", [T_ALL, D], F32, I if probe else (O if dump else S))
    io["X2"] = dt("X2", [T_ALL, D], F32, O if (probe or dump) else S)
    io["Vs"] = dt("Vs", [T_ALL, 780], BF16, S)
    io["OB"] = dt("OB", [T_ALL, 256], BF16, S)
    obr = dt("Obr", [3, T_ALL, 780], F32, S)
    io["Obr"] = [obr[i] for i in range(3)]
    fd = dt("FD", [8, 128, NH * 384], F32, S)
    io["FD"] = [fd[i] for i in range(8)]
    for nm in ("X1", "X2", "Vs", "OB"):
        io["b_" + nm] = [Buf("%s_%d" % (nm, g)) for g in range(33)]
    io["b_Obr"] = [Buf("Obr%d" % i) for i in range(4)]
    io["b_wks_new"], io["b_wvs_new"] = Buf("wksn"), Buf("wvsn")
    io.update(opts)

    P = Prog(nc)
    A = Arena(nc)
    poff = _offsets(P128)[0]
    gslice = lambda n: p128[:, poff[n][0]:poff[n][0] + poff[n][1]]
    with nc.psum_tensor("ps", [128, 8, 512], F32) as PS:
        C = setup_consts(P, A, c128, p128, wgk2)
        base = A.mark()

        def tiles(src_p, src_s, dst_p, dst_s, b_src, b_dst):
            out = []
            for i in range(16):
                out.append([(src_p[g * 128:(g + 1) * 128, :], dst_p[g * 128:(g + 1) * 128, :],
                             [b_src[g]] if b_src else [], [b_dst[g]] if b_dst else [Buf("o")]) for g in (2 * i, 2 * i + 1)])
            out.append([(src_s, dst_s, [b_src[32]] if b_src else [], [b_dst[32]] if b_dst else [Buf("o")])])
            return out

        if not probe:
          W1 = load_ffn_weights(P, A, wts["f1w1"], wts["f1w3"], wts["f1w2"], gslice("f1g"), "f1")
          wdeps = [W1["b_w2"][1].writer]
          for s in range(NSEQ_S):
            for src, dst in ((ck, io["wks"]), (cv, io["wvs"])):
                o = P.I("scalar", "dma_start", dma=True, key="shift", extra=wdeps,
                        out=dst[s, 0:WBUF - TS, :].rearrange("(a b) f -> a (b f)", a=120),
                        in_=src[s, TS:WBUF, :].rearrange("(a b) f -> a (b f)", a=120))
          P.must_finish(o)
          ffn_phase(P, A, PS, C, W1, tiles(xp, xs, io["X1"], io["X1"][SEQ:T_ALL, :], None, io["b_X1"]), "f1")
          P.barrier()
          A.release(base)
          ffn_precise_tile(P, A, PS, C, wts["f1w1"], wts["f1w3"], wts["f1w2"], C["f1g"], xp[0:128, :], io["X1"][0:128, :],
                           [], [io["b_X1"][0]])
          P.barrier()
          A.release(base)
        mg = gslice("mixg")
        Wg = load_win(P, A, wts["w_in"], mg, "gla")
        proj_phase(P, A, PS, C, Wg, None, None, None, None, io, "gla")
        P.barrier()
        A.release(base)
        QT = A.alloc("QT", [128, 6, T_ALL], BF16)
        KT = A.alloc("KT", [128, 6, T_ALL], BF16)
        b_QT, b_KT = Buf("QT"), Buf("KT")
        mq = A.mark()
        Wq = load_win(P, A, wts["w_in"], mg, "qkv")
        proj_phase(P, A, PS, C, Wq, QT, KT, b_QT, b_KT, io, "qkv")
        P.barrier()
        A.release(mq)
        attn_phase(P, A, PS, C, QT, KT, b_QT, b_KT, io)
        P.barrier()
        A.release(base)
        if not probe:
            W2 = load_ffn_weights(P, A, wts["f2w1"], wts["f2w3"], wts["f2w2"], gslice("f2g"), "f2")
        merge_phase(P, A, PS, C, io, wts["w_out"])
        if probe:
            for o in P.live_dma:
                P.must_finish(o)
        P.barrier()
        if not probe:
            ffn_phase(P, A, PS, C, W2, tiles(io["X2"], io["X2"][SEQ:T_ALL, :], yp, ys, io["b_X2"], None), "f2")
        P.emit()
    info = dict(n_sems=P.n_sems, counts=P.counts, peak=A.peak, n_ins={e: len(P.q[e]) for e in ENGS})
    return nc, info


_CACHE = {}


def kernel(**inp):
    f32 = lambda a: np.ascontiguousarray(np.asarray(a, dtype=np.float32))
    if "nc" not in _CACHE:
        _CACHE["nc"], _CACHE["info"] = build_program()
    nc = _CACHE["nc"]
    ohp, ohs, ohg = make_onehots()
    shared = {
        "f1w1": f32(inp["ffn1_w1"][0]), "f1w3": f32(inp["ffn1_w3"][0]), "f1w2": f32(inp["ffn1_w2"][0]),
        "f2w1": f32(inp["ffn2_w1"][0]), "f2w3": f32(inp["ffn2_w3"][0]), "f2w2": f32(inp["ffn2_w2"][0]),
        "w_in": f32(inp["w_in"][0]), "w_out": f32(inp["w_out"][0]),
        "c128": make_consts(), "p128": make_params({k: np.asarray(v) for k, v in inp.items() if k in (
            "q_norm", "k_norm", "b_gk", "gla_norm", "ffn1_norm", "mix_norm", "ffn2_norm")}),
        "wgk2": f32(inp["w_gk2"][0]), "rel_bias": f32(inp["rel_bias"]), "ohp": ohp, "ohs": ohs, "ohg": ohg,
    }
    xp, xs = np.asarray(inp["x_prompt"]), np.asarray(inp["x_sample"])
    ck, cv, sg = np.asarray(inp["cache_win_k"]), np.asarray(inp["cache_win_v"]), np.asarray(inp["state_gla"])
    in_maps = []
    for c in range(NCORES):
        m = dict(shared)
        sl = slice(c * NSEQ_S, (c + 1) * NSEQ_S)
        m["xp"] = f32(xp[c])
        m["xs"] = f32(xs[sl].reshape(NSEQ_S * TS, D))
        m["ck"] = f32(ck[0, sl].reshape(NSEQ_S, WBUF, WA))
        m["cv"] = f32(cv[0, sl].reshape(NSEQ_S, WBUF, WA))
        m["sg"] = f32(sg[0, sl])
        in_maps.append(m)
    res = run_bass_kernel_spmd(nc, in_maps, core_ids=list(range(NCORES)))
    R = res.results
    cat = lambda n: np.concatenate([np.asarray(r[n]) for r in R], axis=0)
    y_prompt = np.stack([np.asarray(r["yp"]) for r in R]).astype(np.float32)
    y_sample = cat("ys").reshape(NCORES * NSEQ_S, TS, D).astype(np.float32)
    wkp = np.stack([np.asarray(r["wkp"]) for r in R]).reshape(1, NCORES, WBUF, NH, HD).astype(np.float32)
    wvp = np.stack([np.asarray(r["wvp"]) for r in R]).reshape(1, NCORES, WBUF, NH, HD).astype(np.float32)
    glp = np.stack([np.asarray(r["glp"]) for r in R]).reshape(1, NCORES, 4, 32, 64).astype(np.float32)
    wks = cat("wks").reshape(1, NCORES * NSEQ_S, WBUF, NH, HD).astype(np.float32)
    wvs = cat("wvs").reshape(1, NCORES * NSEQ_S, WBUF, NH, HD).astype(np.float32)
    gls = cat("gls").reshape(1, NCORES * NSEQ_S, 4, 32, 64).astype(np.float32)
    return (y_prompt, y_sample, wkp, wvp, glp, wks, wvs, gls)
```

```python
import contextlib
import math
import numpy as np
import concourse.bass as bass
import concourse.mybir as mybir
from concourse.bass_utils import run_bass_kernel_spmd

F32 = mybir.dt.float32
BF16 = mybir.dt.bfloat16
U8 = mybir.dt.uint8
AF = mybir.ActivationFunctionType
ALU = mybir.AluOpType
AX = mybir.AxisListType

ENGS = ("tensor", "scalar", "vector", "gpsimd", "sync")
NCORES = 8
D = 1024
DFF = 2816
NFC = DFF // 128
SEQ = 4096
NTP = SEQ // 128
NSEQ_S = 16
TS = 8
WBUF = 2048
NH = 12
HD = 64
WA = NH * HD
PROJ = 3088
EPS = 1e-6


class Buf:
    __slots__ = ("name", "writer", "readers", "excl")

    def __init__(self, name, excl=False):
        self.name = name
        self.writer = None
        self.readers = []
        self.excl = excl


class Op:
    __slots__ = ("eng", "fn", "deps", "is_dma", "key", "sig", "marked", "pos")

    def __init__(self, eng, fn, is_dma, key):
        self.eng = eng
        self.fn = fn
        self.deps = []
        self.is_dma = is_dma
        self.key = key
        self.sig = None
        self.marked = False
        self.pos = 0


def _reduce_ops(ops):
    best = {}
    for d in ops:
        k = ("d", d.key) if d.is_dma else ("e", d.eng)
        if k not in best or best[k].pos < d.pos:
            best[k] = d
    return list(best.values())


class Prog:
    def __init__(self, nc):
        self.nc = nc
        self.q = {e: [] for e in ENGS}
        self.final_waits = []
        self.pending = {e: [] for e in ENGS}
        self.live_dma = []
        self.npos = 0

    def op(self, eng, fn, reads=(), writes=(), dma=False, key=None, extra=()):
        o = Op(eng, fn, dma, key)
        self.npos += 1
        o.pos = self.npos
        deps = []
        for b in reads:
            if b.writer is not None:
                deps.append(b.writer)
            if b.excl:
                deps.extend(r for r in b.readers if r.eng != eng)
        for b in writes:
            if b.writer is not None:
                deps.append(b.writer)
            deps.extend(b.readers)
        deps.extend(extra)
        if self.pending[eng]:
            deps.extend(self.pending[eng])
            self.pending[eng] = []
        for d in _reduce_ops(deps):
            if d is o:
                continue
            if (not dma) and eng == "tensor" and d.eng == "tensor" and not d.is_dma:
                continue
            if dma and d.is_dma and d.key == key:
                continue
            o.deps.append(d)
            d.marked = True
        for b in reads:
            b.readers = _reduce_ops(b.readers + [o])
        for b in writes:
            b.writer = o
            b.readers = []
        if dma:
            assert key is not None
            self.live_dma.append(o)
        self.q[eng].append(o)
        return o

    def I(self, eng, name, reads=(), writes=(), dma=False, key=None, extra=(), **kw):
        return self.op(eng, lambda e: getattr(e, name)(**kw), reads, writes, dma, key, extra)

    def barrier(self):
        lasts = []
        for e in ENGS:
            for o in reversed(self.q[e]):
                if not o.is_dma:
                    lasts.append(o)
                    break
        lasts.extend(self.live_dma)
        self.live_dma = []
        for e in ENGS:
            self.pending[e] = list(lasts)

    def must_finish(self, o):
        o.marked = True
        self.final_waits.append(o)

    def emit(self):
        nc = self.nc
        with contextlib.ExitStack() as st:
            esem = {e: st.enter_context(nc.semaphore("s_" + e)) for e in ENGS if e != "sync"}
            ecnt = {e: 0 for e in esem}
            dsem, dcnt = {}, {}
            for e in ENGS:
                for o in self.q[e]:
                    if o.is_dma:
                        if o.key not in dsem:
                            dsem[o.key] = st.enter_context(nc.semaphore("d_%d" % len(dsem)))
                            dcnt[o.key] = 0
                        dcnt[o.key] += 16
                        o.sig = (dsem[o.key], dcnt[o.key])
                    elif o.marked:
                        ecnt[e] += 1
                        o.sig = (esem[e], ecnt[e])
            self.n_sems = len(dsem) + len(esem)
            self.counts = dict(ecnt)
            block = st.enter_context(nc.Block())

            def run(e, eng):
                waited = {}
                for o in self.q[e]:
                    need = {}
                    for d in o.deps:
                        sem, val = d.sig
                        k = id(sem)
                        if need.get(k, (None, 0))[1] < val:
                            need[k] = (sem, val)
                    for k, (sem, val) in need.items():
                        if waited.get(k, 0) < val:
                            eng.wait_ge(sem, val)
                            waited[k] = val
                    ins = o.fn(eng)
                    if o.sig is not None:
                        ins.then_inc(o.sig[0], 16 if o.is_dma else 1)
                if e == "sync":
                    need = {}
                    for o in self.final_waits:
                        sem, val = o.sig
                        k = id(sem)
                        if need.get(k, (None, 0))[1] < val:
                            need[k] = (sem, val)
                    for k, (sem, val) in need.items():
                        if waited.get(k, 0) < val:
                            eng.wait_ge(sem, val)
                            waited[k] = val

            @block.tensor
            def _(eng):
                run("tensor", eng)

            @block.scalar
            def _(eng):
                run("scalar", eng)

            @block.vector
            def _(eng):
                run("vector", eng)

            @block.gpsimd
            def _(eng):
                run("gpsimd", eng)

            @block.sync
            def _(eng):
                run("sync", eng)


class Arena:
    def __init__(self, nc):
        self.nc = nc
        self.base = (nc.sbuf_base + 63) // 64 * 64
        self.limit = nc.sbuf_top
        self.top = self.base
        self.n = 0
        self.peak = self.base

    def alloc(self, name, shape, dtype):
        isz = {F32: 4, BF16: 2, U8: 1}[dtype]
        nbytes = int(np.prod(shape[1:])) * isz
        nbytes = (nbytes + 63) // 64 * 64
        off = self.top
        assert off + nbytes <= self.limit, ("SBUF overflow", name, off, nbytes, self.limit)
        self.top += nbytes
        self.peak = max(self.peak, self.top)
        self.n += 1
        return self.nc.alloc_sbuf_tensor_at("%s_%d" % (name, self.n), list(shape), dtype, offset=off)

    def mark(self):
        return self.top

    def release(self, m):
        self.top = m


def load_ffn_weights(P, A, w1, w3, w2, gain, tag):
    W = {}
    W["w1"] = A.alloc(tag + "w1", [128, 8, DFF], BF16)
    W["w3"] = A.alloc(tag + "w3", [128, 8, DFF], BF16)
    W["w2"] = A.alloc(tag + "w2", [128, NFC, D], BF16)
    W["g"] = A.alloc(tag + "g", [128, 8], F32)
    W["b_w1"] = [Buf(tag + "w1_%d" % i) for i in range(2)]
    W["b_w3"] = [Buf(tag + "w3_%d" % i) for i in range(2)]
    W["b_w2"] = [Buf(tag + "w2_%d" % i) for i in range(2)]
    W["b_g"] = Buf(tag + "g")
    P.I("sync", "dma_start", writes=[W["b_g"]], dma=True, key=tag + "g", out=W["g"][:], in_=gain)
    half = DFF // 2
    for hf in range(2):
        for nm, src in (("w1", w1), ("w3", w3)):
            for kc in range(8):
                P.I("gpsimd", "dma_start", writes=[W["b_" + nm][hf]], dma=True, key=tag + nm + str(hf), out=W[nm][:, kc, hf * half:(hf + 1) * half],
                         in_=src[kc * 128:(kc + 1) * 128, hf * half:(hf + 1) * half])
        for c in range(hf * 11, hf * 11 + 11):
            P.I("gpsimd", "dma_start", writes=[W["b_w2"][hf]], dma=True, key=tag + "w2" + str(hf), out=W["w2"][:, c, :], in_=w2[c * 128:(c + 1) * 128, :])
    return W


def ffn_phase(P, A, PS, C, W, tiles, tag):
    m0 = A.mark()
    NB = 3
    xt = [A.alloc(tag + "xt", [128, 2, D], F32) for _ in range(NB)]
    b_xt = [[Buf(tag + "xt%d_%d" % (i, t)) for t in range(2)] for i in range(NB)]
    hb = A.alloc(tag + "hb", [128, 2, D], BF16)
    b_hb = [Buf(tag + "hb%d" % t) for t in range(2)]
    hT = [A.alloc(tag + "hT", [128, 8, 256], BF16) for _ in range(2)]
    b_hT = [Buf(tag + "hT%d" % i) for i in range(2)]
    sq = A.alloc(tag + "sq", [128, D], BF16)
    b_sq = Buf(tag + "sq")
    ss = A.alloc(tag + "ss", [128, 2], F32)
    rs = A.alloc(tag + "rs", [128, 2], F32)
    b_ss = [Buf(tag + "ss%d" % t) for t in range(2)]
    b_rs = [Buf(tag + "rs%d" % t) for t in range(2)]
    s1 = [A.alloc(tag + "s1", [128, 256], F32) for _ in range(2)]
    b_s1 = [Buf(tag + "s1_%d" % i) for i in range(2)]
    ub = [A.alloc(tag + "ub", [128, 256], BF16) for _ in range(3)]
    b_ub = [Buf(tag + "ub%d" % i) for i in range(3)]
    b_ps13 = [Buf(tag + "ps13_%d" % i, excl=True) for i in range(2)]
    b_py = [[Buf(tag + "py%d_%d" % (t, h), excl=True) for h in range(2)] for t in range(2)]
    b_pt = [Buf(tag + "pt%d" % t, excl=True) for t in range(2)]
    psT = [PS[:, 6 + t, :].bitcast(BF16) for t in range(2)]

    def load(i):
        st_ = tiles[i]
        for t, (src, dst, sb, db) in enumerate(st_):
            P.I("sync", "dma_start", reads=sb, writes=[b_xt[i % NB][t]], dma=True, key="xt%d_%d" % (i % NB, t), out=xt[i % NB][:, t, :], in_=src)

    def norm(i):
        st_ = tiles[i]
        x = xt[i % NB]
        for t in range(len(st_)):
            P.I("scalar", "activation", reads=[b_xt[i % NB][t]], writes=[b_sq, b_ss[t]], out=sq[:], in_=x[:, t, :], func=AF.Square,
                                                             accum_out=ss[:, t:t + 1])
            P.I("scalar", "activation", reads=[b_ss[t], C["b_const"]], writes=[b_rs[t]], out=rs[:, t:t + 1], in_=ss[:, t:t + 1], func=AF.Ln,
                                                        scale=1.0 / D, bias=C["eps"][:])
            P.I("scalar", "activation", reads=[b_rs[t]], writes=[b_rs[t]], out=rs[:, t:t + 1], in_=rs[:, t:t + 1], func=AF.Exp,
                                                        scale=-0.5)
            P.I("scalar", "activation", reads=[b_xt[i % NB][t], b_rs[t]], writes=[b_hb[t]], out=hb[:, t, :], in_=x[:, t, :], func=AF.Copy,
                                                             scale=rs[:, t:t + 1])
        for t in range(len(st_)):
            for kc in range(8):
                P.I("tensor", "transpose", reads=[b_hb[t], C["b_const"]], writes=[b_pt[t]], out=psT[t][:, kc * 128:(kc + 1) * 128],
                                                                   in_=hb[:, t, kc * 128:(kc + 1) * 128],
                                                                   identity=C["identb"][:])
            P.I("vector", "tensor_tensor", reads=[b_pt[t], W["b_g"]], writes=[b_hT[i % 2]], out=hT[i % 2][:, :, t * 128:(t + 1) * 128],
                in0=psT[t].rearrange("p (c n) -> p c n", c=8),
                in1=W["g"][:].unsqueeze(2).to_broadcast([128, 8, 128]), op=ALU.mult)

    def ffn(i):
        st_ = tiles[i]
        nt = len(st_)
        N = nt * 128
        x = xt[i % NB]
        h = hT[i % 2]
        pend = None
        for c in range(NFC + 1):
            if c < NFC:
                pb = c % 2
                hf = c // 11
                for j, nm in enumerate(("w1", "w3")):
                    for kc in range(8):
                        P.I("tensor", "matmul", reads=[W["b_" + nm][hf], b_hT[i % 2]], writes=[b_ps13[pb]], out=PS[:, pb, j * 256:j * 256 + N], lhsT=W[nm][:, kc, c * 128:(c + 1) * 128],
                            rhs=h[:, kc, 0:N], start=(kc == 0), stop=(kc == 7))
            if pend is not None:
                cc, u, bu = pend
                hf2 = cc // 11
                for t in range(nt):
                    for dh in range(2):
                        P.I("tensor", "matmul", reads=[W["b_w2"][hf2], bu], writes=[b_py[t][dh]], out=PS[:, 2 + t * 2 + dh, :], lhsT=u[:, t * 128:(t + 1) * 128],
                            rhs=W["w2"][:, cc, dh * 512:(dh + 1) * 512], start=(cc == 0), stop=(cc == NFC - 1))
                pend = None
            if c < NFC:
                sb_ = c % 2
                u = ub[c % 3]
                P.I("scalar", "activation", reads=[b_ps13[pb]], writes=[b_s1[sb_]], out=s1[sb_][:, 0:N], in_=PS[:, pb, 0:N],
                                                                         func=AF.Silu)
                P.I("vector", "tensor_tensor", reads=[b_s1[sb_], b_ps13[pb]], writes=[b_ub[c % 3]], out=u[:, 0:N], in0=s1[sb_][:, 0:N], in1=PS[:, pb, 256:256 + N], op=ALU.mult)
                pend = (c, u, b_ub[c % 3])
        for t, (src, dst, sb, db) in enumerate(st_):
            for dh in range(2):
                P.I("vector", "scalar_tensor_tensor", reads=[b_py[t][dh], b_xt[i % NB][t]], writes=[b_xt[i % NB][t]], out=x[:, t, dh * 512:(dh + 1) * 512], in0=PS[:, 2 + t * 2 + dh, :], scalar=0.5,
                    in1=x[:, t, dh * 512:(dh + 1) * 512], op0=ALU.mult, op1=ALU.add)
            o = P.I("sync", "dma_start", reads=[b_xt[i % NB][t]], writes=db, dma=True, key="xo%d_%d" % (i % NB, t), out=dst, in_=x[:, t, :])
            if tag == "f2":
                P.must_finish(o)

    n = len(tiles)
    load(0)
    if n > 1:
        load(1)
    norm(0)
    for i in range(n):
        if i + 2 < n:
            load(i + 2)
        if i + 1 < n:
            norm(i + 1)
        ffn(i)
    A.release(m0)


C128 = [("ident", 128), ("blockones", 128), ("caus", 128), ("caus_s", 128), ("bdmask", 256), ("headmask", 4),
        ("seqmask", 16), ("notstart", 128), ("ones128", 128)]
P128 = [("qg", 1), ("kg", 1), ("bgk", 1), ("gg", 256), ("f1g", 8), ("mixg", 8), ("f2g", 8)]


def _offsets(spec):
    off, o = {}, 0
    for n, w in spec:
        off[n] = (o, w)
        o += w
    return off, o


def make_consts():
    p = np.arange(128)
    c = {}
    c["ident"] = np.eye(128)
    c["blockones"] = (p[:, None] // 64 == p[None, :] // 64)
    c["caus"] = (p[None, :] >= p[:, None])
    c["caus_s"] = (p[None, :] >= p[:, None]) & (p[None, :] // TS == p[:, None] // TS)
    c["bdmask"] = (p[:, None] // 32 == np.arange(256)[None, :] // 64)
    c["headmask"] = (p[:, None] // 32 == np.arange(4)[None, :])
    c["seqmask"] = (p[:, None] // TS == np.arange(NSEQ_S)[None, :])
    c["notstart"] = np.tile((p % TS != 0)[None, :], (128, 1))
    c["ones128"] = np.ones((128, 128))
    off, tot = _offsets(C128)
    out = np.zeros((128, tot), np.float32)
    for n, (o, w) in off.items():
        out[:, o:o + w] = c[n].astype(np.float32)
    return out


def make_params(inp):
    off, tot = _offsets(P128)
    out = np.zeros((128, tot), np.float32)

    def put(n, a):
        o, w = off[n]
        out[:, o:o + w] = a
    put("qg", np.tile(inp["q_norm"][0], 2)[:, None])
    put("kg", np.tile(inp["k_norm"][0], 2)[:, None])
    put("bgk", inp["b_gk"][0][:, None])
    put("gg", np.tile(np.tile(inp["gla_norm"][0], 4)[None, :], (128, 1)))
    put("f1g", inp["ffn1_norm"][0].reshape(8, 128).T)
    put("mixg", inp["mix_norm"][0].reshape(8, 128).T)
    put("f2g", inp["ffn2_norm"][0].reshape(8, 128).T)
    return out


def setup_consts(P, A, c128, p128, wgk2_d):
    C = {}
    bC = C["b_const"] = Buf("const")
    off, tot = _offsets(C128)
    poff, ptot = _offsets(P128)
    cs = A.alloc("c128", [128, tot], F32)
    ps = A.alloc("p128", [128, ptot], F32)
    C["c128"], C["p128"] = cs, ps
    C["p128_d"], C["poff"] = p128, poff
    P.I("sync", "dma_start", writes=[bC], dma=True, key="c_c128", out=cs[:], in_=c128)
    P.I("sync", "dma_start", writes=[bC], dma=True, key="c_p128", out=ps[:], in_=p128)
    for n, (o, w) in off.items():
        C[n] = cs[:, o:o + w]
    for n, (o, w) in poff.items():
        C[n] = ps[:, o:o + w]
    C["identf"] = C["ident"]
    C["eps"] = A.alloc("eps", [128, 1], F32)
    C["one"] = A.alloc("one", [128, 1], F32)
    C["negb"] = A.alloc("negb", [128, 1], F32)
    identb = A.alloc("identb", [128, 128], BF16)
    blockb = A.alloc("blockb", [128, 128], BF16)
    wg_f = A.alloc("wgk2f", [16, 128], F32)
    wg_b = A.alloc("wgk2b", [16, 128], BF16)
    P.I("sync", "dma_start", writes=[bC], dma=True, key="c_wgk2", out=wg_f[:], in_=wgk2_d)
    P.I("vector", "memset", writes=[bC], ap=C["eps"][:], constant=EPS)
    P.I("vector", "memset", writes=[bC], ap=C["one"][:], constant=1.0)
    P.I("vector", "tensor_copy", reads=[bC], writes=[bC], out=identb[:], in_=C["ident"])
    P.I("vector", "tensor_copy", reads=[bC], writes=[bC], out=blockb[:], in_=C["blockones"])
    P.I("vector", "tensor_copy", reads=[bC], writes=[bC], out=wg_b[:], in_=wg_f[:])
    P.I("vector", "tensor_scalar", reads=[bC], writes=[bC], out=C["negb"][:], in0=C["bgk"], scalar1=-1.0,
        scalar2=None, op0=ALU.mult)
    C["identb"] = identb
    C["blockones"] = blockb
    C["wgk2"] = wg_b
    return C


class NormStage:
    def __init__(self, P, A, PS, C, tag, nxt=2, bank_t=(7, 6)):
        self.P, self.C, self.PS, self.tag = P, C, PS, tag
        self.nxt = nxt
        self.xt = [A.alloc(tag + "xt", [128, 2, D], F32) for _ in range(nxt)]
        self.b_xt = [[Buf(tag + "xt%d_%d" % (i, t)) for t in range(2)] for i in range(nxt)]
        self.hb = A.alloc(tag + "hb", [128, 2, D], BF16)
        self.b_hb = [Buf(tag + "hb%d" % t) for t in range(2)]
        self.hT = A.alloc(tag + "hT", [128, 8, 256], BF16)
        self.b_hT = Buf(tag + "hT")
        self.sq = A.alloc(tag + "sq", [128, D], BF16)
        self.b_sq = Buf(tag + "sq")
        self.ss = A.alloc(tag + "ss", [128, 2], F32)
        self.rs = A.alloc(tag + "rs", [128, 2], F32)
        self.b_ss = [Buf(tag + "ss%d" % t) for t in range(2)]
        self.b_rs = [Buf(tag + "rs%d" % t) for t in range(2)]
        self.bank_t = bank_t
        self.psT = [PS[:, bank_t[t], :].bitcast(BF16) for t in range(2)]

    def load(self, i, srcs):
        P = self.P
        for t, (src, sb) in enumerate(srcs):
            P.I("sync", "dma_start", reads=sb, writes=[self.b_xt[i % self.nxt][t]], dma=True, key="xt%d_%d" % (i % self.nxt, t), out=self.xt[i % self.nxt][:, t, :], in_=src)

    def norm(self, i, nt, gain, b_gain, b_banks):
        P, C = self.P, self.C
        x = self.xt[i % self.nxt]
        bx = self.b_xt[i % self.nxt]
        for t in range(nt):
            P.I("scalar", "activation", reads=[bx[t]], writes=[self.b_sq, self.b_ss[t]], out=self.sq[:], in_=x[:, t, :], func=AF.Square,
                                                        accum_out=self.ss[:, t:t + 1])
            P.I("scalar", "activation", reads=[self.b_ss[t], C["b_const"]], writes=[self.b_rs[t]], out=self.rs[:, t:t + 1], in_=self.ss[:, t:t + 1], func=AF.Ln,
                                                        scale=1.0 / D, bias=C["eps"][:])
            P.I("scalar", "activation", reads=[self.b_rs[t]], writes=[self.b_rs[t]], out=self.rs[:, t:t + 1], in_=self.rs[:, t:t + 1],
                                                        func=AF.Exp, scale=-0.5)
            P.I("scalar", "activation", reads=[bx[t], self.b_rs[t]], writes=[self.b_hb[t]], out=self.hb[:, t, :], in_=x[:, t, :], func=AF.Copy,
                                                        scale=self.rs[:, t:t + 1])
        for t in range(nt):
            bb = b_banks[self.bank_t[t]]
            for kc in range(8):
                P.I("tensor", "transpose", reads=[self.b_hb[t], C["b_const"]], writes=bb, out=self.psT[t][:, kc * 128:(kc + 1) * 128],
                                                                   in_=self.hb[:, t, kc * 128:(kc + 1) * 128],
                                                                   identity=C["identb"][:])
            P.I("vector", "tensor_tensor", reads=bb + [b_gain], writes=[self.b_hT], out=self.hT[:, :, t * 128:(t + 1) * 128],
                in0=self.psT[t].rearrange("p (c n) -> p c n", c=8),
                in1=gain[:].unsqueeze(2).to_broadcast([128, 8, 128]), op=ALU.mult)


def load_win(P, A, w_in, mixg, part):
    c0, c1 = (2304, PROJ) if part == "gla" else (0, 2304)
    W = {"c0": c0}
    W["w"] = A.alloc("win" + part, [128, 8, c1 - c0], BF16)
    W["g"] = A.alloc("mixg" + part, [128, 8], F32)
    W["b_w"] = Buf("win" + part)
    W["b_g"] = Buf("mixg" + part)
    P.I("sync", "dma_start", writes=[W["b_g"]], dma=True, key="mixg" + part, out=W["g"][:], in_=mixg)
    hw = (c1 - c0) // 2
    for kc in range(8):
        for hf in range(2):
            P.I("gpsimd", "dma_start", writes=[W["b_w"]], dma=True, key="win" + part,
                out=W["w"][:, kc, hf * hw:(hf + 1) * hw],
                in_=w_in[kc * 128:(kc + 1) * 128, c0 + hf * hw:c0 + (hf + 1) * hw])
    return W


def proj_phase(P, A, PS, C, W, QT, KT, b_QT, b_KT, io, part):
    m0 = A.mark()
    NS = NormStage(P, A, PS, C, "pj" + part, nxt=2, bank_t=(7, 6))
    gla = (part == "gla")
    c0 = W["c0"]
    bk = [[Buf("pjps%d_%d" % (b, h), excl=True) for h in range(2)] for b in range(8)]
    full = lambda b: [bk[b][0], bk[b][1]]
    b_banks = {b: full(b) for b in range(8)}
    Win = W["w"]
    bW = W["b_w"]
    bC = C["b_const"]

    def sb(name, shape, dt, n=1):
        ts = [A.alloc("pj" + name, shape, dt) for _ in range(n)]
        bs = [Buf("pj%s%d" % (name, i)) for i in range(n)]
        return (ts, bs) if n > 1 else (ts[0], bs[0])

    if not gla:
        sqk, b_sqk = sb("sqk", [128, 256], BF16, 2)
        rq, b_rq = sb("rq", [128, 256], F32, 2)
        kf, b_kf = sb("kf", [128, 256], F32, 2)
        klo, b_klo = sb("klo", [128, 256], BF16, 2)
        kout, b_kout = sb("kout", [128, WA], F32, 2)
        vout, b_vout = sb("vout", [128, WA], F32, 2)
        vaug, b_vaug = sb("vaug", [128, NH, 65], BF16, 2)
    else:
        glrT, b_glrT = sb("glrT", [16, 256], BF16)
        e1, b_e1 = sb("e1", [128, 256], F32)
        csm, b_csm = sb("csm", [128, 256], F32)
        eq, b_eq = sb("eq", [128, 256], F32)
        ek, b_ek = sb("ek", [128, 256], F32)
        qtil, b_qtil = sb("qtil", [128, 256], BF16)
        ktil, b_ktil = sb("ktil", [128, 256], BF16)
        ktm, b_ktm = sb("ktm", [128, 4, 256], BF16)
        ktok, b_ktok = sb("ktok", [128, 128], BF16, 2)
        vb, b_vb = sb("vb", [128, 256], BF16, 2)
        er, b_er = sb("er", [128, 256], F32, 2)
        srb, b_srb = sb("srb", [128, 256], F32, 2)
        amt, b_amt = sb("amt", [128, 4, 128], BF16, 2)
        t1, b_t1 = sb("t1", [128, 256], F32)
        sbd, b_sbd = sb("sbd", [128, 256], F32)
        sbdb, b_sbdb = sb("sbdb", [128, 256], BF16)
        osq, b_osq = sb("osq", [128, 256], F32)
        ssum, b_ssum = sb("ssum", [128, 4], F32)
        r4, b_r4 = sb("r4", [128, 4], F32)
        obf, b_obf = sb("obf", [128, 256], F32)
        obb, b_obb = sb("obb", [128, 256], BF16, 2)
        qm, b_qm = sb("qm", [128, NSEQ_S, 128], BF16)
        ktokm, b_ktokm = sb("ktokm", [128, NSEQ_S, 128], BF16)
        s0f, b_s0f = sb("s0f", [128, NSEQ_S, 64], F32)
        s0bd, b_s0bd = sb("s0bd", [128, NSEQ_S, 256], BF16)
        tS, b_tS = sb("tS", [128, 4, 256], F32)
        snew, b_snew = sb("snew", [128, NSEQ_S, 64], F32)
        gfin, b_gfin = sb("gfin", [128, 64], F32)

    if not gla:
        for i in range(2):
            P.I("vector", "memset", writes=[b_vaug[i]], ap=vaug[i][:], constant=1.0)
    else:
        P.I("vector", "memset", writes=[b_sbd], ap=sbd[:], constant=0.0)
        P.I("vector", "memset", writes=[b_sbdb], ap=sbdb[:], constant=0.0)
        P.I("vector", "memset", writes=[b_qm], ap=qm[:], constant=0.0)
        wqk_f, b_wqkf = sb("wqkf", [128, 8, 256], F32)
        wqk_hi, b_wqkh = sb("wqkh", [128, 8, 256], BF16)
        wqk_lo, b_wqkl = sb("wqkl", [128, 8, 256], BF16)
        hf32, b_hf32 = sb("hf32", [128, D], F32)
        hlo, b_hlo = sb("hlo", [128, D], BF16)
        hT0h, b_hT0h = sb("hT0h", [128, 8, 128], BF16)
        hT0l, b_hT0l = sb("hT0l", [128, 8, 128], BF16)
        qv, b_qv = sb("qv", [128, 128], F32)
        kv, b_kv = sb("kv", [128, 128], F32)
        qlo, b_qlo = sb("qlo", [128, 128], BF16)
        klo_, b_klo_ = sb("klo_", [128, 128], BF16)
        ktml, b_ktml = sb("ktml", [128, 4, 128], BF16)
        P.I("sync", "dma_start", writes=[b_wqkf], dma=True, key="wqkf", out=wqk_f[:],
            in_=io["w_in_d"][:, 2304:2560].rearrange("(c p) n -> p c n", p=128))
        P.I("vector", "tensor_tensor", reads=[b_wqkf, W["b_g"]], writes=[b_wqkf], out=wqk_f[:], in0=wqk_f[:],
            in1=W["g"][:].unsqueeze(2).to_broadcast([128, 8, 256]), op=ALU.mult)
        P.I("vector", "tensor_copy", reads=[b_wqkf], writes=[b_wqkh], out=wqk_hi[:], in_=wqk_f[:])
        P.I("vector", "tensor_tensor", reads=[b_wqkf, b_wqkh], writes=[b_wqkl], out=wqk_lo[:], in0=wqk_f[:], in1=wqk_hi[:],
            op=ALU.subtract)
        P.I("sync", "dma_start", writes=[b_s0f], dma=True, key="s0f", out=s0f[:], in_=io["sg"].rearrange("s h k v -> (h k) s v"))
        P.I("vector", "tensor_tensor", reads=[b_s0f, bC], writes=[b_s0bd], out=s0bd[:].rearrange("p s (h v) -> p s h v", h=4),
            in0=s0f[:].unsqueeze(2).to_broadcast([128, NSEQ_S, 4, 64]),
            in1=C["bdmask"].rearrange("p (h v) -> p h v", h=4).unsqueeze(1).to_broadcast([128, NSEQ_S, 4, 64]),
            op=ALU.mult)

    nsup = 33 if gla else 17
    sup_list = io.get("sup_list") or list(range(nsup))
    tiles_of = (lambda i: [i]) if gla else (lambda i: [2 * i, 2 * i + 1] if i < 16 else [32])

    def do_load(ii):
        NS.load(ii, [(io["X1"][g * 128:(g + 1) * 128, :], [io["b_X1"][g]]) for g in tiles_of(sup_list[ii])])

    do_load(0)
    for ii, i in enumerate(sup_list):
        gts = tiles_of(i)
        nt = len(gts)
        N = nt * 128
        sample = (gts[0] == 32)
        col0 = gts[0] * 128
        if ii + 1 < len(sup_list):
            do_load(ii + 1)
        NS.norm(ii, nt, W["g"], W["b_g"], b_banks)
        hT, b_hT = NS.hT, NS.b_hT
        need_out = [(sample or g >= 16) and not io.get("no_out") for g in gts]
        need_vout = [(sample or g >= 16) and not io.get("no_vout") for g in gts]
        wide = [3, 5]

        if not gla:
            def proj_chunk(j):
                fb_ = j % 2
                for kc in range(8):
                    P.I("tensor", "matmul", reads=[bW, b_hT], writes=[bk[fb_][0]], out=PS[:, fb_, 0:N], lhsT=Win[:, kc, j * 128:(j + 1) * 128], rhs=hT[:, kc, 0:N],
                        start=(kc == 0), stop=(kc == 7))
            proj_chunk(0)
            for j in range(12):
                fb, fo = j % 2, 0
                b_f = bk[fb][0]
                if j + 1 < 12:
                    proj_chunk(j + 1)
                s2 = j % 2
                P.I("scalar", "activation", reads=[b_f], writes=[b_sqk[s2]], out=sqk[s2][:, 0:N], in_=PS[:, fb, fo:fo + N],
                                                                            func=AF.Square)
                P.I("tensor", "matmul", reads=[b_sqk[s2], bC], writes=[bk[2][0]], out=PS[:, 2, 0:N], lhsT=C["blockones"][:],
                                                          rhs=sqk[s2][:, 0:N], start=True, stop=True)
                P.I("scalar", "activation", reads=[bk[2][0], bC], writes=[b_rq[s2]], out=rq[s2][:, 0:N], in_=PS[:, 2, 0:N],
                                                              func=AF.Ln, scale=1.0 / HD, bias=C["eps"][:])
                P.I("scalar", "activation", reads=[b_rq[s2]], writes=[b_rq[s2]], out=rq[s2][:, 0:N], in_=rq[s2][:, 0:N], func=AF.Exp,
                                                              scale=-0.5)
                P.I("vector", "tensor_tensor", reads=[b_f, b_rq[s2]], writes=[b_kf[s2]], out=kf[s2][:, 0:N], in0=PS[:, fb, fo:fo + N], in1=rq[s2][:, 0:N], op=ALU.mult)
                if j < 6:
                    P.I("vector", "tensor_scalar", reads=[b_kf[s2], bC], writes=[b_QT], out=QT[:, j, col0:col0 + N], in0=kf[s2][:, 0:N], scalar1=C["qg"][:, 0:1], scalar2=0.125,
                        op0=ALU.mult, op1=ALU.mult)
                else:
                    jj = j - 6
                    P.I("vector", "tensor_scalar", reads=[b_kf[s2], bC], writes=[b_kf[s2]], out=kf[s2][:, 0:N], in0=kf[s2][:, 0:N], scalar1=C["kg"][:, 0:1], scalar2=None, op0=ALU.mult)
                    P.I("scalar", "activation", reads=[b_kf[s2]], writes=[b_KT], out=KT[:, jj, col0:col0 + N], in_=kf[s2][:, 0:N],
                                                                         func=AF.Copy)
                    if any(need_out):
                        P.I("vector", "tensor_tensor", reads=[b_kf[s2], b_KT], writes=[b_klo[s2]], out=klo[s2][:, 0:N],
                            in0=kf[s2][:, 0:N], in1=KT[:, jj, col0:col0 + N], op=ALU.subtract)
                    for t in range(nt):
                        if need_out[t]:
                            wb = wide[t] + (jj * 128) // 512
                            wo = (jj * 128) % 512
                            P.I("tensor", "matmul", reads=[b_KT, bC], writes=full(wb), out=PS[:, wb, wo:wo + 128],
                                lhsT=KT[:, jj, col0 + t * 128:col0 + (t + 1) * 128], rhs=C["identb"][:], start=True, stop=False)
                            P.I("tensor", "matmul", reads=[b_klo[s2], bC], writes=full(wb), out=PS[:, wb, wo:wo + 128],
                                lhsT=klo[s2][:, t * 128:(t + 1) * 128], rhs=C["identb"][:], start=False, stop=True)
            for t in range(nt):
                if need_out[t]:
                    g = gts[t]
                    P.I("scalar", "activation", reads=full(wide[t]), writes=[b_kout[t]], out=kout[t][:, 0:512], in_=PS[:, wide[t], :], func=AF.Copy)
                    P.I("vector", "tensor_copy", reads=full(wide[t] + 1), writes=[b_kout[t]], out=kout[t][:, 512:768], in_=PS[:, wide[t] + 1, 0:256])
                    if sample:
                        for s in range(NSEQ_S):
                            o = P.I("sync", "dma_start", reads=[b_kout[t]], writes=[io["b_wks_new"]], dma=True, key="kout%d" % t, out=io["wks"][s, WBUF - TS:WBUF, :],
                                                                             in_=kout[t][s * TS:(s + 1) * TS, :])
                        P.must_finish(o)
                    else:
                        r0 = (g - 16) * 128
                        o = P.I("sync", "dma_start", reads=[b_kout[t]], writes=[Buf("wkp")], dma=True, key="kout%d" % t, out=io["wkp"][r0:r0 + 128, :], in_=kout[t][:])
                        P.must_finish(o)

            for t in range(nt):
                g = gts[t]
                for hf in range(2):
                    for kc in range(8):
                        P.I("tensor", "matmul", reads=[bW, b_hT], writes=full(wide[t] + hf), out=PS[:, wide[t] + hf, 0:384], lhsT=hT[:, kc, t * 128:(t + 1) * 128],
                            rhs=Win[:, kc, 1536 + hf * 384:1536 + (hf + 1) * 384], start=(kc == 0), stop=(kc == 7))
                for hf in range(2):
                    P.I("vector", "tensor_copy", reads=full(wide[t] + hf), writes=[b_vaug[t]], out=vaug[t][:, hf * 6:(hf + 1) * 6, 0:64],
                        in_=PS[:, wide[t] + hf, 0:384].rearrange("p (h v) -> p h v", h=6))
                    if need_vout[t]:
                        P.I("vector", "tensor_copy", reads=full(wide[t] + hf), writes=[b_vout[t]],
                            out=vout[t][:, hf * 384:(hf + 1) * 384], in_=PS[:, wide[t] + hf, 0:384])
                P.I("sync", "dma_start", reads=[b_vaug[t]], writes=[io["b_Vs"][g]], dma=True, key="vaug%d" % t, out=io["Vs"][g * 128:(g + 1) * 128, :],
                                                             in_=vaug[t][:].rearrange("p h v -> p (h v)"))
                if need_vout[t]:
                    if sample:
                        for s in range(NSEQ_S):
                            o = P.I("sync", "dma_start", reads=[b_vout[t]], writes=[io["b_wvs_new"]], dma=True, key="vout%d" % t, out=io["wvs"][s, WBUF - TS:WBUF, :],
                                                                             in_=vout[t][s * TS:(s + 1) * TS, :])
                        P.must_finish(o)
                    elif not io.get("no_vdma"):
                        r0 = (g - 16) * 128
                        o = P.I("sync", "dma_start", reads=[b_vout[t]], writes=[Buf("wvp")], dma=True, key="vout%d" % t, out=io["wvp"][r0:r0 + 128, :], in_=vout[t][:])
                        P.must_finish(o)

        if gla:
            def fm(colbase, M, slot):
                fb, fo = slot, 0
                for kc in range(8):
                    P.I("tensor", "matmul", reads=[bW, b_hT], writes=[bk[fb][0]], out=PS[0:M, fb, fo:fo + N], lhsT=Win[:, kc, colbase:colbase + M],
                                                             rhs=hT[:, kc, 0:N], start=(kc == 0), stop=(kc == 7))
                return PS[0:M, fb, fo:fo + N], bk[fb][0]
            precise = (gts[0] == 0)
            if precise:
                x0 = NS.xt[ii % NS.nxt]
                bx0 = NS.b_xt[ii % NS.nxt][0]
                P.I("scalar", "activation", reads=[bx0, NS.b_rs[0]], writes=[b_hf32], out=hf32[:], in_=x0[:, 0, :], func=AF.Copy,
                    scale=NS.rs[:, 0:1])
                P.I("vector", "tensor_tensor", reads=[b_hf32, NS.b_hb[0]], writes=[b_hlo], out=hlo[:], in0=hf32[:], in1=NS.hb[:, 0, :],
                    op=ALU.subtract)
                P.I("vector", "tensor_copy", reads=full(7), writes=[b_hT0h], out=hT0h[:].rearrange("p c n -> p (c n)"), in_=NS.psT[0])
                pl = PS[:, 5, :].bitcast(BF16)
                for kc in range(8):
                    P.I("tensor", "transpose", reads=[b_hlo, bC], writes=full(5), out=pl[:, kc * 128:(kc + 1) * 128],
                        in_=hlo[:, kc * 128:(kc + 1) * 128], identity=C["identb"][:])
                P.I("vector", "tensor_copy", reads=full(5), writes=[b_hT0l], out=hT0l[:].rearrange("p c n -> p (c n)"), in_=pl)

                def fm3(col, slot):
                    combos = ((wqk_hi, b_wqkh, hT0h, b_hT0h), (wqk_lo, b_wqkl, hT0h, b_hT0h), (wqk_hi, b_wqkh, hT0l, b_hT0l))
                    n_mm = 0
                    for wt, bw, ht, bh in combos:
                        for kc in range(8):
                            P.I("tensor", "matmul", reads=[bw, bh], writes=[bk[slot][0]], out=PS[:, slot, 0:128],
                                lhsT=wt[:, kc, col:col + 128], rhs=ht[:, kc, :], start=(n_mm == 0), stop=(n_mm == 23))
                            n_mm += 1
                    return PS[:, slot, 0:128], bk[slot][0]
                ps_qb, b_pqb = fm3(0, 0)
                ps_kb, b_pkb = fm3(128, 1)
            else:
                ps_qb, b_pqb = fm(2304 - c0, 128, 0)
                ps_kb, b_pkb = fm(2432 - c0, 128, 1)
            ps_gl, b_pgl = fm(3072 - c0, 16, 2)
            P.I("scalar", "activation", reads=[b_pgl], writes=[b_glrT], out=glrT[:, 0:N], in_=ps_gl, func=AF.Copy)
            ps_xg = PS[:, 6, 0:N]
            P.I("tensor", "matmul", reads=[b_glrT, bC], writes=[bk[6][0]], out=ps_xg, lhsT=C["wgk2"][:], rhs=glrT[:, 0:N], start=True, stop=True)
            P.I("scalar", "activation", reads=[bk[6][0], bC], writes=[b_e1], out=e1[:, 0:N], in_=ps_xg, func=AF.Exp, scale=-1.0, bias=C["negb"][:])
            P.I("scalar", "activation", reads=[b_e1, bC], writes=[b_e1], out=e1[:, 0:N], in_=e1[:, 0:N], func=AF.Ln, scale=1.0, bias=C["one"][:])
            for t in range(nt):
                d0 = C["notstart"] if sample else C["ones128"]
                P.I("vector", "tensor_tensor_scan", reads=[b_e1, bC], writes=[b_csm], out=csm[:, t * 128:(t + 1) * 128], data0=d0[:], data1=e1[:, t * 128:(t + 1) * 128], initial=0.0,
                    op0=ALU.mult, op1=ALU.add)
            P.I("scalar", "activation", reads=[b_csm], writes=[b_eq], out=eq[:, 0:N], in_=csm[:, 0:N], func=AF.Exp, scale=-1.0 / 16)
            P.I("scalar", "activation", reads=[b_csm], writes=[b_ek], out=ek[:, 0:N], in_=csm[:, 0:N], func=AF.Exp, scale=1.0 / 16)
            P.I("vector", "scalar_tensor_tensor", reads=[b_pqb, b_eq], writes=[b_qtil], out=qtil[:, 0:N], in0=ps_qb, scalar=32.0 ** -0.5,
                                                            in1=eq[:, 0:N], op0=ALU.mult, op1=ALU.mult)
            P.I("vector", "tensor_tensor", reads=[b_pkb, b_ek], writes=[b_ktil], out=ktil[:, 0:N], in0=ps_kb, in1=ek[:, 0:N], op=ALU.mult)
            for h in range(4):
                P.I("vector", "scalar_tensor_tensor", reads=[b_pkb, b_ek, bC], writes=[b_ktm], out=ktm[:, h, 0:N], in0=ps_kb, scalar=C["headmask"][:, h:h + 1], in1=ek[:, 0:N],
                    op0=ALU.mult, op1=ALU.mult)

            if precise:
                P.I("vector", "scalar_tensor_tensor", reads=[b_pqb, b_eq], writes=[b_qv], out=qv[:], in0=ps_qb, scalar=32.0 ** -0.5,
                    in1=eq[:, 0:128], op0=ALU.mult, op1=ALU.mult)
                P.I("vector", "tensor_tensor", reads=[b_qv, b_qtil], writes=[b_qlo], out=qlo[:], in0=qv[:], in1=qtil[:, 0:128],
                    op=ALU.subtract)
                P.I("vector", "tensor_tensor", reads=[b_pkb, b_ek], writes=[b_kv], out=kv[:], in0=ps_kb, in1=ek[:, 0:128], op=ALU.mult)
                P.I("vector", "tensor_tensor", reads=[b_kv, b_ktil], writes=[b_klo_], out=klo_[:], in0=kv[:], in1=ktil[:, 0:128],
                    op=ALU.subtract)
                for h in range(4):
                    P.I("vector", "tensor_scalar", reads=[b_klo_, bC], writes=[b_ktml], out=ktml[:, h, :], in0=klo_[:],
                        scalar1=C["headmask"][:, h:h + 1], scalar2=None, op0=ALU.mult)
            for t in range(nt):
                g = gts[t]
                cs = slice(t * 128, (t + 1) * 128)
                pkt = PS[:, 2, 0:64].bitcast(BF16)
                P.I("tensor", "transpose", reads=[b_ktil, bC], writes=[bk[2][0]], out=pkt, in_=ktil[:, cs], identity=C["identb"][:])
                P.I("vector", "tensor_copy", reads=[bk[2][0]],
                     writes=[b_ktok[t]], out=ktok[t][:], in_=pkt)
                for kc in range(8):
                    P.I("tensor", "matmul", reads=[bW, b_hT], writes=full(7), out=PS[:, 7, :], lhsT=hT[:, kc, cs], rhs=Win[:, kc, 2560 - c0:3072 - c0],
                                                                    start=(kc == 0), stop=(kc == 7))
                P.I("scalar", "activation", reads=full(7), writes=[b_vb[t]], out=vb[t][:], in_=PS[:, 7, 0:256], func=AF.Copy)
                P.I("scalar", "activation", reads=full(7), writes=[b_er[t]], out=er[t][:], in_=PS[:, 7, 256:512], func=AF.Exp, scale=-1.0)
                P.I("vector", "tensor_scalar_add", reads=[b_er[t]], writes=[b_er[t]], out=er[t][:], in0=er[t][:], scalar1=1.0)
                P.I("vector", "reciprocal", reads=[b_er[t]], writes=[b_er[t]], out=er[t][:], in_=er[t][:])
                P.I("vector", "tensor_tensor", reads=[b_er[t]] + full(7), writes=[b_srb[t]], out=srb[t][:], in0=er[t][:], in1=PS[:, 7, 256:512], op=ALU.mult)
                P.I("vector", "tensor_tensor", reads=[b_srb[t], bC], writes=[b_srb[t]], out=srb[t][:], in0=srb[t][:], in1=C["gg"], op=ALU.mult)
                wa = wide[t]
                for h in range(4):
                    P.I("tensor", "matmul", reads=[b_ktm, b_qtil], writes=full(wa), out=PS[:, wa, h * 128:(h + 1) * 128], lhsT=ktm[:, h, cs],
                                                                          rhs=qtil[:, cs], start=True, stop=not precise)
                    if precise:
                        P.I("tensor", "matmul", reads=[b_ktm, b_qlo], writes=full(wa), out=PS[:, wa, h * 128:(h + 1) * 128],
                            lhsT=ktm[:, h, cs], rhs=qlo[:], start=False, stop=False)
                        P.I("tensor", "matmul", reads=[b_ktml, b_qtil], writes=full(wa), out=PS[:, wa, h * 128:(h + 1) * 128],
                            lhsT=ktml[:, h, :], rhs=qtil[:, cs], start=False, stop=True)
                cm = C["caus_s"] if sample else C["caus"]
                P.I("vector", "tensor_tensor", reads=full(wa) + [bC], writes=[b_amt[t]], out=amt[t][:], in0=PS[:, wa, :].rearrange("p (h n) -> p h n", h=4),
                    in1=cm[:].unsqueeze(1).to_broadcast([128, 4, 128]), op=ALU.mult)
                po = PS[:, wa + 1, 0:256]
                b_po = bk[wa + 1][0]
                pS = PS[:, wa + 2, 0:256]
                b_pS = bk[wa + 2][0]
                if not sample:
                    P.I("tensor", "matmul", reads=[b_qtil, b_sbdb], writes=[b_po], out=po, lhsT=qtil[:, cs], rhs=sbdb[:], start=True, stop=False)
                else:
                    P.I("vector", "tensor_copy", reads=[b_qtil], writes=[b_qm], out=bass.AP(qm[:].tensor, qm[:].offset, [list(qm[:].ap[0]), [128 + TS, NSEQ_S], [1, TS]]),
                        in_=qtil[:, 0:128].rearrange("p (s t) -> p s t", s=NSEQ_S))
                    for s in range(NSEQ_S):
                        P.I("tensor", "matmul", reads=[b_qm, b_s0bd], writes=[b_po], out=po, lhsT=qm[:, s, :], rhs=s0bd[:, s, :],
                                                                     start=(s == 0), stop=False)
                for h in range(4):
                    P.I("tensor", "matmul", reads=[b_amt[t], b_vb[t]], writes=[b_po], out=PS[:, wa + 1, h * 64:(h + 1) * 64], lhsT=amt[t][:, h, :], rhs=vb[t][:, h * 64:(h + 1) * 64],
                        start=False, stop=(h == 3))
                if not sample:
                    P.I("tensor", "matmul", reads=[b_ktok[t], b_vb[t]], writes=[b_pS], out=pS, lhsT=ktok[t][:], rhs=vb[t][:], start=True, stop=True)
                    ebl = eq[:, t * 128 + 127:t * 128 + 128]
                    P.I("vector", "scalar_tensor_tensor", reads=[b_pS, b_eq, bC], writes=[b_t1], out=t1[:], in0=pS, scalar=ebl, in1=C["bdmask"], op0=ALU.mult, op1=ALU.mult)
                    P.I("vector", "scalar_tensor_tensor", reads=[b_sbd, b_t1, b_eq], writes=[b_sbd], out=sbd[:], in0=sbd[:], scalar=ebl, in1=t1[:], op0=ALU.mult, op1=ALU.add)
                    P.I("scalar", "activation", reads=[b_sbd],
                         writes=[b_sbdb], out=sbdb[:], in_=sbd[:], func=AF.Copy)
                P.I("scalar", "activation", reads=[b_po], writes=[b_osq], out=osq[:], in_=po, func=AF.Square)
                P.I("vector", "tensor_reduce", reads=[b_osq], writes=[b_ssum], out=ssum[:], in_=osq[:].rearrange("p (h v) -> p h v", h=4),
                                                         axis=AX.X, op=ALU.add)
                P.I("scalar", "activation", reads=[b_ssum, bC], writes=[b_r4], out=r4[:], in_=ssum[:], func=AF.Ln, scale=1.0 / 64, bias=C["eps"][:])
                P.I("scalar", "activation", reads=[b_r4],
                     writes=[b_r4], out=r4[:], in_=r4[:], func=AF.Exp, scale=-0.5)
                P.I("vector", "tensor_tensor", reads=[b_po, b_r4], writes=[b_obf], out=obf[:].rearrange("p (h v) -> p h v", h=4), in0=po.rearrange("p (h v) -> p h v", h=4),
                    in1=r4[:].unsqueeze(2).to_broadcast([128, 4, 64]), op=ALU.mult)
                P.I("vector", "tensor_tensor", reads=[b_obf, b_srb[t]], writes=[b_obb[t]], out=obb[t][:], in0=obf[:], in1=srb[t][:], op=ALU.mult)
                P.I("sync", "dma_start", reads=[b_obb[t]], writes=[io["b_OB"][g]], dma=True, key="obb%d" % t, out=io["OB"][g * 128:(g + 1) * 128, :], in_=obb[t][:])

            if sample:
                P.I("vector", "tensor_tensor", reads=[b_ktok[0], bC], writes=[b_ktokm], out=ktokm[:], in0=ktok[0][:].unsqueeze(1).to_broadcast([128, NSEQ_S, 128]),
                    in1=C["seqmask"].unsqueeze(2).to_broadcast([128, NSEQ_S, 128]), op=ALU.mult)
                for gq in range(4):
                    for s4 in range(4):
                        s = gq * 4 + s4
                        bnk = 3 + s4 // 2
                        off = (s4 % 2) * 256
                        P.I("tensor", "matmul", reads=[b_ktokm, b_vb[0]], writes=[bk[bnk][s4 % 2]], out=PS[:, bnk, off:off + 256], lhsT=ktokm[:, s, :], rhs=vb[0][:], start=True, stop=True)
                    P.I("vector", "tensor_tensor", reads=full(3) + full(4) + [bC], writes=[b_tS], out=tS[:].rearrange("p s (h v) -> p s h v", h=4),
                        in0=PS[:, 3:5, :].rearrange("p b (s h v) -> p (b s) h v", s=2, h=4),
                        in1=C["bdmask"].rearrange("p (h v) -> p h v", h=4).unsqueeze(1).to_broadcast([128, 4, 4, 64]),
                        op=ALU.mult)
                    P.I("vector", "tensor_reduce", reads=[b_tS], writes=[b_snew], out=snew[:, gq * 4:(gq + 1) * 4, :], in_=tS[:].rearrange("p s (h v) -> p s v h", h=4),
                        axis=AX.X, op=ALU.add)
                P.I("vector", "tensor_tensor", reads=[b_snew, b_s0f], writes=[b_snew], out=snew[:], in0=snew[:], in1=s0f[:], op=ALU.add)
                P.I("vector", "tensor_tensor", reads=[b_snew, b_eq], writes=[b_snew], out=snew[:], in0=snew[:],
                    in1=eq[:, 0:128].rearrange("p (s t) -> p s t", t=TS)[:, :, TS - 1:TS].to_broadcast([128, NSEQ_S, 64]),
                    op=ALU.mult)
                o = P.I("sync", "dma_start", reads=[b_snew], writes=[Buf("gls")], dma=True, key="gls", out=io["gls"].rearrange("s h k v -> (h k) s v"), in_=snew[:])
                P.must_finish(o)
            if gts[-1] == 31:
                P.I("vector", "tensor_tensor", reads=[b_sbd, bC], writes=[b_t1], out=t1[:], in0=sbd[:], in1=C["bdmask"], op=ALU.mult)
                P.I("vector", "tensor_reduce", reads=[b_t1], writes=[b_gfin], out=gfin[:], in_=t1[:].rearrange("p (h v) -> p v h", h=4),
                                                         axis=AX.X, op=ALU.add)
                o = P.I("sync", "dma_start", reads=[b_gfin], writes=[Buf("glp")],
                         dma=True, key="glp", out=io["glp"], in_=gfin[:])
                P.must_finish(o)
    A.release(m0)


BRANCHES = ((128, 1), (512, 4), (2048, 16))


def _bucket(dist):
    dist = np.asarray(dist, np.int64)
    d = np.maximum(dist, 1).astype(np.float32)
    large = 16 + (np.log(d / np.float32(16)) / np.float32(math.log(2048 / 16)) * np.float32(16)).astype(np.int32)
    large = np.minimum(large, 31)
    return np.where(dist < 16, dist, large)


def make_onehots():
    ohp = np.zeros((3, 32, 384), np.float32)
    for bi, (w, d) in enumerate(BRANCHES):
        for i in range(383):
            st = i - 127
            if 0 <= st <= 128:
                ohp[bi, _bucket(st * d), i] = 1.0

    def mult(dist):
        return (dist <= 128) * 1 + ((dist % 4 == 0) and dist <= 512) * 1 + ((dist % 16 == 0) and dist <= 2048) * 1
    ohs = np.zeros((5, 32, 136), np.float32)
    for ty in range(5):
        dmin = (512 - 128 * ty - 127) if ty < 4 else -127
        for j in range(135):
            dist = dmin + j
            if dist >= 0:
                ohs[ty, _bucket(dist), j] = mult(dist)
    ohg = np.zeros((32, 128), np.float32)
    for p in range(96):
        ohg[_bucket(2048 - 16 * p), p] = 1.0
    return ohp, ohs, ohg


def attn_phase(P, A, PS, C, QT, KT, b_QT, b_KT, io):
    m0 = A.mark()
    bC = C["b_const"]
    bk = [Buf("atps%d" % b, excl=True) for b in range(8)]

    def sb(name, shape, dt, n=1):
        ts = [A.alloc("at" + name, shape, dt) for _ in range(n)]
        bs = [Buf("at%s%d" % (name, i)) for i in range(n)]
        return (ts, bs) if n > 1 else (ts[0], bs[0])

    EBp, b_EBp = sb("EBp", [128, NH, 256], F32, 3)
    EBs, b_EBs = sb("EBs", [128, NH, TS], F32, 13)
    m_setup = A.mark()
    rb_f, b_rbf = sb("rbf", [32, NH], F32)
    e_f, b_ef = sb("ef", [32, NH], F32)
    e_hi, b_ehi = sb("ehi", [32, NH], BF16)
    e_lo, b_elo = sb("elo", [32, NH], BF16)
    ebc_hi, b_ebh = sb("ebchi", [32, NH, 128], BF16)
    ebc_lo, b_ebl = sb("ebclo", [32, NH, 128], BF16)
    ohp_f, b_ohpf = sb("ohpf", [32, 3, 384], F32)
    ohp_b, b_ohpb = sb("ohpb", [32, 3, 384], BF16)
    ohs_f, b_ohsf = sb("ohsf", [32, 5, 136], F32)
    ohs_b, b_ohsb = sb("ohsb", [32, 5, 136], BF16)
    ohg_f, b_ohgf = sb("ohgf", [32, 128], F32)
    ohg_b, b_ohgb = sb("ohgb", [32, 128], BF16)
    fsb, b_fsb = sb("fsb", [128, NH, 384], F32)
    gsb, b_gsb = sb("gsb", [128, NH], F32)

    P.I("sync", "dma_start", writes=[b_rbf], dma=True, key="at_rb", out=rb_f[:], in_=io["rel_bias"])
    P.I("sync", "dma_start", writes=[b_ohpf], dma=True, key="at_ohp", out=ohp_f[:], in_=io["ohp"].rearrange("t b i -> b t i"))
    P.I("sync", "dma_start", writes=[b_ohsf], dma=True, key="at_ohs", out=ohs_f[:], in_=io["ohs"].rearrange("t b i -> b t i"))
    P.I("sync", "dma_start", writes=[b_ohgf], dma=True, key="at_ohg", out=ohg_f[:], in_=io["ohg"])
    P.I("scalar", "activation", reads=[b_rbf], writes=[b_ef], out=e_f[:], in_=rb_f[:], func=AF.Exp)
    P.I("vector", "tensor_copy", reads=[b_ef], writes=[b_ehi], out=e_hi[:], in_=e_f[:])
    P.I("vector", "tensor_tensor", reads=[b_ef, b_ehi], writes=[b_elo], out=e_lo[:], in0=e_f[:], in1=e_hi[:], op=ALU.subtract)
    P.I("vector", "tensor_copy", reads=[b_ehi], writes=[b_ebh], out=ebc_hi[:], in_=e_hi[:].unsqueeze(2).to_broadcast([32, NH, 128]))
    P.I("vector", "tensor_copy", reads=[b_elo], writes=[b_ebl], out=ebc_lo[:], in_=e_lo[:].unsqueeze(2).to_broadcast([32, NH, 128]))
    P.I("vector", "tensor_copy", reads=[b_ohpf], writes=[b_ohpb], out=ohp_b[:], in_=ohp_f[:])
    P.I("vector", "tensor_copy", reads=[b_ohsf], writes=[b_ohsb], out=ohs_b[:], in_=ohs_f[:])
    P.I("vector", "tensor_copy", reads=[b_ohgf], writes=[b_ohgb], out=ohg_b[:], in_=ohg_f[:])

    FD = io["FD"]
    b_FD = [Buf("FD%d" % i) for i in range(8)]

    def toeplitz(ty, oh, Wd, ncols, dest, b_dest):
        for h in range(NH):
            bnk = h % 4
            for eb, b_e, first in ((ebc_hi, b_ebh, True), (ebc_lo, b_ebl, False)):
                P.I("tensor", "matmul", reads=[b_e, b_ohpb, b_ohsb], writes=[bk[bnk]], out=PS[:, bnk, 0:Wd],
                    lhsT=eb[:, h, :], rhs=oh, start=first, stop=not first)
            P.I("vector", "tensor_copy", reads=[bk[bnk]], writes=[b_fsb], out=fsb[:, h, 0:Wd], in_=PS[:, bnk, 0:Wd])
        fd = FD[ty]
        P.I("sync", "dma_start", reads=[b_fsb], writes=[b_FD[ty]], dma=True, key="at_fdw", out=fd.rearrange("p (h w) -> p h w", h=NH)[:, :, 0:Wd], in_=fsb[:, :, 0:Wd])
        src = bass.AP(fd.tensor, fd.offset + 127, [[NH * 384 - 1, 128], [384, NH], [1, ncols]])
        P.I("sync", "dma_start", reads=[b_FD[ty]], writes=[b_dest], dma=True, key="at_fdr%d" % ty, out=dest, in_=src)

    for bi in range(3):
        toeplitz(bi, ohp_b[:, bi, 0:383], 383, 256, EBp[bi][:], b_EBp[bi])
    for ty in range(5):
        toeplitz(3 + ty, ohs_b[:, ty, 0:135], 135, TS, EBs[ty][:], b_EBs[ty])
    for eb_, b_e, first in ((e_hi, b_ehi, True), (e_lo, b_elo, False)):
        P.I("tensor", "matmul", reads=[b_e, b_ohgb], writes=[bk[4]], out=PS[:, 4, 0:NH], lhsT=ohg_b[:], rhs=eb_[:],
            start=first, stop=not first)
    P.I("vector", "tensor_copy", reads=[bk[4]], writes=[b_gsb], out=gsb[:], in_=PS[:, 4, 0:NH])
    for r in range(8):
        P.I("vector", "memset", writes=[b_EBs[5 + r]], ap=EBs[5 + r][:], constant=0.0)
        P.I("vector", "tensor_copy", reads=[b_gsb], writes=[b_EBs[5 + r]], out=EBs[5 + r][:, :, r:r + 1], in_=gsb[:].unsqueeze(2))

    P.barrier()
    A.release(m_setup)
    vt, b_vt = sb("vt", [128, NH, 65], BF16, 4)
    pexp, b_pexp = sb("pexp", [128, 2, 256], F32, 2)
    pT, b_pT = sb("pT", [128, 2, 256], BF16, 2)
    osb, b_osb = sb("osb", [128, 780], F32, 2)
    Vs = io["Vs"]
    ocol = lambda h: (0, h * 65) if h < 7 else (1, (h - 7) * 65)
    if not io.get("skip_prompt_attn"):
        qi = 0
        vi = 0
        pending = []

        def flush():
            while pending:
                pending.pop(0)()
        for bi, (w, d) in enumerate(BRANCHES):
            nb = SEQ // d // 128
            for r in range(d):
                def load_v(n, slot, r=r, d=d):
                    src = bass.AP(Vs.tensor, Vs.offset + (r + d * 128 * n) * 780, [[d * 780, 128], [1, 780]])
                    P.I("sync", "dma_start", reads=io["b_Vs"][:32], writes=[b_vt[slot]], dma=True, key="at_vt%d" % slot,
                        out=vt[slot][:].rearrange("p h v -> p (h v)"), in_=src)
                slots = {}
                slots[0] = vi % 4
                load_v(0, vi % 4)
                vi += 1
                for n in range(nb):
                    if n + 1 < nb:
                        slots[n + 1] = vi % 4
                        load_v(n + 1, vi % 4)
                        vi += 1
                    cq = slice(r + d * 128 * n, r + d * 128 * (n + 1), d)
                    cp = slice(r + d * 128 * (n - 1), r + d * 128 * n, d) if n > 0 else None
                    ob = 4 + 2 * (qi % 2)
                    first_in_bank = [True, True]
                    for hp in range(6):
                        sbk = 2 * (hp % 2)
                        for hh in range(2):
                            pb = hh * 64
                            P.I("tensor", "matmul", reads=[b_QT, b_KT], writes=[bk[sbk + hh]], out=PS[:, sbk + hh, 0:128],
                                lhsT=KT[pb:pb + 64, hp, cq], rhs=QT[pb:pb + 64, hp, cq], start=True, stop=True)
                            if n > 0:
                                P.I("tensor", "matmul", reads=[b_QT, b_KT], writes=[bk[sbk + hh]], out=PS[:, sbk + hh, 128:256],
                                    lhsT=KT[pb:pb + 64, hp, cp], rhs=QT[pb:pb + 64, hp, cq], start=True, stop=True)
                        wc = 256 if n > 0 else 128
                        s2 = hp % 2
                        P.I("scalar", "activation", reads=[bk[sbk], bk[sbk + 1]], writes=[b_pexp[s2]], out=pexp[s2][:, :, 0:wc],
                            in_=PS[:, sbk:sbk + 2, 0:wc], func=AF.Exp)
                        P.I("vector", "tensor_tensor", reads=[b_pexp[s2], b_EBp[bi]], writes=[b_pT[s2]], out=pT[s2][:, :, 0:wc],
                            in0=pexp[s2][:, :, 0:wc], in1=EBp[bi][:, 2 * hp:2 * hp + 2, 0:wc], op=ALU.mult)

                        def pv(hp=hp, s2=s2, n=n, ob=ob, fib=first_in_bank, sl_n=slots[n], sl_p=slots.get(n - 1),
                               qi=qi, bi=bi, r=r, d=d):
                            for hh in range(2):
                                h = 2 * hp + hh
                                bo, co = ocol(h)
                                P.I("tensor", "matmul", reads=[b_pT[s2], b_vt[sl_n]], writes=[bk[ob + bo]],
                                    out=PS[:, ob + bo, co:co + 65], lhsT=pT[s2][:, hh, 0:128], rhs=vt[sl_n][:, h, :],
                                    start=fib[bo], stop=(n == 0), skip_group_check=True)
                                fib[bo] = False
                                if n > 0:
                                    P.I("tensor", "matmul", reads=[b_pT[s2], b_vt[sl_p]], writes=[bk[ob + bo]],
                                        out=PS[:, ob + bo, co:co + 65], lhsT=pT[s2][:, hh, 128:256], rhs=vt[sl_p][:, h, :],
                                        start=False, stop=True, skip_group_check=True)
                            if hp == 5:
                                o2 = qi % 2
                                P.I("vector", "tensor_copy", reads=[bk[ob]], writes=[b_osb[o2]], out=osb[o2][:, 0:455],
                                    in_=PS[:, ob, 0:455])
                                P.I("scalar", "activation", reads=[bk[ob + 1]], writes=[b_osb[o2]], out=osb[o2][:, 455:780],
                                    in_=PS[:, ob + 1, 0:325], func=AF.Copy)
                                Ob = io["Obr"][bi]
                                dst = bass.AP(Ob.tensor, Ob.offset + (r + d * 128 * n) * 780, [[d * 780, 128], [1, 780]])
                                P.I("sync", "dma_start", reads=[b_osb[o2]], writes=[io["b_Obr"][bi]], dma=True,
                                    key="at_osb%d" % o2, out=dst, in_=osb[o2][:])
                        flush()
                        pending.append(pv)
                    qi += 1
        flush()

    P.barrier()
    A.release(m_setup)
    if not io.get("skip_sample_attn"):
        ND = 4
        ktf, b_ktf = sb("ktf", [128, WA], F32, ND)
        vtf, b_vtf = sb("vtf", [128, WA], F32, ND)
        ktb, b_ktb = sb("ktb", [128, WA], BF16, ND)
        vau, b_vau = sb("vau", [128, NH, 65], BF16, ND)
        ktT, b_ktT = sb("ktT", [128, 6, 128], BF16, ND)
        pes, b_pes = sb("pes", [128, 2, 48], F32, 2)
        pTs, b_pTs = sb("pTs", [128, 2, 48], BF16, 2)
        vnew, b_vnew = sb("vnew", [TS, NH * 65], BF16, 2)
        osm, b_osm = sb("osm", [TS, 780], F32, 2)
        for i in range(ND):
            P.I("gpsimd", "memset", writes=[b_vau[i]], ap=vau[i][:], constant=1.0)
        ck, cv = io["ck"], io["cv"]
        recs = []
        for s in range(NSEQ_S):
            tl = [("A", i) for i in range(4)] + [("B", r) for r in range(8)] + [("N", 0)]
            for j, (kind, idx) in enumerate(tl):
                recs.append(dict(s=s, kind=kind, idx=idx, first=(j == 0), last=(kind == "N"), fib=None))
        fibs = {s: [True, True] for s in range(NSEQ_S)}

        def stA(t):
            rc = recs[t]
            s, kind, idx = rc["s"], rc["kind"], rc["idx"]
            sl = t % ND
            if rc["first"]:
                P.I("sync", "dma_start", reads=[io["b_Vs"][32]], writes=[b_vnew[s % 2]], dma=True, key="at_vnew%d" % (s % 2),
                    out=vnew[s % 2][:], in_=Vs[SEQ + s * TS:SEQ + (s + 1) * TS, :])
            if kind == "N":
                return
            if kind == "A":
                nk = 128
                ksrc = ck[s, 1536 + 128 * idx:1536 + 128 * (idx + 1), :]
                vsrc = cv[s, 1536 + 128 * idx:1536 + 128 * (idx + 1), :]
            else:
                nk = 96
                ksrc = bass.AP(ck.tensor, ck[s, idx, :].offset, [[16 * WA, 96], [1, WA]])
                vsrc = bass.AP(cv.tensor, cv[s, idx, :].offset, [[16 * WA, 96], [1, WA]])
            P.I("sync", "dma_start", writes=[b_ktf[sl]], dma=True, key="at_ktf%d" % sl, out=ktf[sl][0:nk, :], in_=ksrc)
            P.I("sync", "dma_start", writes=[b_vtf[sl]], dma=True, key="at_vtf%d" % sl, out=vtf[sl][0:nk, :], in_=vsrc)
            P.I("gpsimd", "tensor_copy", reads=[b_ktf[sl]], writes=[b_ktb[sl]], out=ktb[sl][0:nk, :], in_=ktf[sl][0:nk, :])
            P.I("gpsimd", "tensor_copy", reads=[b_vtf[sl]], writes=[b_vau[sl]], out=vau[sl][0:nk, :, 0:64],
                in_=vtf[sl][0:nk, :].rearrange("p (h v) -> p h v", h=NH))
            pb_ = 2 + (t % 2)
            pst = PS[:, pb_, :].bitcast(BF16)
            for c in range(6):
                P.I("tensor", "transpose", reads=[b_ktb[sl], bC], writes=[bk[pb_]], out=pst[:, c * 128:c * 128 + nk],
                    in_=ktb[sl][0:nk, c * 128:(c + 1) * 128], identity=C["identb"][0:nk, 0:nk])
            P.I("vector", "tensor_copy", reads=[bk[pb_]], writes=[b_ktT[sl]], out=ktT[sl][:, :, 0:nk],
                in_=pst[:, 0:768].rearrange("p (c n) -> p c n", c=6)[:, :, 0:nk])

        def tile_params(t):
            rc = recs[t]
            kind, idx = rc["kind"], rc["idx"]
            if kind == "A":
                return 128, EBs[idx], b_EBs[idx]
            if kind == "B":
                return 96, EBs[5 + idx], b_EBs[5 + idx]
            return TS, EBs[4], b_EBs[4]

        def stB(t):
            rc = recs[t]
            s, last = rc["s"], rc["last"]
            sl = t % ND
            s2 = t % 2
            nk, eb, b_eb = tile_params(t)
            qcols = slice(SEQ + s * TS, SEQ + (s + 1) * TS)
            for hp in range(6):
                for hh in range(2):
                    pb = hh * 64
                    lhs = ktT[sl][pb:pb + 64, hp, 0:nk] if not last else KT[pb:pb + 64, hp, qcols]
                    P.I("tensor", "matmul", reads=[b_ktT[sl], b_QT, b_KT], writes=[bk[hh]], out=PS[0:nk, hh, hp * TS:(hp + 1) * TS],
                        lhsT=lhs, rhs=QT[pb:pb + 64, hp, qcols], start=True, stop=True)
            P.I("scalar", "activation", reads=[bk[0], bk[1]], writes=[b_pes[s2]], out=pes[s2][0:nk], in_=PS[0:nk, 0:2, 0:48],
                func=AF.Exp)
            P.I("vector", "tensor_tensor", reads=[b_pes[s2], b_eb], writes=[b_pTs[s2]],
                out=pTs[s2][0:nk].rearrange("p a (b t) -> p a b t", t=TS),
                in0=pes[s2][0:nk].rearrange("p a (b t) -> p a b t", t=TS),
                in1=eb[0:nk].rearrange("p (b a) t -> p a b t", a=2), op=ALU.mult)

        def stC(t):
            rc = recs[t]
            s, last = rc["s"], rc["last"]
            sl = t % ND
            s2 = t % 2
            nk, eb, b_eb = tile_params(t)
            ob = 4 + 2 * (s % 2)
            fib = fibs[s]
            for h in range(NH):
                hp, hh = h // 2, h % 2
                bo, co = ocol(h)
                rhs = vau[sl][0:nk, h, :] if not last else vnew[s % 2][:, h * 65:(h + 1) * 65]
                P.I("tensor", "matmul", reads=[b_pTs[s2], b_vau[sl], b_vnew[s % 2]], writes=[bk[ob + bo]],
                    out=PS[0:TS, ob + bo, co:co + 65], lhsT=pTs[s2][0:nk, hh, hp * TS:(hp + 1) * TS], rhs=rhs,
                    start=fib[bo], stop=last, skip_group_check=True)
                fib[bo] = False
            if last:
                o2 = s % 2
                P.I("vector", "tensor_copy", reads=[bk[ob]], writes=[b_osm[o2]], out=osm[o2][:, 0:455], in_=PS[0:TS, ob, 0:455])
                P.I("scalar", "activation", reads=[bk[ob + 1]], writes=[b_osm[o2]], out=osm[o2][:, 455:780],
                    in_=PS[0:TS, ob + 1, 0:325], func=AF.Copy)
                P.I("sync", "dma_start", reads=[b_osm[o2]], writes=[io["b_Obr"][3]], dma=True, key="at_osm%d" % o2,
                    out=io["Obr"][0][SEQ + s * TS:SEQ + (s + 1) * TS, :], in_=osm[o2][:])

        NT = len(recs)
        stA(0)
        stA(1)
        stB(0)
        for t in range(NT):
            if t + 2 < NT:
                stA(t + 2)
            if t + 1 < NT:
                stB(t + 1)
            stC(t)
    A.release(m0)


def merge_phase(P, A, PS, C, io, w_out):
    m0 = A.mark()
    bC = C["b_const"]
    bk = [Buf("mgps%d" % b, excl=True) for b in range(8)]

    def sb(name, shape, dt, n=1):
        ts = [A.alloc("mg" + name, shape, dt) for _ in range(n)]
        bs = [Buf("mg%s%d" % (name, i)) for i in range(n)]
        return (ts, bs) if n > 1 else (ts[0], bs[0])

    Wo, b_Wo = sb("wo", [128, 8, D], BF16)
    for c in range(8):
        P.I("gpsimd", "dma_start", writes=[b_Wo], dma=True, key="mg_wo", out=Wo[:, c, :], in_=w_out[c * 128:(c + 1) * 128, :])
    o3, b_o3 = sb("o3", [128, 3, 780], F32, 2)
    rden, b_rden = sb("rden", [128, NH], F32)
    cat, b_cat = sb("cat", [128, D], BF16, 2)
    catT, b_catT = sb("catT", [128, 8, 128], BF16, 2)
    x1, b_x1 = sb("x1", [128, D], F32, 2)
    def stX(g):
        s2 = g % 2
        nb = 3 if g < 32 else 1
        for bi in range(nb):
            P.I("sync", "dma_start", reads=[io["b_Obr"][bi if g < 32 else 3]], writes=[b_o3[s2]], dma=True, key="mg_o3_%d" % s2,
                out=o3[s2][:, bi, :], in_=io["Obr"][bi][g * 128:(g + 1) * 128, :])
        P.I("sync", "dma_start", reads=[io["b_OB"][g]], writes=[b_cat[s2]], dma=True, key="mg_cat%d" % s2,
            out=cat[s2][:, WA:D], in_=io["OB"][g * 128:(g + 1) * 128, :])
        P.I("sync", "dma_start", reads=[io["b_X1"][g]], writes=[b_x1[s2]], dma=True, key="mg_x1_%d" % s2,
            out=x1[s2][:], in_=io["X1"][g * 128:(g + 1) * 128, :])
        for bi in range(1, nb):
            P.I("vector", "tensor_tensor", reads=[b_o3[s2]], writes=[b_o3[s2]], out=o3[s2][:, 0, :], in0=o3[s2][:, 0, :],
                in1=o3[s2][:, bi, :], op=ALU.add)
        ov = o3[s2][:, 0, :].rearrange("p (h v) -> p h v", v=65)
        P.I("vector", "reciprocal", reads=[b_o3[s2]], writes=[b_rden], out=rden[:].unsqueeze(2), in_=ov[:, :, 64:65])
        P.I("vector", "tensor_tensor", reads=[b_o3[s2], b_rden], writes=[b_cat[s2]],
            out=cat[s2][:, 0:WA].rearrange("p (h v) -> p h v", v=64), in0=ov[:, :, 0:64],
            in1=rden[:].unsqueeze(2).to_broadcast([128, NH, 64]), op=ALU.mult)
        pst = PS[:, s2, :].bitcast(BF16)
        for c in range(8):
            P.I("tensor", "transpose", reads=[b_cat[s2], bC], writes=[bk[s2]], out=pst[:, c * 128:(c + 1) * 128],
                in_=cat[s2][:, c * 128:(c + 1) * 128], identity=C["identb"][:])
        P.I("scalar", "activation", reads=[bk[s2]], writes=[b_catT[s2]], out=catT[s2][:].rearrange("p c n -> p (c n)"), in_=pst,
            func=AF.Copy)

    def stY(g):
        s2 = g % 2
        for dh in range(2):
            bnk = 2 + 2 * s2 + dh
            for c in range(8):
                P.I("tensor", "matmul", reads=[b_catT[s2], b_Wo], writes=[bk[bnk]], out=PS[:, bnk, :], lhsT=catT[s2][:, c, :],
                    rhs=Wo[:, c, dh * 512:(dh + 1) * 512], start=(c == 0), stop=(c == 7))
            P.I("vector", "tensor_tensor", reads=[bk[bnk], b_x1[s2]], writes=[b_x1[s2]], out=x1[s2][:, dh * 512:(dh + 1) * 512],
                in0=PS[:, bnk, :], in1=x1[s2][:, dh * 512:(dh + 1) * 512], op=ALU.add)
        P.I("sync", "dma_start", reads=[b_x1[s2]], writes=[io["b_X2"][g]], dma=True, key="mg_x2_%d" % s2,
            out=io["X2"][g * 128:(g + 1) * 128, :], in_=x1[s2][:])

    stX(0)
    for g in range(33):
        if g + 1 < 33:
            stX(g + 1)
        stY(g)
    A.release(m0)


def ffn_precise_tile(P, A, PS, C, w1, w3, w2, g_sb, src, dst, b_src, b_dst):
    m0 = A.mark()
    bC = C["b_const"]
    bk = [Buf("fpps%d" % b, excl=True) for b in range(8)]

    def sb(name, shape, dt, n=1):
        ts = [A.alloc("fp" + name, shape, dt) for _ in range(n)]
        bs = [Buf("fp%s%d" % (name, i)) for i in range(n)]
        return (ts, bs) if n > 1 else (ts[0], bs[0])

    x, b_x = sb("x", [128, D], F32)
    sq, b_sq = sb("sq", [128, D], BF16)
    ss, b_ss = sb("ss", [128, 1], F32)
    rs, b_rs = sb("rs", [128, 1], F32)
    hf, b_hf = sb("hf", [128, D], F32)
    hh, b_hh = sb("hh", [128, D], BF16)
    hl, b_hl = sb("hl", [128, D], BF16)
    hTh, b_hTh = sb("hTh", [128, 8, 128], BF16)
    hTl, b_hTl = sb("hTl", [128, 8, 128], BF16)
    wf = {n: sb("wf" + n, [128, 8, 128], F32, 2) for n in ("w1", "w3")}
    wh = {n: sb("wh" + n, [128, 8, 128], BF16, 2) for n in ("w1", "w3")}
    wl = {n: sb("wl" + n, [128, 8, 128], BF16, 2) for n in ("w1", "w3")}
    w2f, b_w2f = sb("w2f", [128, D], F32, 2)
    w2h, b_w2h = sb("w2h", [128, D], BF16, 2)
    w2l, b_w2l = sb("w2l", [128, D], BF16, 2)
    sa, b_sa = sb("sa", [128, 128], F32)
    uf, b_uf = sb("uf", [128, 128], F32)
    uh, b_uh = sb("uh", [128, 128], BF16, 2)
    ul, b_ul = sb("ul", [128, 128], BF16, 2)

    P.I("sync", "dma_start", reads=b_src, writes=[b_x], dma=True, key="fp_x", out=x[:], in_=src)
    P.I("scalar", "activation", reads=[b_x], writes=[b_sq, b_ss], out=sq[:], in_=x[:], func=AF.Square, accum_out=ss[:])
    P.I("scalar", "activation", reads=[b_ss, bC], writes=[b_rs], out=rs[:], in_=ss[:], func=AF.Ln, scale=1.0 / D, bias=C["eps"][:])
    P.I("scalar", "activation", reads=[b_rs], writes=[b_rs], out=rs[:], in_=rs[:], func=AF.Exp, scale=-0.5)
    P.I("scalar", "activation", reads=[b_x, b_rs], writes=[b_hf], out=hf[:], in_=x[:], func=AF.Copy, scale=rs[:, 0:1])
    P.I("vector", "tensor_copy", reads=[b_hf], writes=[b_hh], out=hh[:], in_=hf[:])
    P.I("vector", "tensor_tensor", reads=[b_hf, b_hh], writes=[b_hl], out=hl[:], in0=hf[:], in1=hh[:], op=ALU.subtract)
    for srcT, b_s, dstT, b_d, bank in ((hh, b_hh, hTh, b_hTh, 6), (hl, b_hl, hTl, b_hTl, 7)):
        pst = PS[:, bank, :].bitcast(BF16)
        for kc in range(8):
            P.I("tensor", "transpose", reads=[b_s, bC], writes=[bk[bank]], out=pst[:, kc * 128:(kc + 1) * 128],
                in_=srcT[:, kc * 128:(kc + 1) * 128], identity=C["identb"][:])
        P.I("vector", "tensor_copy", reads=[bk[bank]], writes=[b_d], out=dstT[:].rearrange("p c n -> p (c n)"), in_=pst)

    ny = 0
    for c in range(NFC):
        s = c % 2
        for n, wsrc in (("w1", w1), ("w3", w3)):
            wft, b_wf = wf[n][0][s], wf[n][1][s]
            wht, b_wh = wh[n][0][s], wh[n][1][s]
            wlt, b_wl = wl[n][0][s], wl[n][1][s]
            P.I("sync", "dma_start", writes=[b_wf], dma=True, key="fp_%s_%d" % (n, s), out=wft[:],
                in_=wsrc[:, c * 128:(c + 1) * 128].rearrange("(kc p) n -> p kc n", p=128))
            P.I("vector", "tensor_tensor", reads=[b_wf, bC], writes=[b_wf], out=wft[:], in0=wft[:],
                in1=g_sb.unsqueeze(2).to_broadcast([128, 8, 128]), op=ALU.mult)
            P.I("scalar", "activation", reads=[b_wf], writes=[b_wh], out=wht[:], in_=wft[:], func=AF.Copy)
            P.I("vector", "tensor_tensor", reads=[b_wf, b_wh], writes=[b_wl], out=wlt[:], in0=wft[:], in1=wht[:], op=ALU.subtract)
        P.I("sync", "dma_start", writes=[b_w2f[s]], dma=True, key="fp_w2_%d" % s, out=w2f[s][:], in_=w2[c * 128:(c + 1) * 128, :])
        P.I("scalar", "activation", reads=[b_w2f[s]], writes=[b_w2h[s]], out=w2h[s][:], in_=w2f[s][:], func=AF.Copy)
        P.I("vector", "tensor_tensor", reads=[b_w2f[s], b_w2h[s]], writes=[b_w2l[s]], out=w2l[s][:], in0=w2f[s][:], in1=w2h[s][:],
            op=ALU.subtract)
        for j, n in enumerate(("w1", "w3")):
            combos = ((wh[n][0][s], wh[n][1][s], hTh, b_hTh), (wl[n][0][s], wl[n][1][s], hTh, b_hTh),
                      (wh[n][0][s], wh[n][1][s], hTl, b_hTl))
            k = 0
            for wt, bw, ht, bh in combos:
                for kc in range(8):
                    P.I("tensor", "matmul", reads=[bw, bh], writes=[bk[j]], out=PS[:, j, 0:128], lhsT=wt[:, kc, :], rhs=ht[:, kc, :],
                        start=(k == 0), stop=(k == 23))
                    k += 1
        P.I("scalar", "activation", reads=[bk[0]], writes=[b_sa], out=sa[:], in_=PS[:, 0, 0:128], func=AF.Silu)
        P.I("vector", "tensor_tensor", reads=[b_sa, bk[1]], writes=[b_uf], out=uf[:], in0=sa[:], in1=PS[:, 1, 0:128], op=ALU.mult)
        P.I("vector", "tensor_copy", reads=[b_uf], writes=[b_uh[s]], out=uh[s][:], in_=uf[:])
        P.I("vector", "tensor_tensor", reads=[b_uf, b_uh[s]], writes=[b_ul[s]], out=ul[s][:], in0=uf[:], in1=uh[s][:], op=ALU.subtract)
        for u_, b_u, w_, b_w in ((uh[s], b_uh[s], w2h[s], b_w2h[s]), (ul[s], b_ul[s], w2h[s], b_w2h[s]),
                                 (uh[s], b_uh[s], w2l[s], b_w2l[s])):
            for dh in range(2):
                P.I("tensor", "matmul", reads=[b_u, b_w], writes=[bk[2 + dh]], out=PS[:, 2 + dh, :], lhsT=u_[:],
                    rhs=w_[:, dh * 512:(dh + 1) * 512], start=(ny < 2), stop=(ny >= 6 * NFC - 2))
                ny += 1
    for dh in range(2):
        P.I("vector", "scalar_tensor_tensor", reads=[bk[2 + dh], b_x], writes=[b_x], out=x[:, dh * 512:(dh + 1) * 512],
            in0=PS[:, 2 + dh, :], scalar=0.5, in1=x[:, dh * 512:(dh + 1) * 512], op0=ALU.mult, op1=ALU.add)
    P.I("sync", "dma_start", reads=[b_x], writes=b_dst, dma=True, key="fp_out", out=dst, in_=x[:])
    A.release(m0)


T_ALL = SEQ + NSEQ_S * TS


def build_program(opts=None):
    opts = opts or {}
    nc = bass.Bass("TRN2", target_bir_lowering=False)

    def dt(n, s, d, k):
        return nc.dram_tensor(n, list(s), d, kind=k).ap()
    I, O, S = "ExternalInput", "ExternalOutput", "Internal"
    xp = dt("xp", [SEQ, D], F32, I)
    xs = dt("xs", [NSEQ_S * TS, D], F32, I)
    ck = dt("ck", [NSEQ_S, WBUF, WA], F32, I)
    cv = dt("cv", [NSEQ_S, WBUF, WA], F32, I)
    sg = dt("sg", [NSEQ_S, 4, 32, 64], F32, I)
    wts = {n: dt(n, s, F32, I) for n, s in (("f1w1", [D, DFF]), ("f1w3", [D, DFF]), ("f1w2", [DFF, D]),
                                              ("f2w1", [D, DFF]), ("f2w3", [D, DFF]), ("f2w2", [DFF, D]),
                                              ("w_in", [D, PROJ]), ("w_out", [D, D]))}
    c128 = dt("c128", [128, _offsets(C128)[1]], F32, I)
    p128 = dt("p128", [128, _offsets(P128)[1]], F32, I)
    wgk2 = dt("wgk2", [16, 128], F32, I)
    rel_bias = dt("rel_bias", [32, NH], F32, I)
    ohp = dt("ohp", [3, 32, 384], F32, I)
    ohs = dt("ohs", [5, 32, 136], F32, I)
    ohg = dt("ohg", [32, 128], F32, I)
    yp = dt("yp", [SEQ, D], F32, O)
    ys = dt("ys", [NSEQ_S * TS, D], F32, O)
    io = dict(sg=sg, ck=ck, cv=cv, rel_bias=rel_bias, ohp=ohp, ohs=ohs, ohg=ohg, w_in_d=wts["w_in"])
    io["wkp"] = dt("wkp", [WBUF, WA], F32, O)
    io["wvp"] = dt("wvp", [WBUF, WA], F32, O)
    io["glp"] = dt("glp", [128, 64], F32, O)
    io["wks"] = dt("wks", [NSEQ_S, WBUF, WA], F32, O)
    io["wvs"] = dt("wvs", [NSEQ_S, WBUF, WA], F32, O)
    io["gls"] = dt("gls", [NSEQ_S, 4, 32, 64], F32, O)
    probe = bool(opts.get("probe"))
    dump = bool(opts.get("dump"))
    io["X1"] = dt("X1", [T_ALL, D], F32, I if probe else (O if dump else S))
    io["X2"] = dt("X2", [T_ALL, D], F32, O if (probe or dump) else S)
    io["Vs"] = dt("Vs", [T_ALL, 780], BF16, S)
    io["OB"] = dt("OB", [T_ALL, 256], BF16, S)
    obr = dt("Obr", [3, T_ALL, 780], F32, S)
    io["Obr"] = [obr[i] for i in range(3)]
    fd = dt("FD", [8, 128, NH * 384], F32, S)
    io["FD"] = [fd[i] for i in range(8)]
    for nm in ("X1", "X2", "Vs", "OB"):
        io["b_" + nm] = [Buf("%s_%d" % (nm, g)) for g in range(33)]
    io["b_Obr"] = [Buf("Obr%d" % i) for i in range(4)]
    io["b_wks_new"], io["b_wvs_new"] = Buf("wksn"), Buf("wvsn")
    io.update(opts)

    P = Prog(nc)
    A = Arena(nc)
    poff = _offsets(P128)[0]
    gslice = lambda n: p128[:, poff[n][0]:poff[n][0] + poff[n][1]]
    with nc.psum_tensor("ps", [128, 8, 512], F32) as PS:
        C = setup_consts(P, A, c128, p128, wgk2)
        base = A.mark()

        def tiles(src_p, src_s, dst_p, dst_s, b_src, b_dst):
            out = []
            for i in range(16):
                out.append([(src_p[g * 128:(g + 1) * 128, :], dst_p[g * 128:(g + 1) * 128, :],
                             [b_src[g]] if b_src else [], [b_dst[g]] if b_dst else [Buf("o")]) for g in (2 * i, 2 * i + 1)])
            out.append([(src_s, dst_s, [b_src[32]] if b_src else [], [b_dst[32]] if b_dst else [Buf("o")])])
            return out

        if not probe:
          W1 = load_ffn_weights(P, A, wts["f1w1"], wts["f1w3"], wts["f1w2"], gslice("f1g"), "f1")
          wdeps = [W1["b_w2"][1].writer]
          for s in range(NSEQ_S):
            for src, dst in ((ck, io["wks"]), (cv, io["wvs"])):
                o = P.I("scalar", "dma_start", dma=True, key="shift", extra=wdeps,
                        out=dst[s, 0:WBUF - TS, :].rearrange("(a b) f -> a (b f)", a=120),
                        in_=src[s, TS:WBUF, :].rearrange("(a b) f -> a (b f)", a=120))
          P.must_finish(o)
          ffn_phase(P, A, PS, C, W1, tiles(xp, xs, io["X1"], io["X1"][SEQ:T_ALL, :], None, io["b_X1"]), "f1")
          P.barrier()
          A.release(base)
          ffn_precise_tile(P, A, PS, C, wts["f1w1"], wts["f1w3"], wts["f1w2"], C["f1g"], xp[0:128, :], io["X1"][0:128, :],
                           [], [io["b_X1"][0]])
          P.barrier()
          A.release(base)
        mg = gslice("mixg")
        Wg = load_win(P, A, wts["w_in"], mg, "gla")
        proj_phase(P, A, PS, C, Wg, None, None, None, None, io, "gla")
        P.barrier()
        A.release(base)
        QT = A.alloc("QT", [128, 6, T_ALL], BF16)
        KT = A.alloc("KT", [128, 6, T_ALL], BF16)
        b_QT, b_KT = Buf("QT"), Buf("KT")
        mq = A.mark()
        Wq = load_win(P, A, wts["w_in"], mg, "qkv")
        proj_phase(P, A, PS, C, Wq, QT, KT, b_QT, b_KT, io, "qkv")
        P.barrier()
        A.release(mq)
        attn_phase(P, A, PS, C, QT, KT, b_QT, b_KT, io)
        P.barrier()
        A.release(base)
        if not probe:
            W2 = load_ffn_weights(P, A, wts["f2w1"], wts["f2w3"], wts["f2w2"], gslice("f2g"), "f2")
        merge_phase(P, A, PS, C, io, wts["w_out"])
        if probe:
            for o in P.live_dma:
                P.must_finish(o)
        P.barrier()
        if not probe:
            ffn_phase(P, A, PS, C, W2, tiles(io["X2"], io["X2"][SEQ:T_ALL, :], yp, ys, io["b_X2"], None), "f2")
        P.emit()
    info = dict(n_sems=P.n_sems, counts=P.counts, peak=A.peak, n_ins={e: len(P.q[e]) for e in ENGS})
    return nc, info


_CACHE = {}


def kernel(**inp):
    f32 = lambda a: np.ascontiguousarray(np.asarray(a, dtype=np.float32))
    if "nc" not in _CACHE:
        _CACHE["nc"], _CACHE["info"] = build_program()
    nc = _CACHE["nc"]
    ohp, ohs, ohg = make_onehots()
    shared = {
        "f1w1": f32(inp["ffn1_w1"][0]), "f1w3": f32(inp["ffn1_w3"][0]), "f1w2": f32(inp["ffn1_w2"][0]),
        "f2w1": f32(inp["ffn2_w1"][0]), "f2w3": f32(inp["ffn2_w3"][0]), "f2w2": f32(inp["ffn2_w2"][0]),
        "w_in": f32(inp["w_in"][0]), "w_out": f32(inp["w_out"][0]),
        "c128": make_consts(), "p128": make_params({k: np.asarray(v) for k, v in inp.items() if k in (
            "q_norm", "k_norm", "b_gk", "gla_norm", "ffn1_norm", "mix_norm", "ffn2_norm")}),
        "wgk2": f32(inp["w_gk2"][0]), "rel_bias": f32(inp["rel_bias"]), "ohp": ohp, "ohs": ohs, "ohg": ohg,
    }
    xp, xs = np.asarray(inp["x_prompt"]), np.asarray(inp["x_sample"])
    ck, cv, sg = np.asarray(inp["cache_win_k"]), np.asarray(inp["cache_win_v"]), np.asarray(inp["state_gla"])
    in_maps = []
    for c in range(NCORES):
        m = dict(shared)
        sl = slice(c * NSEQ_S, (c + 1) * NSEQ_S)
        m["xp"] = f32(xp[c])
        m["xs"] = f32(xs[sl].reshape(NSEQ_S * TS, D))
        m["ck"] = f32(ck[0, sl].reshape(NSEQ_S, WBUF, WA))
        m["cv"] = f32(cv[0, sl].reshape(NSEQ_S, WBUF, WA))
        m["sg"] = f32(sg[0, sl])
        in_maps.append(m)
    res = run_bass_kernel_spmd(nc, in_maps, core_ids=list(range(NCORES)))
    R = res.results
    cat = lambda n: np.concatenate([np.asarray(r[n]) for r in R], axis=0)
    y_prompt = np.stack([np.asarray(r["yp"]) for r in R]).astype(np.float32)
    y_sample = cat("ys").reshape(NCORES * NSEQ_S, TS, D).astype(np.float32)
    wkp = np.stack([np.asarray(r["wkp"]) for r in R]).reshape(1, NCORES, WBUF, NH, HD).astype(np.float32)
    wvp = np.stack([np.asarray(r["wvp"]) for r in R]).reshape(1, NCORES, WBUF, NH, HD).astype(np.float32)
    glp = np.stack([np.asarray(r["glp"]) for r in R]).reshape(1, NCORES, 4, 32, 64).astype(np.float32)
    wks = cat("wks").reshape(1, NCORES * NSEQ_S, WBUF, NH, HD).astype(np.float32)
    wvs = cat("wvs").reshape(1, NCORES * NSEQ_S, WBUF, NH, HD).astype(np.float32)
    gls = cat("gls").reshape(1, NCORES * NSEQ_S, 4, 32, 64).astype(np.float32)
    return (y_prompt, y_sample, wkp, wvp, glp, wks, wvs, gls)
```
